# Optimizing a Trainium2 kernel written in Bass

```python
import math
import jax
import jax.numpy as jnp
from jax import lax
import numpy as np

D_MODEL = 1024
BATCH = 2
SEQ = 16384
DEPTH = 2
DEC_BATCH = 8
DEC_SEQ = 16
PAST_LEN = 1024

CHUNK = 64
WINDOW = 128
N_WIN_CHUNKS = WINDOW // CHUNK
HEAD_DIM = 64
N_Q_HEADS = 8
N_KV_HEADS = 2
GROUP = N_Q_HEADS // N_KV_HEADS
ATTN_WIDTH = N_Q_HEADS * HEAD_DIM
KV_WIDTH = N_KV_HEADS * HEAD_DIM
POOL_WINDOWS = (2, 4, 8, 16)
N_POOL_GROUPS = 4
POOL_WIDTH = 512
POOL_GROUP_DIM = POOL_WIDTH // N_POOL_GROUPS
POOL_MAXW = max(POOL_WINDOWS)
POOL_HIST = POOL_MAXW - 1
N_BRANCHES = 2
IN_WIDTH = ATTN_WIDTH + 2 * KV_WIDTH + POOL_WIDTH + N_BRANCHES * D_MODEL
D_FF = 2816
CONV_WIDTH = 3
CONV_HIST = CONV_WIDTH - 1
EPS = 1e-6
NEG_INF = -1e30

kernel_name = "hybrid_streaming_swa_pool_convffn_step"


def rms_norm(x, g):
    xf = x.astype(jnp.float32)
    y = xf * lax.rsqrt(jnp.mean(xf * xf, axis=-1, keepdims=True) + EPS)
    return (y * g.astype(jnp.float32)).astype(x.dtype)


def alibi_slopes():
    return jnp.asarray([2.0 ** (-8.0 * (h + 1) / N_Q_HEADS) for h in range(N_Q_HEADS)], dtype=jnp.float32)


def alibi_bias(rel):
    b = -alibi_slopes()[:, None, None] * jnp.abs(rel).astype(jnp.float32)[None]
    return b.reshape(N_KV_HEADS, GROUP, rel.shape[0], rel.shape[1])


def sink_softmax_apply(scores, sinks_b, v, eq):
    m = jnp.maximum(scores.max(-1), sinks_b)
    p = jnp.exp(scores - m[..., None])
    denom = p.sum(-1) + jnp.exp(sinks_b - m)
    p = p / denom[..., None]
    return jnp.einsum(eq, p.astype(v.dtype), v)


def swa_prompt(q, k, v, sinks):
    B, S = q.shape[0], q.shape[1]
    n_c = S // CHUNK
    kb_len = (N_WIN_CHUNKS + 1) * CHUNK
    qc = q.reshape(B, n_c, CHUNK, N_KV_HEADS, GROUP, HEAD_DIM)
    pad = ((0, 0), (WINDOW, 0), (0, 0), (0, 0))
    kp = jnp.pad(k, pad).reshape(B, n_c + N_WIN_CHUNKS, CHUNK, N_KV_HEADS, HEAD_DIM)
    vp = jnp.pad(v, pad).reshape(B, n_c + N_WIN_CHUNKS, CHUNK, N_KV_HEADS, HEAD_DIM)
    kb = jnp.concatenate([kp[:, j:j + n_c] for j in range(N_WIN_CHUNKS + 1)], axis=2)
    vb = jnp.concatenate([vp[:, j:j + n_c] for j in range(N_WIN_CHUNKS + 1)], axis=2)
    scores = jnp.einsum('bcqkgd,bcskd->bckgqs', qc, kb).astype(jnp.float32) * (HEAD_DIM ** -0.5)
    rel = WINDOW + jnp.arange(CHUNK)[:, None] - jnp.arange(kb_len)[None, :]
    key_pos = jnp.arange(n_c)[:, None] * CHUNK - WINDOW + jnp.arange(kb_len)[None, :]
    valid = key_pos >= 0
    scores = jnp.where(valid[None, :, None, None, None, :], scores + alibi_bias(rel), NEG_INF)
    sk = sinks.astype(jnp.float32).reshape(N_KV_HEADS, GROUP, 1)
    out = sink_softmax_apply(scores, sk, vb, 'bckgqs,bcskd->bcqkgd')
    return out.reshape(B, S, ATTN_WIDTH)


def swa_sample(q, k_new, v_new, cache_k, cache_v, sinks):
    B, T = q.shape[0], q.shape[1]
    kk = jnp.concatenate([cache_k, k_new.astype(cache_k.dtype)], axis=1)
    vv = jnp.concatenate([cache_v, v_new.astype(cache_v.dtype)], axis=1)
    qg = q.reshape(B, T, N_KV_HEADS, GROUP, HEAD_DIM)
    scores = jnp.einsum('btkgd,bskd->bkgts', qg, kk).astype(jnp.float32) * (HEAD_DIM ** -0.5)
    rel = WINDOW + jnp.arange(T)[:, None] - jnp.arange(WINDOW + T)[None, :]
    scores = scores + alibi_bias(rel)
    sk = sinks.astype(jnp.float32).reshape(N_KV_HEADS, GROUP, 1)
    out = sink_softmax_apply(scores, sk, vv, 'bkgts,bskd->btkgd')
    return out.reshape(B, T, ATTN_WIDTH), kk[:, -WINDOW:], vv[:, -WINDOW:]


def pool_mixer(p_ext, pos, w_pool, pool_scale):
    B, L, C = p_ext.shape
    T = L - POOL_HIST
    f = p_ext.astype(jnp.float32)
    cs = jnp.cumsum(f, axis=1)
    cs_pad = jnp.pad(cs, ((0, 0), (POOL_MAXW, 0), (0, 0)))
    means = []
    for g, w in enumerate(POOL_WINDOWS):
        lo, hi = g * POOL_GROUP_DIM, (g + 1) * POOL_GROUP_DIM
        win = cs[:, POOL_HIST:, lo:hi] - cs_pad[:, POOL_MAXW - w + POOL_HIST:POOL_MAXW - w + L, lo:hi]
        cnt = jnp.minimum(pos + 1, w).astype(jnp.float32)[None, :, None]
        means.append(win / cnt)
    d = (jnp.concatenate(means, axis=-1) - f[:, POOL_HIST:]).reshape(B, T, N_POOL_GROUPS, POOL_GROUP_DIM)
    y = jnp.einsum('btgc,gcd->btgd', d, w_pool.astype(jnp.float32)).reshape(B, T, C)
    y = y * pool_scale.astype(jnp.float32)
    return y.astype(p_ext.dtype)


def conv_ffn(xn, conv_hist, w_up, conv_w, conv_b, w_down):
    h = xn @ w_up
    T = h.shape[1]
    h_ext = jnp.concatenate([conv_hist.astype(h.dtype), h], axis=1)
    hc = conv_b
    for j in range(CONV_WIDTH):
        hc = hc + conv_w[j] * h_ext[:, j:j + T]
    gate, val = jnp.split(hc, 2, axis=-1)
    y = (jax.nn.gelu(gate, approximate=False) * val) @ w_down
    return y, h_ext[:, -CONV_HIST:]


def trunk_layer(x, pos, kv_cache, pool_hist, conv_hist, lp):
    B, T = x.shape[0], x.shape[1]
    xn = rms_norm(x, lp['norm_mix'])
    proj = xn @ lp['w_in']
    cuts = [ATTN_WIDTH, ATTN_WIDTH + KV_WIDTH, ATTN_WIDTH + 2 * KV_WIDTH, ATTN_WIDTH + 2 * KV_WIDTH + POOL_WIDTH]
    q, k, v, pin, gates = jnp.split(proj, cuts, axis=-1)
    q = rms_norm(q.reshape(B, T, N_Q_HEADS, HEAD_DIM), lp['q_norm'])
    k = rms_norm(k.reshape(B, T, N_KV_HEADS, HEAD_DIM), lp['k_norm'])
    v = v.reshape(B, T, N_KV_HEADS, HEAD_DIM)
    if kv_cache is None:
        a = swa_prompt(q, k, v, lp['sinks'])
        k_state, v_state = k[:, -WINDOW:], v[:, -WINDOW:]
    else:
        a, k_state, v_state = swa_sample(q, k, v, kv_cache[0], kv_cache[1], lp['sinks'])
    p_ext = jnp.concatenate([pool_hist.astype(pin.dtype), pin], axis=1)
    pl = pool_mixer(p_ext, pos, lp['w_pool'], lp['pool_scale'])
    pool_state = p_ext[:, -POOL_HIST:]
    ya = a @ lp['w_br_attn']
    yb = pl @ lp['w_br_pool']
    g = jax.nn.sigmoid(gates.reshape(B, T, N_BRANCHES, D_MODEL) + lp['gate_bias'])
    mix = (g[:, :, 0] * ya + g[:, :, 1] * yb) @ lp['w_out']
    x = x + mix
    y, conv_state = conv_ffn(rms_norm(x, lp['norm_ffn']), conv_hist, lp['w_up'], lp['conv_w'], lp['conv_b'], lp['w_down'])
    x = x + y
    return x, k_state, v_state, pool_state, conv_state


def setup_inputs(seed: int = 0) -> dict:
    key = jax.random.key(seed)
    ks = jax.random.split(key, 24)

    def nrm(k, shape, scale):
        return jax.random.normal(k, shape, jnp.float32) * scale

    return {
        'x_prompt': nrm(ks[0], (BATCH, SEQ, D_MODEL), 1.0),
        'x_sample': nrm(ks[1], (DEC_BATCH, DEC_SEQ, D_MODEL), 1.0),
        'cache_k': nrm(ks[2], (DEPTH, DEC_BATCH, WINDOW, N_KV_HEADS, HEAD_DIM), 1.0),
        'cache_v': nrm(ks[3], (DEPTH, DEC_BATCH, WINDOW, N_KV_HEADS, HEAD_DIM), 1.0),
        'state_pool': nrm(ks[4], (DEPTH, DEC_BATCH, POOL_HIST, POOL_WIDTH), 1.0),
        'state_conv': nrm(ks[5], (DEPTH, DEC_BATCH, CONV_HIST, 2 * D_FF), 1.0),
        'norm_mix': 1.0 + nrm(ks[6], (DEPTH, D_MODEL), 0.05),
        'w_in': nrm(ks[7], (DEPTH, D_MODEL, IN_WIDTH), D_MODEL ** -0.5),
        'q_norm': 1.0 + nrm(ks[8], (DEPTH, HEAD_DIM), 0.05),
        'k_norm': 1.0 + nrm(ks[9], (DEPTH, HEAD_DIM), 0.05),
        'sinks': nrm(ks[10], (DEPTH, N_Q_HEADS), 1.0),
        'w_pool': nrm(ks[11], (DEPTH, N_POOL_GROUPS, POOL_GROUP_DIM, POOL_GROUP_DIM), POOL_GROUP_DIM ** -0.5),
        'pool_scale': 1.0 + nrm(ks[12], (DEPTH, POOL_WIDTH), 0.05),
        'w_br_attn': nrm(ks[13], (DEPTH, ATTN_WIDTH, D_MODEL), ATTN_WIDTH ** -0.5),
        'w_br_pool': nrm(ks[14], (DEPTH, POOL_WIDTH, D_MODEL), POOL_WIDTH ** -0.5),
        'gate_bias': nrm(ks[15], (DEPTH, N_BRANCHES, D_MODEL), 0.01),
        'w_out': nrm(ks[16], (DEPTH, D_MODEL, D_MODEL), D_MODEL ** -0.5),
        'norm_ffn': 1.0 + nrm(ks[17], (DEPTH, D_MODEL), 0.05),
        'w_up': nrm(ks[18], (DEPTH, D_MODEL, 2 * D_FF), D_MODEL ** -0.5),
        'conv_w': nrm(ks[19], (DEPTH, CONV_WIDTH, 2 * D_FF), CONV_WIDTH ** -0.5),
        'conv_b': nrm(ks[20], (DEPTH, 2 * D_FF), 0.01),
        'w_down': nrm(ks[21], (DEPTH, D_FF, D_MODEL), D_FF ** -0.5),
    }


def reference(x_prompt, x_sample, cache_k, cache_v, state_pool, state_conv,
              norm_mix, w_in, q_norm, k_norm, sinks, w_pool, pool_scale,
              w_br_attn, w_br_pool, gate_bias, w_out, norm_ffn, w_up, conv_w, conv_b, w_down):
    B, S = x_prompt.shape[0], x_prompt.shape[1]
    DB, T = x_sample.shape[0], x_sample.shape[1]
    pos_p = jnp.arange(S)
    pos_s = PAST_LEN + jnp.arange(T)
    zero_pool = jnp.zeros((B, POOL_HIST, POOL_WIDTH), x_prompt.dtype)
    zero_conv = jnp.zeros((B, CONV_HIST, 2 * D_FF), x_prompt.dtype)
    xp, xs = x_prompt, x_sample
    kp_l, vp_l, pp_l, cp_l = [], [], [], []
    ks_l, vs_l, ps_l, cs_l = [], [], [], []
    for l in range(DEPTH):
        lp = {
            'norm_mix': norm_mix[l], 'w_in': w_in[l], 'q_norm': q_norm[l], 'k_norm': k_norm[l],
            'sinks': sinks[l], 'w_pool': w_pool[l], 'pool_scale': pool_scale[l],
            'w_br_attn': w_br_attn[l], 'w_br_pool': w_br_pool[l], 'gate_bias': gate_bias[l],
            'w_out': w_out[l], 'norm_ffn': norm_ffn[l], 'w_up': w_up[l], 'conv_w': conv_w[l],
            'conv_b': conv_b[l], 'w_down': w_down[l],
        }
        xp, kp, vp, pp, cp = trunk_layer(xp, pos_p, None, zero_pool, zero_conv, lp)
        xs, ks_, vs_, ps_, cs_ = trunk_layer(xs, pos_s, (cache_k[l], cache_v[l]), state_pool[l], state_conv[l], lp)
        kp_l.append(kp); vp_l.append(vp); pp_l.append(pp); cp_l.append(cp)
        ks_l.append(ks_); vs_l.append(vs_); ps_l.append(ps_); cs_l.append(cs_)
    k_prompt = jnp.stack(kp_l, axis=0)
    v_prompt = jnp.stack(vp_l, axis=0)
    pool_prompt = jnp.stack(pp_l, axis=0)
    conv_prompt = jnp.stack(cp_l, axis=0)
    k_sample = jnp.stack(ks_l, axis=0)
    v_sample = jnp.stack(vs_l, axis=0)
    pool_sample = jnp.stack(ps_l, axis=0)
    conv_sample = jnp.stack(cs_l, axis=0)
    return (xp, xs, k_prompt, v_prompt, pool_prompt, conv_prompt, k_sample, v_sample, pool_sample, conv_sample)
```

```python
from contextlib import ExitStack
import numpy as np
import concourse.bass as bass
import concourse.mybir as mybir
from concourse.bass_utils import run_bass_kernel_spmd

F32 = mybir.dt.float32
BF16 = mybir.dt.bfloat16
ACTF = mybir.ActivationFunctionType
ALU = mybir.AluOpType

NCORES = 8
D = 1024
OWN_TILES = 32
HALO_TILES = 3
NPT = OWN_TILES + HALO_TILES
TS = 16
NSLOT = 6
UEL = 4096
NXS = 7
EPS = 1e-6
DFF = 2816
NCH = 44
UNITS_PER_LAYER = 31
DEBUG_ONE_LAYER = False
DEBUG_STOP = None
PVL = 222


class _Op:
    __slots__ = ("eng", "fn", "reads", "writes", "kind", "stream", "is_output", "eidx", "sidx",
                 "waits", "signal", "sig", "K", "epoch", "gidx")


class Sched:
    ENGS = ["pe", "act", "dve", "pool", "sp"]

    def __init__(self, nc, es, nsem=4):
        self.nc = nc
        self.es = es
        self.ops = []
        self.epoch = 0
        self.nsem = nsem
        self.eng_sems = {}
        for e in self.ENGS:
            self.eng_sems[e] = [es.enter_context(nc.semaphore(f"sem_{e}{i}")) for i in range(nsem)]
        self.stream_sem = {}
        self.stream_cnt = {}
        self.out_streams = set()
        self.touch = {}

    def op(self, eng, fn, reads=(), writes=()):
        o = _Op()
        o.eng = eng; o.fn = fn; o.reads = tuple(reads); o.writes = tuple(writes)
        o.kind = "op"; o.stream = None; o.is_output = False
        o.waits = []; o.signal = False; o.sig = None; o.K = None; o.epoch = self.epoch
        o.gidx = len(self.ops)
        self.ops.append(o)
        for r in o.reads + o.writes:
            if r.startswith("ps"):
                self.touch[r] = o.gidx
        return o

    def dma(self, eng, fn, reads=(), writes=(), stream=None, is_output=False):
        o = self.op(eng, fn, reads, writes)
        o.kind = "dma"
        o.stream = stream
        o.is_output = is_output
        if stream not in self.stream_sem:
            self.stream_sem[stream] = self.es.enter_context(self.nc.semaphore(f"dsem_{stream}"))
            self.stream_cnt[stream] = 0
        o.sidx = self.stream_cnt[stream]
        self.stream_cnt[stream] += 1
        if is_output:
            self.out_streams.add(stream)
        return o

    def _analyze(self):
        last_writer = {}
        readers = {}
        stream_last = {}
        eng_ops = {e: [] for e in self.ENGS}
        eng_K = {e: {} for e in self.ENGS}
        for op in self.ops:
            deps = {}
            for r in op.reads:
                w = last_writer.get(r)
                if w is not None:
                    deps[w] = "raw"
                if r.startswith("ps"):
                    for rd in readers.get(r, ()):
                        if rd.eng != op.eng and rd not in deps:
                            deps[rd] = "rar"
            for wr in op.writes:
                w = last_writer.get(wr)
                if w is not None and w not in deps:
                    deps[w] = "waw"
                for rd in readers.get(wr, ()):
                    if rd is not op and rd not in deps:
                        deps[rd] = "war"
            if op.kind == "dma":
                prev = stream_last.get(op.stream)
                if prev is not None:
                    deps[prev] = "raw"
                stream_last[op.stream] = op
            for r in op.reads:
                readers.setdefault(r, []).append(op)
            for wr in op.writes:
                last_writer[wr] = op
                readers[wr] = []
            op.eidx = len(eng_ops[op.eng])
            eng_ops[op.eng].append(op)
            K = dict(eng_K[op.eng])
            need = {}
            for Dp, t in deps.items():
                if Dp.kind == "op":
                    if Dp.eng == op.eng and op.kind == "op" and op.eng == "pe":
                        continue
                    key = Dp.eng
                    val = Dp.eidx
                else:
                    key = ("s", Dp.stream)
                    val = Dp.sidx
                if K.get(key, -1) >= val:
                    continue
                cur = need.get(key)
                if cur is None or cur[0] < val:
                    need[key] = (val, Dp)
            for key, (val, Dp) in sorted(need.items(), key=lambda kv: -kv[1][1].gidx):
                if K.get(key, -1) >= val:
                    continue
                op.waits.append(Dp)
                Dp.signal = True
                for k2, v2 in Dp.K.items():
                    if K.get(k2, -1) < v2:
                        K[k2] = v2
                K[key] = val
            op.K = K
            eng_K[op.eng] = K
        cnt = {}
        for e in self.ENGS:
            for op in eng_ops[e]:
                if op.kind == "dma":
                    op.sig = (self.stream_sem[op.stream], 16 * (op.sidx + 1), 16)
                elif op.signal:
                    sem = self.eng_sems[e][op.epoch % self.nsem]
                    c = cnt.get(id(sem), 0) + 1
                    cnt[id(sem)] = c
                    op.sig = (sem, c, 1)
        return eng_ops

    def emit(self):
        eng_ops = self._analyze()
        nc = self.nc
        self.stats = {e: (len(eng_ops[e]), sum(len(o.waits) for o in eng_ops[e])) for e in self.ENGS}

        def run(eh, name):
            for op in eng_ops[name]:
                for Dp in op.waits:
                    eh.wait_ge(Dp.sig[0], Dp.sig[1])
                ins = op.fn(eh)
                if op.sig is not None:
                    ins.then_inc(op.sig[0], op.sig[2])
            if name == "sp":
                for s in sorted(self.out_streams):
                    eh.wait_ge(self.stream_sem[s], 16 * self.stream_cnt[s])

        with nc.Block() as block:
            @block.tensor
            def _(e):
                run(e, "pe")

            @block.scalar
            def _(e):
                run(e, "act")

            @block.vector
            def _(e):
                run(e, "dve")

            @block.gpsimd
            def _(e):
                run(e, "pool")

            @block.sync
            def _(e):
                run(e, "sp")


def unit_sizes():
    sz = [4096, 3072, 4096, 512]
    for _ in range(4):
        sz += [4096, 2048]
    sz += [4096, 4096]
    sz += [4096] * 11
    sz += [4096] * 5 + [2048]
    return sz


U_Q, U_KV, U_PIN, U_POOL = 0, 1, 2, 3
def U_EG(mp): return 4 + 2 * mp
def U_EB(mp): return 5 + 2 * mp
U_OUT = 12
U_UP = 14
U_DN = 25


def build_nc(n_groups_limit=None):
    nc = bass.Bass("TRN2", target_bir_lowering=False)

    def din(name, shape, dt=F32):
        return nc.dram_tensor(name, list(shape), dt, kind="ExternalInput").ap()

    def dout(name, shape, dt=F32):
        return nc.dram_tensor(name, list(shape), dt, kind="ExternalOutput").ap()

    xp = din("xp", [NPT * 128, D])
    xsm = din("xsm", [TS, D])
    wun = din("wun", [2 * UNITS_PER_LAYER, 128, UEL])
    pvec = din("pvec", [128, 2 * PVL])
    biasA_d = din("biasA", [128, 8, 128])
    biasB_d = din("biasB", [128, 8, 128])
    tcur_d = din("tcur", [128, 4, 128])
    tprev_d = din("tprev", [128, 4, 128])
    tfirst_d = din("tfirst", [128, 4, 128])
    bones_d = din("bones", [128, 128])
    ident_d = din("ident", [128, 128])
    hv_d = din("hv", [128, 1])
    ckT_d = din("ckT", [2, 128, 4, 128])
    cv_d = din("cv", [2, 128, 2, 64])
    sph_d = din("sph", [2, 128, 512])
    scT_d = din("scT", [2, 128, NCH, 2])
    cktm_d = din("cktm", [2, 128, 128])

    yp_o = dout("yp", [OWN_TILES * 128, D])
    ys_o = dout("ys", [TS, D])
    kTo_o = dout("kTo", [2, 64, 2, 128])
    vo_o = dout("vo", [2, 128, 128])
    po_o = dout("po", [2, 128, 512])
    co_o = dout("co", [2, 128, NCH, 2])
    kTs_o = dout("kTs", [2, 64, 2, TS])
    kcp_o = dout("kcp", [2, 128 - TS, 128])
    vs_o = dout("vs", [2, 128, 128])
    ps_o = dout("pso", [2, TS, 512])
    cs_o = dout("cso", [2, 128, NCH, 2])

    wsc = nc.dram_tensor("wsc", [2 * UNITS_PER_LAYER, 128, UEL], BF16, kind="Internal").ap()
    USZ = unit_sizes()

    es = ExitStack()
    with es:
        def sb(name, shape, dt):
            return es.enter_context(nc.sbuf_tensor(name, list(shape), dt))

        S = Sched(nc, es)

        xsl = [sb(f"xsl{i}", [128, D], F32) for i in range(NXS)]
        xh = [sb(f"xh{i}", [128, D], BF16) for i in range(4)]
        ssq = sb("ssq", [128, 4], F32)
        srs = sb("srs", [128, 4], F32)
        rstd = sb("rstd", [128, 4], F32)
        xnT = sb("xnT", [128, 8, 512], BF16)
        qT = sb("qT", [128, 4, 512], BF16)
        kTw = sb("kTw", [128, 4, 512], BF16)
        kcar = [sb(f"kcar{l}", [128, 4, 128], BF16) for l in range(2)]
        ksam = sb("ksam", [128, 4, TS], BF16)
        kcache = [sb(f"kcache{l}", [128, 4, 128], BF16) for l in range(2)]
        sq = [sb(f"sq{i}", [128, 512], BF16) for i in range(2)]
        srt = [sb(f"srt{i}", [128, 512], F32) for i in range(2)]
        Vw = sb("Vw", [128, 4, 2, 66], BF16)
        Vcar = [sb(f"Vcar{l}", [128, 2, 66], BF16) for l in range(2)]
        Vsam = sb("Vsam", [128, 2, 66], BF16)
        Vcache = [sb(f"Vcache{l}", [128, 2, 66], BF16) for l in range(2)]
        pinw = sb("pinw", [128, 4, 512], BF16)
        pincar = [sb(f"pincar{l}", [128, 512], BF16) for l in range(2)]
        pinsam = sb("pinsam", [128, 512], BF16)
        pinhist = [sb(f"pinhist{l}", [128, 512], BF16) for l in range(2)]
        dT = sb("dT", [128, 4, 512], BF16)
        plT = sb("plT", [128, 4, 512], BF16)
        PT = [sb(f"PT{i}", [128, 2, 2, 4, 128], BF16) for i in range(2)]
        atok = [sb(f"atok{i}", [128, 512], BF16) for i in range(2)]
        den = sb("den", [128, 8], F32)
        rden = sb("rden", [128, 8], F32)
        aT = sb("aT", [128, 4, 512], BF16)
        s0b = sb("s0b", [128, 512], F32)
        s1b = sb("s1b", [128, 512], F32)
        mixT = sb("mixT", [128, 8, 512], BF16)
        ug = [sb(f"ug{i}", [128, 512], F32) for i in range(2)]
        uv = [sb(f"uv{i}", [128, 512], F32) for i in range(2)]
        actT = sb("actT", [128, 22, 512], BF16)
        hist = {(h, l): sb(f"hist{h}{l}", [128, NCH, 2], F32) for h in "ps" for l in range(2)}
        corr = {h: sb(f"corr{h}", [128, NCH, 2], F32) for h in "ps"}
        ctmp = sb("ctmp", [128, NCH], F32)
        pv = sb("pv", [128, 2 * PVL], F32)
        esink = sb("esink", [128, 16], F32)
        epst = sb("epst", [128, 1], F32)
        hvt = sb("hvt", [128, 1], F32)
        biasA = sb("biasA_s", [128, 8, 128], BF16)
        biasB = sb("biasB_s", [128, 8, 128], BF16)
        tcur = sb("tcur_s", [128, 4, 128], BF16)
        tprev = sb("tprev_s", [128, 4, 128], BF16)
        tfirst = sb("tfirst_s", [128, 4, 128], BF16)
        bones = sb("bones_s", [128, 128], BF16)
        ident = sb("ident_s", [128, 128], BF16)
        wslot = [sb(f"wslot{i}", [128, UEL], BF16) for i in range(NSLOT)]
        kn32 = {(h, l, g): sb(f"kn32{h}{l}{g}", [128, 128 if h == "p" else TS], F32)
                for h in "ps" for l in range(2) for g in range(2)}
        v32 = {(h, l): sb(f"v32{h}{l}", [128, 128], F32) for h in "ps" for l in range(2)}
        _pin32 = [sb(f"pin32_{l}", [128, 512], F32) for l in range(2)]
        pin32 = {(h, l): _pin32[l] for h in "ps" for l in range(2)}

        psb = [es.enter_context(nc.psum_tensor(f"psb{i}", [128, 512], F32)) for i in range(8)]
        pstate = {"i": 0}

        def pbank():
            i = min(range(8), key=lambda b: S.touch.get(f"ps{b}", -1 - (8 - b)))
            S.touch[f"ps{i}"] = len(S.ops)
            pstate["i"] = (pstate["i"] + 1) % 8
            return i

        def pvc(l, off, n=1):
            return pv[:, l * PVL + off: l * PVL + off + n]
        OFF_G = [0, 8]; OFF_GB = [16, 24]; OFF_PS = 32
        OFF_CW = [36, 80, 124]; OFF_CB = 168; OFF_GQ = 212; OFF_GK = 213; OFF_SK = 214

        S.dma("sp", lambda e: e.dma_start(out=pv[:], in_=pvec), writes=["pv"], stream="setup0")
        S.dma("sp", lambda e: e.dma_start(out=hvt[:], in_=hv_d), writes=["hvt"], stream="setup1")
        def const_load(nm, dst, src):
            S.dma("pool", lambda e, dst=dst, src=src: e.dma_start(out=dst[:], in_=src), writes=[nm], stream="c_" + nm[:5] + nm[-1])
        for (nm, dst, src) in (("ident", ident, ident_d), ("bones", bones, bones_d)):
            const_load(nm, dst, src)

        def late_setup():
          for (nm, dst, src) in (("tcur", tcur, tcur_d), ("tprev", tprev, tprev_d), ("tfirst", tfirst, tfirst_d),
                                 ("biasA", biasA, biasA_d), ("biasB", biasB, biasB_d)):
              const_load(nm, dst, src)
          for l in range(2):
            S.dma("pool", lambda e, l=l: e.dma_start(out=kcache[l][:], in_=ckT_d[l]), writes=[f"kcache{l}"], stream="c_kc")
            S.dma("pool", lambda e, l=l: e.dma_start(out=Vcache[l][:, :, 0:64], in_=cv_d[l]), writes=[f"Vcache{l}"], stream="c_vc")
            S.dma("pool", lambda e, l=l: e.dma_start(out=pinhist[l][:], in_=sph_d[l]), writes=[f"pinhist{l}"], stream="c_ph")
            S.dma("sp", lambda e, l=l: e.dma_start(out=hist[("s", l)][:], in_=scT_d[l]), writes=[f"hists{l}"], stream="setup2")
            S.op("pool", lambda e, l=l: e.memset(Vcache[l][:, :, 64:66], 1.0), writes=[f"Vcache{l}"])
            S.op("pool", lambda e, l=l: e.memset(kcar[l][:], 0.0), writes=[f"kcar{l}"])
            S.op("pool", lambda e, l=l: e.memset(Vcar[l][:], 0.0), writes=[f"Vcar{l}"])
            S.op("pool", lambda e, l=l: e.memset(pincar[l][:], 0.0), writes=[f"pincar{l}"])
            S.op("pool", lambda e, l=l: e.memset(hist[("p", l)][:], 0.0), writes=[f"histp{l}"])
            S.dma("sp", lambda e, l=l: e.dma_start(out=vs_o[l, 0:128 - TS, :],
                                                   in_=cv_d[l, TS:128].rearrange("s g d -> s (g d)")),
                  stream="misc_out", is_output=True)
            S.dma("sp", lambda e, l=l: e.dma_start(out=kcp_o[l], in_=cktm_d[l, TS:128, :]),
                  stream="misc_out", is_output=True)
        S.op("pool", lambda e: e.memset(epst[:], EPS), writes=["epst"])
        S.op("pool", lambda e: e.memset(kTw[:], 0.0), writes=["kTw0", "kTw1"])
        S.op("pool", lambda e: e.memset(ksam[:], 0.0), writes=["ksam0", "ksam1"])
        S.op("pool", lambda e: e.memset(Vsam[:], 1.0), writes=["Vsam"])
        S.op("pool", lambda e: e.memset(Vw[:], 0.0), writes=[f"Vw{t}" for t in range(4)])
        for t in range(4):
            S.op("dve", lambda e, t=t: e.tensor_copy(out=Vw[:, t, :, 64:65], in_=hvt[:, 0:1].unsqueeze(1).to_broadcast([128, 2, 1])),
                 reads=["hvt"], writes=[f"Vw{t}"])
        for l in range(2):
            S.op("act", lambda e, l=l: e.activation(out=esink[:, l * 8:(l + 1) * 8], in_=pvc(l, OFF_SK, 8), func=ACTF.Exp),
                 reads=["pv"], writes=["esink"])

        n_groups = 1 + OWN_TILES // 4
        if n_groups_limit is not None:
            n_groups = n_groups_limit
        useq = [(g, l, u) for g in range(n_groups) for l in range(2) for u in range(UNITS_PER_LAYER)]
        wst = {"next": 0}

        def issue_load():
            i = wst["next"]
            if i >= len(useq):
                return
            wst["next"] = i + 1
            g, l, u = useq[i]
            slot = i % NSLOT
            gu = l * UNITS_PER_LAYER + u
            n = USZ[u]
            if g == 0:
                S.dma("pool", lambda e: e.dma_start(out=wslot[slot][:, 0:n], in_=wun[gu, :, 0:n]),
                      writes=[f"ws{slot}"], stream=f"wp{slot}")
                S.dma("sp", lambda e: e.dma_start(out=wsc[gu, :, 0:n], in_=wslot[slot][:, 0:n]),
                      reads=[f"ws{slot}"], writes=[f"wsc{gu}"], stream=f"ww{slot}")
            else:
                S.dma("sp", lambda e: e.dma_start(out=wslot[slot][:, 0:n], in_=wsc[gu, :, 0:n]),
                      reads=[f"wsc{gu}"], writes=[f"ws{slot}"], stream=f"ws{slot}")

        ucur = {"i": 0}

        class WU:
            def __init__(self):
                self.base = ucur["i"]

            def get(self, u):
                i = self.base + u
                slot = i % NSLOT
                return wslot[slot], f"ws{slot}"

            def done(self, u):
                issue_load()

        for _ in range(NSLOT):
            issue_load()

        groups = []
        g0 = {"tiles": [], "N": 3 * 128 + TS, "segs": [(0, 384, "p"), (384, 384 + TS, "s")], "np": 3}
        T = 0
        for i in range(3):
            g0["tiles"].append(dict(kind="p", n=128, col=i * 128, row=i * 128, halo=True, wslot=i, T=T, first=False, last=False))
            T += 1
        g0["tiles"].append(dict(kind="s", n=TS, col=384, row=0, halo=False, wslot=3, T=T, first=False, last=False))
        T += 1
        groups.append(g0)
        for gi in range(OWN_TILES // 4):
            gg = {"tiles": [], "N": 512, "segs": [(0, 512, "p")], "np": 4}
            for i in range(4):
                ot = gi * 4 + i
                gg["tiles"].append(dict(kind="p", n=128, col=i * 128, row=(3 + ot) * 128, halo=False, wslot=i, T=T,
                                        first=(ot == 0), last=(ot == OWN_TILES - 1), orow=ot * 128))
                T += 1
            groups.append(gg)
        groups = groups[:n_groups]
        if n_groups_limit is not None:
            groups[-1]["tiles"][-1]["last"] = True

        def xs_of(t):
            return xsl[t["T"] % NXS], f"x{t['T'] % NXS}"

        all_tiles = [t for g_ in groups for t in g_["tiles"]]

        xloaded = set()

        def load_x_tile(Tn):
            if Tn >= len(all_tiles):
                return
            xloaded.add(Tn)
            t = all_tiles[Tn]
            xt, xr = xs_of(t)
            if t["kind"] == "p":
                S.dma("sp", lambda e, xt=xt, t=t: e.dma_start(out=xt[:, :], in_=xp[t["row"]:t["row"] + 128, :]),
                      writes=[xr], stream=xr)
            else:
                S.dma("sp", lambda e, xt=xt: e.dma_start(out=xt[0:TS, :], in_=xsm), writes=[xr], stream=xr)

        pre_done = set()

        def norm_pre(l, ni, t, ti):
            pre_done.add((t["T"], l, ni))
            xt, xr = xs_of(t)
            n = t["n"]
            b = ti
            S.op("act", lambda e: e.activation(out=xh[b][0:n, :], in_=xt[0:n, :], func=ACTF.Square, scale=1.0 / 32.0,
                                               accum_out=ssq[0:n, ti:ti + 1]), reads=[xr], writes=[f"xh{b}", f"ssq{ti}"])
            S.op("act", lambda e: e.activation(out=srs[0:n, ti:ti + 1], in_=ssq[0:n, ti:ti + 1], func=ACTF.Ln,
                                               bias=epst[0:n, 0:1], scale=1.0), reads=[f"ssq{ti}", "epst"], writes=[f"srs{ti}"])
            S.op("act", lambda e: e.activation(out=rstd[0:n, ti:ti + 1], in_=srs[0:n, ti:ti + 1], func=ACTF.Exp, scale=-0.5),
                 reads=[f"srs{ti}"], writes=[f"rstd{ti}"])
            S.op("dve", lambda e: e.tensor_scalar(out=xh[b][0:n, :], in0=xt[0:n, :], scalar1=rstd[0:n, ti:ti + 1], scalar2=None,
                                                  op0=ALU.mult),
                 reads=[xr, f"rstd{ti}"], writes=[f"xh{b}"])

        post_done = set()
        pending_post = []

        def norm_post(l, ni, t, ti):
            post_done.add((t["T"], l, ni))
            n = t["n"]
            b = ti
            cs = slice(t["col"], t["col"] + n)
            bk = pbank()
            pT = psb[bk][:].bitcast(BF16).rearrange("p (k t) -> p k t", k=8)
            for k in range(8):
                S.op("pe", lambda e, k=k: e.transpose(out=pT[:, k, 0:n], in_=xh[b][0:n, k * 128:(k + 1) * 128], identity=ident[0:n, 0:n]),
                     reads=[f"xh{b}", "ident"], writes=[f"ps{bk}"])
            gv = pvc(l, OFF_G[ni], 8).unsqueeze(2).to_broadcast([128, 8, n])
            S.op("dve", lambda e: e.tensor_tensor(out=xnT[:, :, cs], in0=pT[:, :, 0:n], in1=gv, op=ALU.mult),
                 reads=[f"ps{bk}", "pv"], writes=[f"xnT{ti}"])

        def xnT_res(grp):
            return [f"xnT{ti}" for ti in range(len(grp["tiles"]))]

        qkst = {"n": 0, "pend": None}

        def qk_flush():
            if qkst["pend"] is not None:
                args = qkst["pend"]
                qkst["pend"] = None
                qk_post(*args)

        def qk_norm(l, bk, N, gain_off, dests, xres, extra32=None):
            b = qkst["n"] % 2
            qkst["n"] += 1
            S.op("act", lambda e: e.activation(out=sq[b][:, 0:N], in_=psb[bk][:, 0:N], func=ACTF.Square),
                 reads=[f"ps{bk}"], writes=[f"sq{b}"])
            qkst["pend"] = (l, bk, N, gain_off, dests, b)

        def qk_post(l, bk, N, gain_off, dests, b):
            bk2 = pbank()
            S.op("pe", lambda e: e.matmul(out=psb[bk2][:, 0:N], lhsT=bones[:, :], rhs=sq[b][:, 0:N], start=True, stop=True),
                 reads=[f"sq{b}", "bones"], writes=[f"ps{bk2}"])
            S.op("act", lambda e: e.activation(out=srt[b][:, 0:N], in_=psb[bk2][:, 0:N], func=ACTF.Ln, bias=epst[:, 0:1], scale=1.0),
                 reads=[f"ps{bk2}", "epst"], writes=[f"srt{b}"])
            S.op("act", lambda e: e.activation(out=srt[b][:, 0:N], in_=srt[b][:, 0:N], func=ACTF.Exp, scale=-0.5),
                 reads=[f"srt{b}"], writes=[f"srt{b}"])
            gain = pvc(l, gain_off, 1)
            for dd in dests:
                (a, bb, dst, rn) = dd[:4]
                p0, p1 = dd[4] if len(dd) > 4 else (0, 128)
                S.op("dve", lambda e, a=a, bb=bb, dst=dst, p0=p0, p1=p1: e.scalar_tensor_tensor(
                    out=dst, in0=psb[bk][p0:p1, a:bb], scalar=gain[p0:p1, :], in1=srt[b][p0:p1, a:bb], op0=ALU.mult, op1=ALU.mult),
                     reads=[f"ps{bk}", f"srt{b}", "pv"], writes=[rn])

        def stage_B(l, grp, W):
            N = grp["N"]
            tiles = grp["tiles"]
            xres = xnT_res(grp)
            wq, wqr = W.get(U_Q)
            wqv = wq[:].rearrange("p (k c) -> p k c", k=8)
            def q_job(j):
                bk = pbank()
                for k in range(8):
                    S.op("pe", lambda e, j=j, k=k, bk=bk: e.matmul(out=psb[bk][:, 0:N], lhsT=wqv[:, k, j * 128:(j + 1) * 128],
                                                                   rhs=xnT[:, k, 0:N], start=(k == 0), stop=(k == 7)),
                         reads=[wqr] + xres, writes=[f"ps{bk}"])
                qk_flush()
                qk_norm(l, bk, N, OFF_GQ, [(0, N, qT[:, j, 0:N], f"qT{j}")], xres)
                if j == 3:
                    W.done(U_Q)
            wkv, wkvr = W.get(U_KV)
            wkvv = wkv[:, 0:3072].rearrange("p (k c) -> p k c", k=8)
            npr = grp["np"]
            def k_job(g):
                bk = pbank()
                for (c0, c1, rr) in ((0, N, xres),):
                    for k in range(8):
                        S.op("pe", lambda e, g=g, k=k, bk=bk, c0=c0, c1=c1: e.matmul(out=psb[bk][:, c0:c1], lhsT=wkvv[:, k, g * 128:(g + 1) * 128],
                                                                       rhs=xnT[:, k, c0:c1], start=(k == 0), stop=(k == 7)),
                             reads=[wkvr] + rr, writes=[f"ps{bk}"])
                dests = [(0, npr * 128, kTw[0:64, 2 * g, 0:npr * 128], f"kTw{g}", (0, 64)),
                         (0, npr * 128, kTw[64:128, 2 * g + 1, 0:npr * 128], f"kTw{g}", (64, 128))]
                for t in tiles:
                    if t["kind"] == "s":
                        dests.append((t["col"], t["col"] + TS, ksam[0:64, 2 * g, :], f"ksam{g}", (0, 64)))
                        dests.append((t["col"], t["col"] + TS, ksam[64:128, 2 * g + 1, :], f"ksam{g}", (64, 128)))
                for t in tiles:
                    if t["kind"] == "s" or t["last"]:
                        h = t["kind"]
                        dests.append((t["col"], t["col"] + t["n"], kn32[(h, l, g)][:, :], f"kn32{h}{l}{g}"))
                qk_flush()
                qk_norm(l, bk, N, OFF_GK, dests, xres)

            def k_state_out():
                qk_flush()
                for g in range(2):
                    for t in tiles:
                        if t["kind"] == "s":
                            S.dma("sp", lambda e, g=g: e.dma_start(out=kTs_o[l, :, g, :], in_=kn32[("s", l, g)][0:64, :]),
                                  reads=[f"kn32s{l}{g}"], stream="misc_out", is_output=True)
                        elif t["last"]:
                            S.dma("sp", lambda e, g=g: e.dma_start(out=kTo_o[l, :, g, :], in_=kn32[("p", l, g)][0:64, :]),
                                  reads=[f"kn32p{l}{g}"], stream="misc_out", is_output=True)
            wpin, wpinr = W.get(U_PIN)
            wpinv = wpin[:].rearrange("p (k c) -> p k c", k=8)

            def tile_job(ti, t):
                n = t["n"]
                cs = slice(t["col"], t["col"] + n)
                bk = pbank()
                for k in range(8):
                    S.op("pe", lambda e, k=k, bk=bk, cs=cs, n=n: e.matmul(out=psb[bk][0:n, 0:128], lhsT=xnT[:, k, cs], rhs=wkvv[:, k, 256:384],
                                                                         start=(k == 0), stop=(k == 7)),
                         reads=[wkvr, f"xnT{ti}"], writes=[f"ps{bk}"])
                if t["kind"] == "p":
                    vdst = Vw[0:n, t["wslot"], :, 0:64]; vres = f"Vw{t['wslot']}"
                else:
                    vdst = Vsam[0:n, :, 0:64]; vres = "Vsam"
                src = psb[bk][0:n, 0:128].rearrange("p (g d) -> p g d", g=2)
                S.op("act", lambda e, vdst=vdst, src=src: e.activation(out=vdst, in_=src, func=ACTF.Copy),
                     reads=[f"ps{bk}"], writes=[vres])
                if DEBUG_STOP == "B3":
                    return
                if t["kind"] == "s" or t["last"]:
                    h = t["kind"]
                    S.op("act", lambda e, bk=bk, n=n, h=h: e.activation(out=v32[(h, l)][0:n, :], in_=psb[bk][0:n, 0:128], func=ACTF.Copy),
                         reads=[f"ps{bk}"], writes=[f"v32{h}{l}"])
                    if h == "s":
                        S.dma("sp", lambda e: e.dma_start(out=vs_o[l, 128 - TS:128, :], in_=v32[("s", l)][0:TS, :]),
                              reads=[f"v32s{l}"], stream="misc_out", is_output=True)
                    else:
                        S.dma("sp", lambda e: e.dma_start(out=vo_o[l], in_=v32[("p", l)][:, :]),
                              reads=[f"v32p{l}"], stream="misc_out", is_output=True)
                if DEBUG_STOP == "B4":
                    return
                bk = pbank()
                for k in range(8):
                    S.op("pe", lambda e, k=k, bk=bk, cs=cs, n=n: e.matmul(out=psb[bk][0:n, 0:512], lhsT=xnT[:, k, cs], rhs=wpinv[:, k, :],
                                                                         start=(k == 0), stop=(k == 7)),
                         reads=[wpinr, f"xnT{ti}"], writes=[f"ps{bk}"])
                if t["kind"] == "p":
                    pdst = pinw[0:n, t["wslot"], :]; pres = f"pinw{t['wslot']}"
                else:
                    pdst = pinsam[0:n, :]; pres = "pinsam"
                S.op("dve", lambda e, pdst=pdst, bk=bk, n=n: e.tensor_copy(out=pdst, in_=psb[bk][0:n, 0:512]),
                     reads=[f"ps{bk}"], writes=[pres])
                if DEBUG_STOP == "B5":
                    return
                if t["kind"] == "s" or t["last"]:
                    h = t["kind"]
                    S.op("act", lambda e, bk=bk, n=n, h=h: e.activation(out=pin32[(h, l)][0:n, :], in_=psb[bk][0:n, 0:512], func=ACTF.Copy),
                         reads=[f"ps{bk}"], writes=[f"pin32{l}"])
                    if h == "s":
                        S.dma("sp", lambda e: e.dma_start(out=ps_o[l], in_=pin32[("s", l)][0:TS, :]),
                              reads=[f"pin32{l}"], stream="misc_out", is_output=True)
                    else:
                        S.dma("sp", lambda e: e.dma_start(out=po_o[l], in_=pin32[("p", l)][:, :]),
                              reads=[f"pin32{l}"], stream="misc_out", is_output=True)
            _tile_job = tile_job

            def tile_job(ti, t):
                _tile_job(ti, t)
                qk_flush()
                if ti > 0:
                    pool_toep(l, grp, ti - 1)
            cjobs = [lambda g=g: k_job(g) for g in range(2)] + [lambda j=j: q_job(j) for j in range(4)]
            tjobs = [lambda ti=ti, t=t: tile_job(ti, t) for ti, t in enumerate(tiles)]
            order = []
            tj = 0
            if pending_post:
                n_early = max(len(tjobs) - 1, 0)
                order += tjobs[:n_early]
                tj = n_early
                order.append(lambda: [p() for p in [pending_post.pop(0) for _ in range(len(pending_post))]])
            if pending_post or tj > 0:
                order += cjobs
                ci = len(cjobs)
            else:
                order += [cjobs[0], cjobs[1]]
                ci = 2
            while ci < len(cjobs) or tj < len(tjobs):
                if tj < len(tjobs):
                    order.append(tjobs[tj]); tj += 1
                if ci < len(cjobs):
                    order.append(cjobs[ci]); ci += 1
            for jb in order:
                jb()
            k_state_out()
            pool_toep(l, grp, len(tiles) - 1)
            W.done(U_KV)
            W.done(U_PIN)

        def cstart(l, grp):
            if grp is groups[0] and n_groups_limit != 1:
                return 128 if l == 0 else 256
            return 0

        def prev_of(l, t):
            if t["kind"] == "s":
                return (kcache[l], [f"kcache{l}"]), (Vcache[l], f"Vcache{l}"), (pinhist[l], f"pinhist{l}")
            w = t["wslot"]
            if w == 0:
                return (kcar[l], [f"kcar{l}"]), (Vcar[l], f"Vcar{l}"), (pincar[l], f"pincar{l}")
            return ((kTw[:, :, (w - 1) * 128: w * 128], ["kTw0", "kTw1"]), (Vw[:, w - 1], f"Vw{w - 1}"),
                    (pinw[:, w - 1, :], f"pinw{w - 1}"))

        def cur_of(l, t):
            if t["kind"] == "s":
                return (ksam, ["ksam0", "ksam1"]), (Vsam, "Vsam"), (pinsam, "pinsam")
            w = t["wslot"]
            return ((kTw[:, :, w * 128:(w + 1) * 128], ["kTw0", "kTw1"]), (Vw[:, w], f"Vw{w}"), (pinw[:, w, :], f"pinw{w}"))

        def ap3(x):
            return x if not hasattr(x, "ap") or True else x

        def pool_toep(l, grp, ti):
            if grp["tiles"][ti]["col"] >= cstart(l, grp):
                t = grp["tiles"][ti]
                n = t["n"]
                cs = slice(t["col"], t["col"] + n)
                (_, _), (_, _), (pp, ppr) = prev_of(l, t)
                (_, _), (_, _), (pc, pcr) = cur_of(l, t)
                tc_tab, tcr = (tfirst, "tfirst") if t["first"] else (tcur, "tcur")
                bk = pbank()
                pv4 = psb[bk][:].rearrange("p (g t) -> p g t", g=4)
                for g in range(4):
                    S.op("pe", lambda e, g=g, n=n, pp=pp, pv4=pv4: e.matmul(out=pv4[:, g, 0:n], lhsT=pp[:, g * 128:(g + 1) * 128], rhs=tprev[:, g, 0:n],
                                                                   start=True, stop=False),
                         reads=[ppr, "tprev"], writes=[f"ps{bk}"])
                    S.op("pe", lambda e, g=g, n=n, pc=pc, tc_tab=tc_tab, pv4=pv4: e.matmul(out=pv4[:, g, 0:n], lhsT=pc[0:n, g * 128:(g + 1) * 128],
                                                                                  rhs=tc_tab[0:n, g, 0:n], start=False, stop=True),
                         reads=[pcr, tcr], writes=[f"ps{bk}"])
                S.op("act", lambda e, n=n, cs=cs, pv4=pv4: e.activation(out=dT[:, :, cs], in_=pv4[:, :, 0:n], func=ACTF.Copy),
                     reads=[f"ps{bk}"], writes=[f"dT{ti}"])

        def pool_proj(l, grp, W):
            N = grp["N"]
            C0 = cstart(l, grp)
            wp, wpr = W.get(U_POOL)
            dres = [f"dT{ti}" for ti, t_ in enumerate(grp["tiles"]) if t_["col"] >= C0]
            for g in range(4):
                bk = pbank()
                S.op("pe", lambda e, g=g, bk=bk: e.matmul(out=psb[bk][:, C0:N], lhsT=wp[:, g * 128:(g + 1) * 128], rhs=dT[:, g, C0:N],
                                                          start=True, stop=True),
                     reads=[wpr] + dres, writes=[f"ps{bk}"])
                S.op("dve", lambda e, g=g, bk=bk: e.tensor_scalar(out=plT[:, g, C0:N], in0=psb[bk][:, C0:N], scalar1=pvc(l, OFF_PS + g, 1),
                                                                  scalar2=None, op0=ALU.mult),
                     reads=[f"ps{bk}", "pv"], writes=[f"plT{g}"])
            W.done(U_POOL)

        def attn_scores(l, t, pbi):
            n = t["n"]
            cs = slice(t["col"], t["col"] + n)
            (kp, kpr), _, _ = prev_of(l, t)
            (kc, kcr), _, _ = cur_of(l, t)
            nkp, nkc = 128, n
            for grp_ in range(2):
                for X, (kx, kxr, nk, btab, bres) in enumerate(((kp, kpr, nkp, biasA, "biasA"), (kc, kcr, nkc, biasB, "biasB"))):
                    bk = pbank()
                    Sv = psb[bk][0:nk, 0:4 * n].rearrange("p (h q) -> p h q", h=4)
                    S.op("pe", lambda e, Sv=Sv, nk=nk, btab=btab, grp_=grp_: e.matmul(
                        out=Sv, lhsT=ident[0:nk, 0:nk], rhs=btab[0:nk, grp_ * 4:(grp_ + 1) * 4, 0:n], start=True, stop=False),
                        reads=["ident", bres], writes=[f"ps{bk}"])
                    for hh in range(4):
                        h = grp_ * 4 + hh
                        hb = (h % 2) * 64
                        S.op("pe", lambda e, Sv=Sv, hh=hh, hb=hb, kx=kx, nk=nk, grp_=grp_, h=h: e.matmul(
                            out=Sv[:, hh, :], lhsT=kx[:, grp_ * 2 + (h % 2), 0:nk], rhs=qT[:, h // 2, cs], start=False, stop=(hh == 3)),
                            reads=kxr + [f"qT{h // 2}"], writes=[f"ps{bk}"])
                    S.op("act", lambda e, Sv=Sv, nk=nk, X=X, grp_=grp_: e.activation(out=PT[pbi][0:nk, X, grp_, :, 0:n], in_=Sv, func=ACTF.Exp, scale=0.125),
                         reads=[f"ps{bk}"], writes=[f"PT{pbi}_{X}{grp_}"])

        def attn_pv(l, t, ti, pbi):
            n = t["n"]
            cs = slice(t["col"], t["col"] + n)
            _, (vp, vpr), _ = prev_of(l, t)
            _, (vc, vcr), _ = cur_of(l, t)
            nkp, nkc = 128, n
            ab = t["T"] % 2
            for grp_ in range(2):
                bk = pbank()
                O = psb[bk][0:n, 0:260].rearrange("p (h e) -> p h e", h=4)
                for hh in range(4):
                    S.op("pe", lambda e, O=O, hh=hh, grp_=grp_, vp=vp: e.matmul(out=O[:, hh, :], lhsT=PT[pbi][0:nkp, 0, grp_, hh, 0:n],
                                                                               rhs=vp[0:nkp, grp_, 0:65], start=True, stop=False),
                         reads=[f"PT{pbi}_0{grp_}", vpr], writes=[f"ps{bk}"])
                    S.op("pe", lambda e, O=O, hh=hh, grp_=grp_, vc=vc: e.matmul(out=O[:, hh, :], lhsT=PT[pbi][0:nkc, 1, grp_, hh, 0:n],
                                                                               rhs=vc[0:nkc, grp_, 0:65], start=False, stop=True),
                         reads=[f"PT{pbi}_1{grp_}", vcr], writes=[f"ps{bk}"])
                dn = den[0:n, grp_ * 4:(grp_ + 1) * 4].unsqueeze(2)
                rd = rden[0:n, grp_ * 4:(grp_ + 1) * 4]
                esk = esink[0:n, l * 8 + grp_ * 4: l * 8 + grp_ * 4 + 4].unsqueeze(2)
                S.op("dve", lambda e, O=O, dn=dn, esk=esk: e.tensor_tensor(out=dn, in0=O[:, :, 64:65], in1=esk, op=ALU.add),
                     reads=[f"ps{bk}", "esink"], writes=[f"den{grp_}"])
                S.op("dve", lambda e, rd=rd, grp_=grp_: e.reciprocal(out=rd, in_=den[0:n, grp_ * 4:(grp_ + 1) * 4]),
                     reads=[f"den{grp_}"], writes=[f"rden{grp_}"])
                S.op("dve", lambda e, O=O, rd=rd, grp_=grp_: e.tensor_tensor(
                    out=atok[ab][0:n, grp_ * 256:(grp_ + 1) * 256].rearrange("p (h d) -> p h d", h=4), in0=O[:, :, 0:64],
                    in1=rd.unsqueeze(2).to_broadcast([n, 4, 64]), op=ALU.mult),
                    reads=[f"ps{bk}", f"rden{grp_}"], writes=[f"atok{ab}"])

        def attn_tr(l, t, ti):
            n = t["n"]
            cs = slice(t["col"], t["col"] + n)
            ab = t["T"] % 2
            bk = pbank()
            pT = psb[bk][:].bitcast(BF16).rearrange("p (k t) -> p k t", k=8)
            for j in range(4):
                S.op("pe", lambda e, j=j, pT=pT: e.transpose(out=pT[:, j, 0:n], in_=atok[ab][0:n, j * 128:(j + 1) * 128], identity=ident[0:n, 0:n]),
                     reads=[f"atok{ab}", "ident"], writes=[f"ps{bk}"])
            S.op("act", lambda e, pT=pT: e.activation(out=aT[:, :, cs], in_=pT[:, 0:4, 0:n], func=ACTF.Copy),
                 reads=[f"ps{bk}"], writes=[f"aT{ti}"])

        def stage_D(l, grp, W):
            sub = [(ti, t) for ti, t in enumerate(grp["tiles"]) if t["col"] >= cstart(l, grp)]
            nt_ = len(sub)
            for i in range(nt_ + 2):
                if i < nt_:
                    attn_scores(l, sub[i][1], i % 2)
                if 0 <= i - 1 < nt_:
                    attn_pv(l, sub[i - 1][1], sub[i - 1][0], (i - 1) % 2)
                if 0 <= i - 2 < nt_:
                    attn_tr(l, sub[i - 2][1], sub[i - 2][0])
                if i == 0:
                    pass

        def stage_E(l, grp, W):
            N = grp["N"]
            C0 = cstart(l, grp)
            nt = len(grp["tiles"])
            xres = [f"xnT{ti}" for ti, t_ in enumerate(grp["tiles"]) if t_["col"] >= C0]
            ares = [f"aT{ti}" for ti, t_ in enumerate(grp["tiles"]) if t_["col"] >= C0]
            pres = [f"plT{g}" for g in range(4)]
            for mp in range(4):
                wg, wgr = W.get(U_EG(mp))
                wb, wbr = W.get(U_EB(mp))
                wgv = wg[:].rearrange("p (k c) -> p k c", k=8)
                wbv = wb[:, 0:2048].rearrange("p (k c) -> p k c", k=4)
                for mi in range(2):
                    m = 2 * mp + mi
                    bYA, bYB, bG0, bG1 = pbank(), pbank(), pbank(), pbank()
                    for gi, bG in enumerate((bG0, bG1)):
                        for k in range(8):
                            S.op("pe", lambda e, k=k, mi=mi, bG=bG, gi=gi, wgv=wgv: e.matmul(
                                out=psb[bG][:, C0:N], lhsT=wgv[:, k, gi * 256 + mi * 128: gi * 256 + (mi + 1) * 128],
                                rhs=xnT[:, k, C0:N], start=(k == 0), stop=(k == 7)),
                                reads=[wgr] + xres, writes=[f"ps{bG}"])
                    for k in range(4):
                        S.op("pe", lambda e, k=k, mi=mi, bYB=bYB, wbv=wbv: e.matmul(out=psb[bYB][:, C0:N], lhsT=wbv[:, k, 256 + mi * 128:256 + (mi + 1) * 128],
                                                                          rhs=plT[:, k, C0:N], start=(k == 0), stop=(k == 3)),
                             reads=[wbr] + pres, writes=[f"ps{bYB}"])
                    for k in range(4):
                        S.op("pe", lambda e, k=k, mi=mi, bYA=bYA, wbv=wbv: e.matmul(out=psb[bYA][:, C0:N], lhsT=wbv[:, k, mi * 128:(mi + 1) * 128],
                                                                          rhs=aT[:, k, C0:N], start=(k == 0), stop=(k == 3)),
                             reads=[wbr] + ares, writes=[f"ps{bYA}"])
                    S.op("act", lambda e, bG0=bG0, m=m: e.activation(out=s0b[:, C0:N], in_=psb[bG0][:, C0:N], func=ACTF.Sigmoid,
                                                                     bias=pvc(l, OFF_GB[0] + m, 1), scale=1.0),
                         reads=[f"ps{bG0}", "pv"], writes=["s0b"])
                    S.op("act", lambda e, bG1=bG1, m=m: e.activation(out=s1b[:, C0:N], in_=psb[bG1][:, C0:N], func=ACTF.Sigmoid,
                                                                     bias=pvc(l, OFF_GB[1] + m, 1), scale=1.0),
                         reads=[f"ps{bG1}", "pv"], writes=["s1b"])
                    S.op("dve", lambda e, bYA=bYA: e.tensor_tensor(out=s0b[:, C0:N], in0=psb[bYA][:, C0:N], in1=s0b[:, C0:N], op=ALU.mult),
                         reads=[f"ps{bYA}", "s0b"], writes=["s0b"])
                    S.op("dve", lambda e, bYB=bYB: e.tensor_tensor(out=s1b[:, C0:N], in0=psb[bYB][:, C0:N], in1=s1b[:, C0:N], op=ALU.mult),
                         reads=[f"ps{bYB}", "s1b"], writes=["s1b"])
                    S.op("dve", lambda e, m=m: e.tensor_tensor(out=mixT[:, m, C0:N], in0=s0b[:, C0:N], in1=s1b[:, C0:N], op=ALU.add),
                         reads=["s0b", "s1b"], writes=[f"mixT{m}"])
                W.done(U_EG(mp))
                W.done(U_EB(mp))

        def resid_add(grp, t, bk, cb):
            xt, xr = xs_of(t)
            n = t["n"]
            xv = xt[0:n, cb * 512:(cb + 1) * 512]
            if grp is groups[0] and t["kind"] == "p":
                S.op("dve", lambda e: e.scalar_tensor_tensor(out=xv, in0=psb[bk][0:n, 0:512], scalar=hvt[0:n, 0:1], in1=xv,
                                                             op0=ALU.mult, op1=ALU.add),
                     reads=[f"ps{bk}", xr, "hvt"], writes=[xr])
            else:
                S.op("dve", lambda e: e.tensor_tensor(out=xv, in0=psb[bk][0:n, 0:512], in1=xv, op=ALU.add),
                     reads=[f"ps{bk}", xr], writes=[xr])

        def stage_F(l, grp, W):
            sub = [(ti, t) for ti, t in enumerate(grp["tiles"]) if t["col"] >= cstart(l, grp)]
            nt_ = len(sub)
            wovs = []
            for cb in range(2):
                wo, wor = W.get(U_OUT + cb)
                wovs.append((wo[:].rearrange("p (k c) -> p k c", k=8), wor))
            for si in range(nt_ + 2):
                if si < nt_:
                    ti, t = sub[si]
                    n = t["n"]
                    cs = slice(t["col"], t["col"] + n)
                    for cb in range(2):
                        wov, wor = wovs[cb]
                        bk = pbank()
                        for k in range(8):
                            S.op("pe", lambda e, k=k, bk=bk, cs=cs, n=n, wov=wov: e.matmul(out=psb[bk][0:n, 0:512], lhsT=mixT[:, k, cs], rhs=wov[:, k, :],
                                                                                 start=(k == 0), stop=(k == 7)),
                                 reads=[wor, f"mixT{k}"], writes=[f"ps{bk}"])
                        resid_add(grp, t, bk, cb)
                    norm_pre(l, 1, t, ti)
                if 0 <= si - 2 < nt_:
                    norm_post(l, 1, sub[si - 2][1], sub[si - 2][0])
            W.done(U_OUT)
            W.done(U_OUT + 1)

        def stage_G(l, grp, W):
            N = grp["N"]
            C0 = cstart(l, grp)
            xres = [f"xnT{ti}" for ti, t_ in enumerate(grp["tiles"]) if t_["col"] >= C0]
            segs = [(max(a, C0), bb, hid) for (a, bb, hid) in grp["segs"] if bb > C0]
            cw0 = pvc(l, OFF_CW[0], NCH); cw1 = pvc(l, OFF_CW[1], NCH)
            for (a, bb, hid) in segs:
                hs = hist[(hid, l)]
                hr = f"hist{hid}{l}"
                cr = corr[hid]
                S.op("pool", lambda e, hs=hs, cr=cr: e.tensor_tensor(out=cr[:, :, 0], in0=hs[:, :, 0], in1=cw0, op=ALU.mult),
                     reads=[hr, "pv"], writes=[f"corr{hid}"])
                S.op("pool", lambda e, hs=hs: e.tensor_tensor(out=ctmp[:, :], in0=hs[:, :, 1], in1=cw1, op=ALU.mult),
                     reads=[hr, "pv"], writes=["ctmp"])
                S.op("pool", lambda e, cr=cr: e.tensor_tensor(out=cr[:, :, 0], in0=cr[:, :, 0], in1=ctmp[:, :], op=ALU.add),
                     reads=[f"corr{hid}", "ctmp"], writes=[f"corr{hid}"])
                S.op("pool", lambda e, hs=hs, cr=cr: e.tensor_tensor(out=cr[:, :, 1], in0=hs[:, :, 1], in1=cw0, op=ALU.mult),
                     reads=[hr, "pv"], writes=[f"corr{hid}"])
            for f in range(11):
                wu, wur = W.get(U_UP + f)
                wuv = wu[:].rearrange("p (k c) -> p k c", k=8)
                for pi in range(2):
                    j = 2 * f + pi
                    ub = j % 2
                    for half, (c, ubuf, ures) in enumerate(((j, ug[ub], f"ug{ub}"), (22 + j, uv[ub], f"uv{ub}"))):
                        bk = pbank()
                        csp = grp["tiles"][-1]["col"]
                        splits = ((C0, csp, xres[:-1]), (csp, N, xres[-1:])) if (j == 0 and csp > C0) else ((C0, N, xres),)
                        for (c0, c1, rr) in splits:
                            for k in range(8):
                                S.op("pe", lambda e, k=k, bk=bk, half=half, pi=pi, wuv=wuv, c0=c0, c1=c1: e.matmul(
                                    out=psb[bk][:, c0:c1], lhsT=wuv[:, k, half * 256 + pi * 128: half * 256 + (pi + 1) * 128],
                                    rhs=xnT[:, k, c0:c1], start=(k == 0), stop=(k == 7)),
                                    reads=[wur] + rr, writes=[f"ps{bk}"])
                        S.op("act", lambda e, bk=bk, c=c, ubuf=ubuf: e.activation(out=ubuf[:, C0:N], in_=psb[bk][:, C0:N], func=ACTF.Identity,
                                                                                 bias=pvc(l, OFF_CB + c, 1), scale=pvc(l, OFF_CW[2] + c, 1)),
                             reads=[f"ps{bk}", "pv"], writes=[ures])
                        for (a, bb, hid) in segs:
                            S.op("act", lambda e, bk=bk, c=c, bb=bb, hid=hid: e.activation(out=hist[(hid, l)][:, c, :], in_=psb[bk][:, bb - 2:bb],
                                                                                         func=ACTF.Copy),
                                 reads=[f"ps{bk}"], writes=[f"hist{hid}{l}"])
                        for (a, bb, hid) in segs:
                            S.op("dve", lambda e, bk=bk, c=c, ubuf=ubuf, a=a, bb=bb: e.scalar_tensor_tensor(
                                out=ubuf[:, a + 1:bb], in0=psb[bk][:, a:bb - 1], scalar=pvc(l, OFF_CW[1] + c, 1), in1=ubuf[:, a + 1:bb],
                                op0=ALU.mult, op1=ALU.add), reads=[f"ps{bk}", ures, "pv"], writes=[ures])
                            S.op("dve", lambda e, bk=bk, c=c, ubuf=ubuf, a=a, bb=bb: e.scalar_tensor_tensor(
                                out=ubuf[:, a + 2:bb], in0=psb[bk][:, a:bb - 2], scalar=pvc(l, OFF_CW[0] + c, 1), in1=ubuf[:, a + 2:bb],
                                op0=ALU.mult, op1=ALU.add), reads=[f"ps{bk}", ures, "pv"], writes=[ures])
                            S.op("dve", lambda e, c=c, ubuf=ubuf, a=a, hid=hid: e.tensor_tensor(
                                out=ubuf[:, a:a + 2], in0=ubuf[:, a:a + 2], in1=corr[hid][:, c, :], op=ALU.add),
                                reads=[ures, f"corr{hid}"], writes=[ures])
                    S.op("act", lambda e, ub=ub: e.activation(out=ug[ub][:, C0:N], in_=ug[ub][:, C0:N], func=ACTF.Gelu),
                         reads=[f"ug{ub}"], writes=[f"ug{ub}"])
                    S.op("dve" if (grp is groups[0] or j >= 20) else "pool", lambda e, ub=ub, j=j: e.tensor_tensor(out=actT[:, j, C0:N], in0=ug[ub][:, C0:N], in1=uv[ub][:, C0:N], op=ALU.mult),
                         reads=[f"ug{ub}", f"uv{ub}"], writes=[f"actT{j}"])
                W.done(U_UP + f)
            for t in grp["tiles"]:
                if t["kind"] == "s":
                    S.dma("sp", lambda e: e.dma_start(out=cs_o[l], in_=hist[("s", l)][:]), reads=[f"hists{l}"],
                          stream="misc_out", is_output=True)
                elif t["last"]:
                    S.dma("sp", lambda e: e.dma_start(out=co_o[l], in_=hist[("p", l)][:]), reads=[f"histp{l}"],
                          stream="misc_out", is_output=True)

        def stage_H(l, grp, W):
            halves = ((0, 12), (12, 22))
            if l == 1:
                gi_ = groups.index(grp)
                if gi_ + 1 < len(groups):
                    for ti2, t2 in enumerate(groups[gi_ + 1]["tiles"]):
                        if t2["T"] in xloaded:
                            norm_pre(0, 0, t2, ti2)
            for hi, (k0, k1) in enumerate(halves):
                for ti, t in enumerate(grp["tiles"]):
                    n = t["n"]
                    cs = slice(t["col"], t["col"] + n)
                    if (l == 1 and t["halo"]) or t["col"] < cstart(l, grp):
                        if hi == 1 and l == 1:
                            load_x_tile(t["T"] + NXS)
                        continue
                    for cb in range(2):
                        bk = pbank()
                        for kc in range(k0, k1):
                            wd, wdr = W.get(U_DN + kc // 4)
                            wdv = wd[:].rearrange("p (k c) -> p k c", k=4)
                            S.op("pe", lambda e, kc=kc, bk=bk, cs=cs, n=n, wdv=wdv, cb=cb, k0=k0, k1=k1: e.matmul(
                                out=psb[bk][0:n, 0:512], lhsT=actT[:, kc, cs], rhs=wdv[:, kc % 4, cb * 512:(cb + 1) * 512],
                                start=(kc == k0), stop=(kc == k1 - 1)),
                                reads=[wdr, f"actT{kc}"], writes=[f"ps{bk}"])
                        resid_add(grp, t, bk, cb)
                    if hi == 1 and l == 0:
                        norm_pre(1, 0, t, ti)
                        if ti > 0 and (grp["tiles"][ti - 1]["T"], 1, 0) in pre_done:
                            norm_post(1, 0, grp["tiles"][ti - 1], ti - 1)
                    if hi == 1 and l == 1:
                        xt, xr = xs_of(t)
                        if t["kind"] == "s":
                            S.dma("sp", lambda e, xt=xt: e.dma_start(out=ys_o, in_=xt[0:TS, :]), reads=[xr], stream=xr, is_output=True)
                        elif not t["halo"]:
                            S.dma("sp", lambda e, xt=xt, t=t: e.dma_start(out=yp_o[t["orow"]:t["orow"] + 128, :], in_=xt[:, :]),
                                  reads=[xr], stream=xr, is_output=True)
                        load_x_tile(t["T"] + NXS)
                for u in range(3):
                    W.done(U_DN + hi * 3 + u)
                if hi == 0 and l == 1:
                    gi_ = groups.index(grp)
                    if gi_ + 1 < len(groups):
                        for ti2, t2 in enumerate(groups[gi_ + 1]["tiles"]):
                            if (t2["T"], 0, 0) in pre_done:
                                norm_post(0, 0, t2, ti2)

        def carry(l, grp):
            npr = grp["np"]
            w = npr - 1
            S.op("pool", lambda e: e.tensor_copy(out=kcar[l][:], in_=kTw[:, :, w * 128:(w + 1) * 128]),
                 reads=["kTw0", "kTw1"], writes=[f"kcar{l}"])
            S.op("pool", lambda e: e.tensor_copy(out=Vcar[l][:], in_=Vw[:, w]), reads=[f"Vw{w}"], writes=[f"Vcar{l}"])
            S.op("pool", lambda e: e.tensor_copy(out=pincar[l][:], in_=pinw[:, w, :]), reads=[f"pinw{w}"], writes=[f"pincar{l}"])

        for Tn in range(NXS):
            load_x_tile(Tn)
        late_setup()
        for gi, grp in enumerate(groups):
            S.epoch = gi
            for l in range(2):
                W = WU()
                if DEBUG_STOP == "setup":
                    break
                for ti, t in enumerate(grp["tiles"]):
                    if (t["T"], l, 0) not in pre_done:
                        norm_pre(l, 0, t, ti)
                for ti, t in enumerate(grp["tiles"]):
                    if (t["T"], l, 0) not in post_done:
                        if ti == len(grp["tiles"]) - 1 and ti > 0:
                            pending_post.append(lambda l=l, t=t, ti=ti: norm_post(l, 0, t, ti))
                        else:
                            norm_post(l, 0, t, ti)
                if DEBUG_STOP == "A":
                    break
                stage_B(l, grp, W)
                if DEBUG_STOP in ("B", "B1", "B2", "B3", "B4", "B5"):
                    break
                pool_proj(l, grp, W)
                stage_D(l, grp, W)
                if DEBUG_STOP == "D":
                    break
                stage_E(l, grp, W)
                if DEBUG_STOP == "E":
                    break
                stage_F(l, grp, W)
                if DEBUG_STOP == "F":
                    break
                stage_G(l, grp, W)
                if DEBUG_STOP == "G":
                    break
                stage_H(l, grp, W)
                carry(l, grp)
                ucur["i"] += UNITS_PER_LAYER
                if DEBUG_ONE_LAYER and gi == len(groups) - 1:
                    break
            if gi == 0:
                for t in range(4):
                    S.op("pool", lambda e, t=t: e.memset(Vw[:, t, :, 64:65], 1.0), writes=[f"Vw{t}"])
        S.emit()
        build_nc.stats = S.stats
    return nc


def _fm(a, k):
    C = a.shape[1]
    return np.ascontiguousarray(a.reshape(k, 128, C).transpose(1, 0, 2)).reshape(128, k * C)


def _pack_units(w_in, w_pool, w_br_attn, w_br_pool, w_out, w_up, w_down):
    out = np.zeros((2 * UNITS_PER_LAYER, 128, UEL), np.float32)
    for l in range(2):
        b = l * UNITS_PER_LAYER
        wi = w_in[l]
        out[b + U_Q] = _fm(wi[:, 0:512], 8)
        k0, k1, v = wi[:, 512:576], wi[:, 576:640], wi[:, 640:768]
        out[b + U_KV, :, 0:3072] = _fm(np.concatenate([k0, k0, k1, k1, v], axis=1), 8)
        out[b + U_PIN] = _fm(wi[:, 768:1280], 8)
        out[b + U_POOL, :, 0:512] = np.ascontiguousarray(w_pool[l].transpose(1, 0, 2)).reshape(128, 512)
        for mp in range(4):
            g0 = wi[:, 1280 + mp * 256: 1280 + (mp + 1) * 256]
            g1 = wi[:, 2304 + mp * 256: 2304 + (mp + 1) * 256]
            out[b + U_EG(mp)] = _fm(np.concatenate([g0, g1], axis=1), 8)
            ba = w_br_attn[l][:, mp * 256:(mp + 1) * 256]
            bp = w_br_pool[l][:, mp * 256:(mp + 1) * 256]
            out[b + U_EB(mp), :, 0:2048] = _fm(np.concatenate([ba, bp], axis=1), 4)
        out[b + U_OUT] = _fm(w_out[l][:, 0:512], 8)
        out[b + U_OUT + 1] = _fm(w_out[l][:, 512:1024], 8)
        for f in range(11):
            ga = w_up[l][:, 2 * f * 128:(2 * f + 2) * 128]
            va = w_up[l][:, DFF + 2 * f * 128: DFF + (2 * f + 2) * 128]
            out[b + U_UP + f] = _fm(np.concatenate([ga, va], axis=1), 8)
        for i in range(6):
            rows = w_down[l][i * 512:(i + 1) * 512]
            kk = rows.shape[0] // 128
            out[b + U_DN + i, :, 0:kk * 1024] = _fm(rows, kk)
    return out


def _pack_vec(norm_mix, norm_ffn, gate_bias, pool_scale, conv_w, conv_b, q_norm, k_norm, sinks):
    pvt = np.zeros((128, 2 * PVL), np.float32)
    for l in range(2):
        b = l * PVL
        pvt[:, b + 0:b + 8] = norm_mix[l].reshape(8, 128).T
        pvt[:, b + 8:b + 16] = norm_ffn[l].reshape(8, 128).T
        pvt[:, b + 16:b + 24] = gate_bias[l, 0].reshape(8, 128).T
        pvt[:, b + 24:b + 32] = gate_bias[l, 1].reshape(8, 128).T
        pvt[:, b + 32:b + 36] = pool_scale[l].reshape(4, 128).T
        for j in range(3):
            pvt[:, b + 36 + 44 * j: b + 36 + 44 * (j + 1)] = conv_w[l, j].reshape(NCH, 128).T
        pvt[:, b + 168:b + 212] = conv_b[l].reshape(NCH, 128).T
        pvt[:, b + 212] = np.concatenate([q_norm[l], q_norm[l]])
        pvt[:, b + 213] = np.concatenate([k_norm[l], k_norm[l]])
        pvt[:, b + 214:b + 222] = np.broadcast_to(sinks[l][None, :], (128, 8))
    return pvt


def _const_tables():
    slopes = np.array([2.0 ** (-(h + 1)) for h in range(8)], np.float32)
    s = np.arange(128)[:, None]
    q = np.arange(128)[None, :]
    NEG = np.float32(-1e30)
    distA = (q + 128 - s).astype(np.float32)
    maskA = (q >= 64) & (s < 64)
    distB = np.abs(q - s).astype(np.float32)
    maskB = (q < 64) & (s >= 64)
    biasA = np.zeros((128, 8, 128), np.float32)
    biasB = np.zeros((128, 8, 128), np.float32)
    for h in range(8):
        biasA[:, h, :] = np.where(maskA, NEG, -8.0 * slopes[h] * distA)
        biasB[:, h, :] = np.where(maskB, NEG, -8.0 * slopes[h] * distB)
    wins = (2, 4, 8, 16)
    tcur = np.zeros((128, 4, 128), np.float32)
    tprev = np.zeros((128, 4, 128), np.float32)
    tfirst = np.zeros((128, 4, 128), np.float32)
    t = np.arange(128)[None, :]
    for g, w in enumerate(wins):
        inwin = (s <= t) & (s > t - w)
        tcur[:, g, :] = np.where(inwin, 1.0 / w, 0.0) - (s == t)
        tprev[:, g, :] = np.where(s > 128 + t - w, 1.0 / w, 0.0)
        cnt = np.minimum(t + 1, w).astype(np.float32)
        tfirst[:, g, :] = np.where(inwin, 1.0 / cnt, 0.0) - (s == t)
    bones = np.zeros((128, 128), np.float32)
    bones[0:64, 0:64] = 1.0 / 64
    bones[64:128, 64:128] = 1.0 / 64
    ident = np.eye(128, dtype=np.float32)
    return biasA, biasB, tcur, tprev, tfirst, bones, ident


_NC_CACHE = {}


def kernel(x_prompt, x_sample, cache_k, cache_v, state_pool, state_conv,
           norm_mix, w_in, q_norm, k_norm, sinks, w_pool, pool_scale,
           w_br_attn, w_br_pool, gate_bias, w_out, norm_ffn, w_up, conv_w, conv_b, w_down):
    f = lambda a: np.asarray(a, dtype=np.float32)
    x_prompt, x_sample, cache_k, cache_v, state_pool, state_conv = map(f, (x_prompt, x_sample, cache_k, cache_v, state_pool, state_conv))
    wun = _pack_units(f(w_in), f(w_pool), f(w_br_attn), f(w_br_pool), f(w_out), f(w_up), f(w_down))
    pvt = _pack_vec(f(norm_mix), f(norm_ffn), f(gate_bias), f(pool_scale), f(conv_w), f(conv_b), f(q_norm), f(k_norm), f(sinks))
    biasA, biasB, tcur, tprev, tfirst, bones, ident = _const_tables()
    B, SEQ = x_prompt.shape[0], x_prompt.shape[1]
    per = SEQ // 4
    in_maps = []
    for c in range(NCORES):
        b, qt = c // 4, c % 4
        start = qt * per
        xp = np.zeros((NPT * 128, D), np.float32)
        if qt > 0:
            xp[0:384] = x_prompt[b, start - 384:start]
        xp[384:] = x_prompt[b, start:start + per]
        ckT = np.zeros((2, 128, 4, 128), np.float32)
        for l in range(2):
            kt = cache_k[l, c].transpose(1, 2, 0)
            for g in range(2):
                ckT[l, 0:64, 2 * g] = kt[g]
                ckT[l, 64:128, 2 * g + 1] = kt[g]
        sph = np.zeros((2, 128, 512), np.float32)
        sph[:, 113:128] = state_pool[:, c]
        scT = np.ascontiguousarray(state_conv[:, c].reshape(2, 2, NCH, 128).transpose(0, 3, 2, 1))
        in_maps.append({
            "xp": xp, "xsm": np.ascontiguousarray(x_sample[c]), "wun": wun, "pvec": pvt,
            "biasA": biasA, "biasB": biasB, "tcur": tcur, "tprev": tprev,
            "tfirst": tfirst if qt == 0 else tcur, "bones": bones, "ident": ident,
            "hv": np.full((128, 1), 0.0 if qt == 0 else 1.0, np.float32),
            "ckT": ckT, "cktm": np.ascontiguousarray(cache_k[:, c].reshape(2, 128, 128)), "cv": np.ascontiguousarray(cache_v[:, c]), "sph": sph, "scT": scT,
        })
    if "nc" not in _NC_CACHE:
        _NC_CACHE["nc"] = build_nc()
    nc = _NC_CACHE["nc"]
    res = run_bass_kernel_spmd(nc, in_maps, core_ids=list(range(NCORES)))
    R = res.results
    y_prompt = np.zeros((B, SEQ, D), np.float32)
    for c in range(NCORES):
        b, qt = c // 4, c % 4
        y_prompt[b, qt * per:(qt + 1) * per] = R[c]["yp"]
    y_sample = np.stack([R[c]["ys"] for c in range(NCORES)], 0)

    def kfix(a):
        return np.ascontiguousarray(a.transpose(0, 3, 2, 1))

    def cfix(a):
        return np.ascontiguousarray(a.transpose(0, 3, 2, 1)).reshape(2, 2, NCH * 128)

    lastc = [3, 7]
    k_prompt = np.stack([kfix(R[c]["kTo"]) for c in lastc], 1)
    v_prompt = np.stack([R[c]["vo"].reshape(2, 128, 2, 64) for c in lastc], 1)
    pool_prompt = np.stack([R[c]["po"][:, 113:128] for c in lastc], 1)
    conv_prompt = np.stack([cfix(R[c]["co"]) for c in lastc], 1)
    k_sample = np.stack([np.concatenate([R[c]["kcp"].reshape(2, 128 - TS, 2, 64), kfix(R[c]["kTs"])], axis=1) for c in range(NCORES)], 1)
    v_sample = np.stack([R[c]["vs"].reshape(2, 128, 2, 64) for c in range(NCORES)], 1)
    pool_sample = np.stack([R[c]["pso"][:, 1:TS] for c in range(NCORES)], 1)
    conv_sample = np.stack([cfix(R[c]["cso"]) for c in range(NCORES)], 1)
    return (y_prompt, y_sample, k_prompt.astype(np.float32), v_prompt.astype(np.float32), pool_prompt, conv_prompt,
            k_sample.astype(np.float32), v_sample, pool_sample, conv_sample)
```

```python
from contextlib import ExitStack
import numpy as np
import concourse.bass as bass
import concourse.mybir as mybir
from concourse.bass_utils import run_bass_kernel_spmd

F32 = mybir.dt.float32
BF16 = mybir.dt.bfloat16
ACTF = mybir.ActivationFunctionType
ALU = mybir.AluOpType

NCORES = 8
D = 1024
OWN_TILES = 32
HALO_TILES = 3
NPT = OWN_TILES + HALO_TILES
TS = 16
NSLOT = 6
UEL = 4096
NXS = 7
EPS = 1e-6
DFF = 2816
NCH = 44
UNITS_PER_LAYER = 31
DEBUG_ONE_LAYER = False
DEBUG_STOP = None
PVL = 222


class _Op:
    __slots__ = ("eng", "fn", "reads", "writes", "kind", "stream", "is_output", "eidx", "sidx",
                 "waits", "signal", "sig", "K", "epoch", "gidx")


class Sched:
    ENGS = ["pe", "act", "dve", "pool", "sp"]

    def __init__(self, nc, es, nsem=4):
        self.nc = nc
        self.es = es
        self.ops = []
        self.epoch = 0
        self.nsem = nsem
        self.eng_sems = {}
        for e in self.ENGS:
            self.eng_sems[e] = [es.enter_context(nc.semaphore(f"sem_{e}{i}")) for i in range(nsem)]
        self.stream_sem = {}
        self.stream_cnt = {}
        self.out_streams = set()
        self.touch = {}

    def op(self, eng, fn, reads=(), writes=()):
        o = _Op()
        o.eng = eng; o.fn = fn; o.reads = tuple(reads); o.writes = tuple(writes)
        o.kind = "op"; o.stream = None; o.is_output = False
        o.waits = []; o.signal = False; o.sig = None; o.K = None; o.epoch = self.epoch
        o.gidx = len(self.ops)
        self.ops.append(o)
        for r in o.reads + o.writes:
            if r.startswith("ps"):
                self.touch[r] = o.gidx
        return o

    def dma(self, eng, fn, reads=(), writes=(), stream=None, is_output=False):
        o = self.op(eng, fn, reads, writes)
        o.kind = "dma"
        o.stream = stream
        o.is_output = is_output
        if stream not in self.stream_sem:
            self.stream_sem[stream] = self.es.enter_context(self.nc.semaphore(f"dsem_{stream}"))
            self.stream_cnt[stream] = 0
        o.sidx = self.stream_cnt[stream]
        self.stream_cnt[stream] += 1
        if is_output:
            self.out_streams.add(stream)
        return o

    def _analyze(self):
        last_writer = {}
        readers = {}
        stream_last = {}
        eng_ops = {e: [] for e in self.ENGS}
        eng_K = {e: {} for e in self.ENGS}
        for op in self.ops:
            deps = {}
            for r in op.reads:
                w = last_writer.get(r)
                if w is not None:
                    deps[w] = "raw"
                if r.startswith("ps"):
                    for rd in readers.get(r, ()):
                        if rd.eng != op.eng and rd not in deps:
                            deps[rd] = "rar"
            for wr in op.writes:
                w = last_writer.get(wr)
                if w is not None and w not in deps:
                    deps[w] = "waw"
                for rd in readers.get(wr, ()):
                    if rd is not op and rd not in deps:
                        deps[rd] = "war"
            if op.kind == "dma":
                prev = stream_last.get(op.stream)
                if prev is not None:
                    deps[prev] = "raw"
                stream_last[op.stream] = op
            for r in op.reads:
                readers.setdefault(r, []).append(op)
            for wr in op.writes:
                last_writer[wr] = op
                readers[wr] = []
            op.eidx = len(eng_ops[op.eng])
            eng_ops[op.eng].append(op)
            K = dict(eng_K[op.eng])
            need = {}
            for Dp, t in deps.items():
                if Dp.kind == "op":
                    if Dp.eng == op.eng and op.kind == "op" and op.eng == "pe":
                        continue
                    key = Dp.eng
                    val = Dp.eidx
                else:
                    key = ("s", Dp.stream)
                    val = Dp.sidx
                if K.get(key, -1) >= val:
                    continue
                cur = need.get(key)
                if cur is None or cur[0] < val:
                    need[key] = (val, Dp)
            for key, (val, Dp) in sorted(need.items(), key=lambda kv: -kv[1][1].gidx):
                if K.get(key, -1) >= val:
                    continue
                op.waits.append(Dp)
                Dp.signal = True
                for k2, v2 in Dp.K.items():
                    if K.get(k2, -1) < v2:
                        K[k2] = v2
                K[key] = val
            op.K = K
            eng_K[op.eng] = K
        cnt = {}
        for e in self.ENGS:
            for op in eng_ops[e]:
                if op.kind == "dma":
                    op.sig = (self.stream_sem[op.stream], 16 * (op.sidx + 1), 16)
                elif op.signal:
                    sem = self.eng_sems[e][op.epoch % self.nsem]
                    c = cnt.get(id(sem), 0) + 1
                    cnt[id(sem)] = c
                    op.sig = (sem, c, 1)
        return eng_ops

    def emit(self):
        eng_ops = self._analyze()
        nc = self.nc
        self.stats = {e: (len(eng_ops[e]), sum(len(o.waits) for o in eng_ops[e])) for e in self.ENGS}

        def run(eh, name):
            for op in eng_ops[name]:
                for Dp in op.waits:
                    eh.wait_ge(Dp.sig[0], Dp.sig[1])
                ins = op.fn(eh)
                if op.sig is not None:
                    ins.then_inc(op.sig[0], op.sig[2])
            if name == "sp":
                for s in sorted(self.out_streams):
                    eh.wait_ge(self.stream_sem[s], 16 * self.stream_cnt[s])

        with nc.Block() as block:
            @block.tensor
            def _(e):
                run(e, "pe")

            @block.scalar
            def _(e):
                run(e, "act")

            @block.vector
            def _(e):
                run(e, "dve")

            @block.gpsimd
            def _(e):
                run(e, "pool")

            @block.sync
            def _(e):
                run(e, "sp")


def unit_sizes():
    sz = [4096, 3072, 4096, 512]
    for _ in range(4):
        sz += [4096, 2048]
    sz += [4096, 4096]
    sz += [4096] * 11
    sz += [4096] * 5 + [2048]
    return sz


U_Q, U_KV, U_PIN, U_POOL = 0, 1, 2, 3
def U_EG(mp): return 4 + 2 * mp
def U_EB(mp): return 5 + 2 * mp
U_OUT = 12
U_UP = 14
U_DN = 25


def build_nc(n_groups_limit=None):
    nc = bass.Bass("TRN2", target_bir_lowering=False)

    def din(name, shape, dt=F32):
        return nc.dram_tensor(name, list(shape), dt, kind="ExternalInput").ap()

    def dout(name, shape, dt=F32):
        return nc.dram_tensor(name, list(shape), dt, kind="ExternalOutput").ap()

    xp = din("xp", [NPT * 128, D])
    xsm = din("xsm", [TS, D])
    wun = din("wun", [2 * UNITS_PER_LAYER, 128, UEL])
    pvec = din("pvec", [128, 2 * PVL])
    biasA_d = din("biasA", [128, 8, 128])
    biasB_d = din("biasB", [128, 8, 128])
    tcur_d = din("tcur", [128, 4, 128])
    tprev_d = din("tprev", [128, 4, 128])
    tfirst_d = din("tfirst", [128, 4, 128])
    bones_d = din("bones", [128, 128])
    ident_d = din("ident", [128, 128])
    hv_d = din("hv", [128, 1])
    ckT_d = din("ckT", [2, 128, 4, 128])
    cv_d = din("cv", [2, 128, 2, 64])
    sph_d = din("sph", [2, 128, 512])
    scT_d = din("scT", [2, 128, NCH, 2])
    cktm_d = din("cktm", [2, 128, 128])

    yp_o = dout("yp", [OWN_TILES * 128, D])
    ys_o = dout("ys", [TS, D])
    kTo_o = dout("kTo", [2, 64, 2, 128])
    vo_o = dout("vo", [2, 128, 128])
    po_o = dout("po", [2, 128, 512])
    co_o = dout("co", [2, 128, NCH, 2])
    kTs_o = dout("kTs", [2, 64, 2, TS])
    kcp_o = dout("kcp", [2, 128 - TS, 128])
    vs_o = dout("vs", [2, 128, 128])
    ps_o = dout("pso", [2, TS, 512])
    cs_o = dout("cso", [2, 128, NCH, 2])

    wsc = nc.dram_tensor("wsc", [2 * UNITS_PER_LAYER, 128, UEL], BF16, kind="Internal").ap()
    USZ = unit_sizes()

    es = ExitStack()
    with es:
        def sb(name, shape, dt):
            return es.enter_context(nc.sbuf_tensor(name, list(shape), dt))

        S = Sched(nc, es)

        xsl = [sb(f"xsl{i}", [128, D], F32) for i in range(NXS)]
        xh = [sb(f"xh{i}", [128, D], BF16) for i in range(4)]
        ssq = sb("ssq", [128, 4], F32)
        srs = sb("srs", [128, 4], F32)
        rstd = sb("rstd", [128, 4], F32)
        xnT = sb("xnT", [128, 8, 512], BF16)
        qT = sb("qT", [128, 4, 512], BF16)
        kTw = sb("kTw", [128, 4, 512], BF16)
        kcar = [sb(f"kcar{l}", [128, 4, 128], BF16) for l in range(2)]
        ksam = sb("ksam", [128, 4, TS], BF16)
        kcache = [sb(f"kcache{l}", [128, 4, 128], BF16) for l in range(2)]
        sq = [sb(f"sq{i}", [128, 512], BF16) for i in range(2)]
        srt = [sb(f"srt{i}", [128, 512], F32) for i in range(2)]
        Vw = sb("Vw", [128, 4, 2, 66], BF16)
        Vcar = [sb(f"Vcar{l}", [128, 2, 66], BF16) for l in range(2)]
        Vsam = sb("Vsam", [128, 2, 66], BF16)
        Vcache = [sb(f"Vcache{l}", [128, 2, 66], BF16) for l in range(2)]
        pinw = sb("pinw", [128, 4, 512], BF16)
        pincar = [sb(f"pincar{l}", [128, 512], BF16) for l in range(2)]
        pinsam = sb("pinsam", [128, 512], BF16)
        pinhist = [sb(f"pinhist{l}", [128, 512], BF16) for l in range(2)]
        dT = sb("dT", [128, 4, 512], BF16)
        plT = sb("plT", [128, 4, 512], BF16)
        PT = [sb(f"PT{i}", [128, 2, 2, 4, 128], BF16) for i in range(2)]
        atok = [sb(f"atok{i}", [128, 512], BF16) for i in range(2)]
        den = sb("den", [128, 8], F32)
        rden = sb("rden", [128, 8], F32)
        aT = sb("aT", [128, 4, 512], BF16)
        s0b = sb("s0b", [128, 512], F32)
        s1b = sb("s1b", [128, 512], F32)
        mixT = sb("mixT", [128, 8, 512], BF16)
        ug = [sb(f"ug{i}", [128, 512], F32) for i in range(2)]
        uv = [sb(f"uv{i}", [128, 512], F32) for i in range(2)]
        actT = sb("actT", [128, 22, 512], BF16)
        hist = {(h, l): sb(f"hist{h}{l}", [128, NCH, 2], F32) for h in "ps" for l in range(2)}
        corr = {h: sb(f"corr{h}", [128, NCH, 2], F32) for h in "ps"}
        ctmp = sb("ctmp", [128, NCH], F32)
        pv = sb("pv", [128, 2 * PVL], F32)
        esink = sb("esink", [128, 16], F32)
        epst = sb("epst", [128, 1], F32)
        hvt = sb("hvt", [128, 1], F32)
        biasA = sb("biasA_s", [128, 8, 128], BF16)
        biasB = sb("biasB_s", [128, 8, 128], BF16)
        tcur = sb("tcur_s", [128, 4, 128], BF16)
        tprev = sb("tprev_s", [128, 4, 128], BF16)
        tfirst = sb("tfirst_s", [128, 4, 128], BF16)
        bones = sb("bones_s", [128, 128], BF16)
        ident = sb("ident_s", [128, 128], BF16)
        wslot = [sb(f"wslot{i}", [128, UEL], BF16) for i in range(NSLOT)]
        kn32 = {(h, l, g): sb(f"kn32{h}{l}{g}", [128, 128 if h == "p" else TS], F32)
                for h in "ps" for l in range(2) for g in range(2)}
        v32 = {(h, l): sb(f"v32{h}{l}", [128, 128], F32) for h in "ps" for l in range(2)}
        _pin32 = [sb(f"pin32_{l}", [128, 512], F32) for l in range(2)]
        pin32 = {(h, l): _pin32[l] for h in "ps" for l in range(2)}

        psb = [es.enter_context(nc.psum_tensor(f"psb{i}", [128, 512], F32)) for i in range(8)]
        pstate = {"i": 0}

        def pbank():
            i = min(range(8), key=lambda b: S.touch.get(f"ps{b}", -1 - (8 - b)))
            S.touch[f"ps{i}"] = len(S.ops)
            pstate["i"] = (pstate["i"] + 1) % 8
            return i

        def pvc(l, off, n=1):
            return pv[:, l * PVL + off: l * PVL + off + n]
        OFF_G = [0, 8]; OFF_GB = [16, 24]; OFF_PS = 32
        OFF_CW = [36, 80, 124]; OFF_CB = 168; OFF_GQ = 212; OFF_GK = 213; OFF_SK = 214

        S.dma("sp", lambda e: e.dma_start(out=pv[:], in_=pvec), writes=["pv"], stream="setup0")
        S.dma("sp", lambda e: e.dma_start(out=hvt[:], in_=hv_d), writes=["hvt"], stream="setup1")
        def const_load(nm, dst, src):
            S.dma("pool", lambda e, dst=dst, src=src: e.dma_start(out=dst[:], in_=src), writes=[nm], stream="c_" + nm[:5] + nm[-1])
        for (nm, dst, src) in (("ident", ident, ident_d), ("bones", bones, bones_d)):
            const_load(nm, dst, src)

        def late_setup():
          for (nm, dst, src) in (("tcur", tcur, tcur_d), ("tprev", tprev, tprev_d), ("tfirst", tfirst, tfirst_d),
                                 ("biasA", biasA, biasA_d), ("biasB", biasB, biasB_d)):
              const_load(nm, dst, src)
          for l in range(2):
            S.dma("pool", lambda e, l=l: e.dma_start(out=kcache[l][:], in_=ckT_d[l]), writes=[f"kcache{l}"], stream="c_kc")
            S.dma("pool", lambda e, l=l: e.dma_start(out=Vcache[l][:, :, 0:64], in_=cv_d[l]), writes=[f"Vcache{l}"], stream="c_vc")
            S.dma("pool", lambda e, l=l: e.dma_start(out=pinhist[l][:], in_=sph_d[l]), writes=[f"pinhist{l}"], stream="c_ph")
            S.dma("sp", lambda e, l=l: e.dma_start(out=hist[("s", l)][:], in_=scT_d[l]), writes=[f"hists{l}"], stream="setup2")
            S.op("pool", lambda e, l=l: e.memset(Vcache[l][:, :, 64:66], 1.0), writes=[f"Vcache{l}"])
            S.op("pool", lambda e, l=l: e.memset(kcar[l][:], 0.0), writes=[f"kcar{l}"])
            S.op("pool", lambda e, l=l: e.memset(Vcar[l][:], 0.0), writes=[f"Vcar{l}"])
            S.op("pool", lambda e, l=l: e.memset(pincar[l][:], 0.0), writes=[f"pincar{l}"])
            S.op("pool", lambda e, l=l: e.memset(hist[("p", l)][:], 0.0), writes=[f"histp{l}"])
            S.dma("sp", lambda e, l=l: e.dma_start(out=vs_o[l, 0:128 - TS, :],
                                                   in_=cv_d[l, TS:128].rearrange("s g d -> s (g d)")),
                  stream="misc_out", is_output=True)
            S.dma("sp", lambda e, l=l: e.dma_start(out=kcp_o[l], in_=cktm_d[l, TS:128, :]),
                  stream="misc_out", is_output=True)
        S.op("pool", lambda e: e.memset(epst[:], EPS), writes=["epst"])
        S.op("pool", lambda e: e.memset(kTw[:], 0.0), writes=["kTw0", "kTw1"])
        S.op("pool", lambda e: e.memset(ksam[:], 0.0), writes=["ksam0", "ksam1"])
        S.op("pool", lambda e: e.memset(Vsam[:], 1.0), writes=["Vsam"])
        S.op("pool", lambda e: e.memset(Vw[:], 0.0), writes=[f"Vw{t}" for t in range(4)])
        for t in range(4):
            S.op("dve", lambda e, t=t: e.tensor_copy(out=Vw[:, t, :, 64:65], in_=hvt[:, 0:1].unsqueeze(1).to_broadcast([128, 2, 1])),
                 reads=["hvt"], writes=[f"Vw{t}"])
        for l in range(2):
            S.op("act", lambda e, l=l: e.activation(out=esink[:, l * 8:(l + 1) * 8], in_=pvc(l, OFF_SK, 8), func=ACTF.Exp),
                 reads=["pv"], writes=["esink"])

        n_groups = 1 + OWN_TILES // 4
        if n_groups_limit is not None:
            n_groups = n_groups_limit
        useq = [(g, l, u) for g in range(n_groups) for l in range(2) for u in range(UNITS_PER_LAYER)]
        wst = {"next": 0}

        def issue_load():
            i = wst["next"]
            if i >= len(useq):
                return
            wst["next"] = i + 1
            g, l, u = useq[i]
            slot = i % NSLOT
            gu = l * UNITS_PER_LAYER + u
            n = USZ[u]
            if g == 0:
                S.dma("pool", lambda e: e.dma_start(out=wslot[slot][:, 0:n], in_=wun[gu, :, 0:n]),
                      writes=[f"ws{slot}"], stream=f"wp{slot}")
                S.dma("sp", lambda e: e.dma_start(out=wsc[gu, :, 0:n], in_=wslot[slot][:, 0:n]),
                      reads=[f"ws{slot}"], writes=[f"wsc{gu}"], stream=f"ww{slot}")
            else:
                S.dma("sp", lambda e: e.dma_start(out=wslot[slot][:, 0:n], in_=wsc[gu, :, 0:n]),
                      reads=[f"wsc{gu}"], writes=[f"ws{slot}"], stream=f"ws{slot}")

        ucur = {"i": 0}

        class WU:
            def __init__(self):
                self.base = ucur["i"]

            def get(self, u):
                i = self.base + u
                slot = i % NSLOT
                return wslot[slot], f"ws{slot}"

            def done(self, u):
                issue_load()

        for _ in range(NSLOT):
            issue_load()

        groups = []
        g0 = {"tiles": [], "N": 3 * 128 + TS, "segs": [(0, 384, "p"), (384, 384 + TS, "s")], "np": 3}
        T = 0
        for i in range(3):
            g0["tiles"].append(dict(kind="p", n=128, col=i * 128, row=i * 128, halo=True, wslot=i, T=T, first=False, last=False))
            T += 1
        g0["tiles"].append(dict(kind="s", n=TS, col=384, row=0, halo=False, wslot=3, T=T, first=False, last=False))
        T += 1
        groups.append(g0)
        for gi in range(OWN_TILES // 4):
            gg = {"tiles": [], "N": 512, "segs": [(0, 512, "p")], "np": 4}
            for i in range(4):
                ot = gi * 4 + i
                gg["tiles"].append(dict(kind="p", n=128, col=i * 128, row=(3 + ot) * 128, halo=False, wslot=i, T=T,
                                        first=(ot == 0), last=(ot == OWN_TILES - 1), orow=ot * 128))
                T += 1
            groups.append(gg)
        groups = groups[:n_groups]
        if n_groups_limit is not None:
            groups[-1]["tiles"][-1]["last"] = True

        def xs_of(t):
            return xsl[t["T"] % NXS], f"x{t['T'] % NXS}"

        all_tiles = [t for g_ in groups for t in g_["tiles"]]

        xloaded = set()

        def load_x_tile(Tn):
            if Tn >= len(all_tiles):
                return
            xloaded.add(Tn)
            t = all_tiles[Tn]
            xt, xr = xs_of(t)
            if t["kind"] == "p":
                S.dma("sp", lambda e, xt=xt, t=t: e.dma_start(out=xt[:, :], in_=xp[t["row"]:t["row"] + 128, :]),
                      writes=[xr], stream=xr)
            else:
                S.dma("sp", lambda e, xt=xt: e.dma_start(out=xt[0:TS, :], in_=xsm), writes=[xr], stream=xr)

        pre_done = set()

        def norm_pre(l, ni, t, ti):
            pre_done.add((t["T"], l, ni))
            xt, xr = xs_of(t)
            n = t["n"]
            b = ti
            S.op("act", lambda e: e.activation(out=xh[b][0:n, :], in_=xt[0:n, :], func=ACTF.Square, scale=1.0 / 32.0,
                                               accum_out=ssq[0:n, ti:ti + 1]), reads=[xr], writes=[f"xh{b}", f"ssq{ti}"])
            S.op("act", lambda e: e.activation(out=srs[0:n, ti:ti + 1], in_=ssq[0:n, ti:ti + 1], func=ACTF.Ln,
                                               bias=epst[0:n, 0:1], scale=1.0), reads=[f"ssq{ti}", "epst"], writes=[f"srs{ti}"])
            S.op("act", lambda e: e.activation(out=rstd[0:n, ti:ti + 1], in_=srs[0:n, ti:ti + 1], func=ACTF.Exp, scale=-0.5),
                 reads=[f"srs{ti}"], writes=[f"rstd{ti}"])
            S.op("act", lambda e: e.activation(out=xh[b][0:n, :], in_=xt[0:n, :], func=ACTF.Identity, scale=rstd[0:n, ti:ti + 1]),
                 reads=[xr, f"rstd{ti}"], writes=[f"xh{b}"])

        post_done = set()
        pending_post = []

        def norm_post(l, ni, t, ti):
            post_done.add((t["T"], l, ni))
            n = t["n"]
            b = ti
            cs = slice(t["col"], t["col"] + n)
            bk = pbank()
            pT = psb[bk][:].bitcast(BF16).rearrange("p (k t) -> p k t", k=8)
            for k in range(8):
                S.op("pe", lambda e, k=k: e.transpose(out=pT[:, k, 0:n], in_=xh[b][0:n, k * 128:(k + 1) * 128], identity=ident[0:n, 0:n]),
                     reads=[f"xh{b}", "ident"], writes=[f"ps{bk}"])
            gv = pvc(l, OFF_G[ni], 8).unsqueeze(2).to_broadcast([128, 8, n])
            S.op("dve", lambda e: e.tensor_tensor(out=xnT[:, :, cs], in0=pT[:, :, 0:n], in1=gv, op=ALU.mult),
                 reads=[f"ps{bk}", "pv"], writes=[f"xnT{ti}"])

        def xnT_res(grp):
            return [f"xnT{ti}" for ti in range(len(grp["tiles"]))]

        qkst = {"n": 0, "pend": None}

        def qk_flush():
            if qkst["pend"] is not None:
                args = qkst["pend"]
                qkst["pend"] = None
                qk_post(*args)

        def qk_norm(l, bk, N, gain_off, dests, xres, extra32=None):
            b = qkst["n"] % 2
            qkst["n"] += 1
            S.op("act", lambda e: e.activation(out=sq[b][:, 0:N], in_=psb[bk][:, 0:N], func=ACTF.Square),
                 reads=[f"ps{bk}"], writes=[f"sq{b}"])
            qkst["pend"] = (l, bk, N, gain_off, dests, b)

        def qk_post(l, bk, N, gain_off, dests, b):
            bk2 = pbank()
            S.op("pe", lambda e: e.matmul(out=psb[bk2][:, 0:N], lhsT=bones[:, :], rhs=sq[b][:, 0:N], start=True, stop=True),
                 reads=[f"sq{b}", "bones"], writes=[f"ps{bk2}"])
            S.op("act", lambda e: e.activation(out=srt[b][:, 0:N], in_=psb[bk2][:, 0:N], func=ACTF.Ln, bias=epst[:, 0:1], scale=1.0),
                 reads=[f"ps{bk2}", "epst"], writes=[f"srt{b}"])
            S.op("act", lambda e: e.activation(out=srt[b][:, 0:N], in_=srt[b][:, 0:N], func=ACTF.Exp, scale=-0.5),
                 reads=[f"srt{b}"], writes=[f"srt{b}"])
            gain = pvc(l, gain_off, 1)
            for dd in dests:
                (a, bb, dst, rn) = dd[:4]
                p0, p1 = dd[4] if len(dd) > 4 else (0, 128)
                S.op("dve", lambda e, a=a, bb=bb, dst=dst, p0=p0, p1=p1: e.scalar_tensor_tensor(
                    out=dst, in0=psb[bk][p0:p1, a:bb], scalar=gain[p0:p1, :], in1=srt[b][p0:p1, a:bb], op0=ALU.mult, op1=ALU.mult),
                     reads=[f"ps{bk}", f"srt{b}", "pv"], writes=[rn])

        def stage_B(l, grp, W):
            N = grp["N"]
            tiles = grp["tiles"]
            xres = xnT_res(grp)
            wq, wqr = W.get(U_Q)
            wqv = wq[:].rearrange("p (k c) -> p k c", k=8)
            def q_job(j):
                bk = pbank()
                for k in range(8):
                    S.op("pe", lambda e, j=j, k=k, bk=bk: e.matmul(out=psb[bk][:, 0:N], lhsT=wqv[:, k, j * 128:(j + 1) * 128],
                                                                   rhs=xnT[:, k, 0:N], start=(k == 0), stop=(k == 7)),
                         reads=[wqr] + xres, writes=[f"ps{bk}"])
                qk_flush()
                qk_norm(l, bk, N, OFF_GQ, [(0, N, qT[:, j, 0:N], f"qT{j}")], xres)
                if j == 3:
                    W.done(U_Q)
            wkv, wkvr = W.get(U_KV)
            wkvv = wkv[:, 0:3072].rearrange("p (k c) -> p k c", k=8)
            npr = grp["np"]
            def k_job(g):
                bk = pbank()
                for (c0, c1, rr) in ((0, N, xres),):
                    for k in range(8):
                        S.op("pe", lambda e, g=g, k=k, bk=bk, c0=c0, c1=c1: e.matmul(out=psb[bk][:, c0:c1], lhsT=wkvv[:, k, g * 128:(g + 1) * 128],
                                                                       rhs=xnT[:, k, c0:c1], start=(k == 0), stop=(k == 7)),
                             reads=[wkvr] + rr, writes=[f"ps{bk}"])
                dests = [(0, npr * 128, kTw[0:64, 2 * g, 0:npr * 128], f"kTw{g}", (0, 64)),
                         (0, npr * 128, kTw[64:128, 2 * g + 1, 0:npr * 128], f"kTw{g}", (64, 128))]
                for t in tiles:
                    if t["kind"] == "s":
                        dests.append((t["col"], t["col"] + TS, ksam[0:64, 2 * g, :], f"ksam{g}", (0, 64)))
                        dests.append((t["col"], t["col"] + TS, ksam[64:128, 2 * g + 1, :], f"ksam{g}", (64, 128)))
                for t in tiles:
                    if t["kind"] == "s" or t["last"]:
                        h = t["kind"]
                        dests.append((t["col"], t["col"] + t["n"], kn32[(h, l, g)][:, :], f"kn32{h}{l}{g}"))
                qk_flush()
                qk_norm(l, bk, N, OFF_GK, dests, xres)

            def k_state_out():
                qk_flush()
                for g in range(2):
                    for t in tiles:
                        if t["kind"] == "s":
                            S.dma("sp", lambda e, g=g: e.dma_start(out=kTs_o[l, :, g, :], in_=kn32[("s", l, g)][0:64, :]),
                                  reads=[f"kn32s{l}{g}"], stream="misc_out", is_output=True)
                        elif t["last"]:
                            S.dma("sp", lambda e, g=g: e.dma_start(out=kTo_o[l, :, g, :], in_=kn32[("p", l, g)][0:64, :]),
                                  reads=[f"kn32p{l}{g}"], stream="misc_out", is_output=True)
            wpin, wpinr = W.get(U_PIN)
            wpinv = wpin[:].rearrange("p (k c) -> p k c", k=8)

            def tile_job(ti, t):
                n = t["n"]
                cs = slice(t["col"], t["col"] + n)
                bk = pbank()
                for k in range(8):
                    S.op("pe", lambda e, k=k, bk=bk, cs=cs, n=n: e.matmul(out=psb[bk][0:n, 0:128], lhsT=xnT[:, k, cs], rhs=wkvv[:, k, 256:384],
                                                                         start=(k == 0), stop=(k == 7)),
                         reads=[wkvr, f"xnT{ti}"], writes=[f"ps{bk}"])
                if t["kind"] == "p":
                    vdst = Vw[0:n, t["wslot"], :, 0:64]; vres = f"Vw{t['wslot']}"
                else:
                    vdst = Vsam[0:n, :, 0:64]; vres = "Vsam"
                src = psb[bk][0:n, 0:128].rearrange("p (g d) -> p g d", g=2)
                S.op("act", lambda e, vdst=vdst, src=src: e.activation(out=vdst, in_=src, func=ACTF.Copy),
                     reads=[f"ps{bk}"], writes=[vres])
                if DEBUG_STOP == "B3":
                    return
                if t["kind"] == "s" or t["last"]:
                    h = t["kind"]
                    S.op("act", lambda e, bk=bk, n=n, h=h: e.activation(out=v32[(h, l)][0:n, :], in_=psb[bk][0:n, 0:128], func=ACTF.Copy),
                         reads=[f"ps{bk}"], writes=[f"v32{h}{l}"])
                    if h == "s":
                        S.dma("sp", lambda e: e.dma_start(out=vs_o[l, 128 - TS:128, :], in_=v32[("s", l)][0:TS, :]),
                              reads=[f"v32s{l}"], stream="misc_out", is_output=True)
                    else:
                        S.dma("sp", lambda e: e.dma_start(out=vo_o[l], in_=v32[("p", l)][:, :]),
                              reads=[f"v32p{l}"], stream="misc_out", is_output=True)
                if DEBUG_STOP == "B4":
                    return
                bk = pbank()
                for k in range(8):
                    S.op("pe", lambda e, k=k, bk=bk, cs=cs, n=n: e.matmul(out=psb[bk][0:n, 0:512], lhsT=xnT[:, k, cs], rhs=wpinv[:, k, :],
                                                                         start=(k == 0), stop=(k == 7)),
                         reads=[wpinr, f"xnT{ti}"], writes=[f"ps{bk}"])
                if t["kind"] == "p":
                    pdst = pinw[0:n, t["wslot"], :]; pres = f"pinw{t['wslot']}"
                else:
                    pdst = pinsam[0:n, :]; pres = "pinsam"
                S.op("dve", lambda e, pdst=pdst, bk=bk, n=n: e.tensor_copy(out=pdst, in_=psb[bk][0:n, 0:512]),
                     reads=[f"ps{bk}"], writes=[pres])
                if DEBUG_STOP == "B5":
                    return
                if t["kind"] == "s" or t["last"]:
                    h = t["kind"]
                    S.op("act", lambda e, bk=bk, n=n, h=h: e.activation(out=pin32[(h, l)][0:n, :], in_=psb[bk][0:n, 0:512], func=ACTF.Copy),
                         reads=[f"ps{bk}"], writes=[f"pin32{l}"])
                    if h == "s":
                        S.dma("sp", lambda e: e.dma_start(out=ps_o[l], in_=pin32[("s", l)][0:TS, :]),
                              reads=[f"pin32{l}"], stream="misc_out", is_output=True)
                    else:
                        S.dma("sp", lambda e: e.dma_start(out=po_o[l], in_=pin32[("p", l)][:, :]),
                              reads=[f"pin32{l}"], stream="misc_out", is_output=True)
            _tile_job = tile_job

            def tile_job(ti, t):
                _tile_job(ti, t)
                qk_flush()
                if ti > 0:
                    pool_toep(l, grp, ti - 1)
            cjobs = [lambda g=g: k_job(g) for g in range(2)] + [lambda j=j: q_job(j) for j in range(4)]
            tjobs = [lambda ti=ti, t=t: tile_job(ti, t) for ti, t in enumerate(tiles)]
            order = []
            tj = 0
            if pending_post:
                n_early = max(len(tjobs) - 1, 0)
                flush_pending = lambda: [p() for p in [pending_post.pop(0) for _ in range(len(pending_post))]]
                if n_early >= 2:
                    order += tjobs[:n_early - 1]
                    order.append(flush_pending)
                    order.append(tjobs[n_early - 1])
                else:
                    order += tjobs[:n_early]
                    order.append(flush_pending)
                tj = n_early
            if pending_post or tj > 0:
                order += cjobs
                ci = len(cjobs)
            else:
                order += [cjobs[0], cjobs[1]]
                ci = 2
            while ci < len(cjobs) or tj < len(tjobs):
                if tj < len(tjobs):
                    order.append(tjobs[tj]); tj += 1
                if ci < len(cjobs):
                    order.append(cjobs[ci]); ci += 1
            for jb in order:
                jb()
            k_state_out()
            pool_toep(l, grp, len(tiles) - 1)
            W.done(U_KV)
            W.done(U_PIN)

        def cstart(l, grp):
            if grp is groups[0] and n_groups_limit != 1:
                return 128 if l == 0 else 256
            return 0

        def prev_of(l, t):
            if t["kind"] == "s":
                return (kcache[l], [f"kcache{l}"]), (Vcache[l], f"Vcache{l}"), (pinhist[l], f"pinhist{l}")
            w = t["wslot"]
            if w == 0:
                return (kcar[l], [f"kcar{l}"]), (Vcar[l], f"Vcar{l}"), (pincar[l], f"pincar{l}")
            return ((kTw[:, :, (w - 1) * 128: w * 128], ["kTw0", "kTw1"]), (Vw[:, w - 1], f"Vw{w - 1}"),
                    (pinw[:, w - 1, :], f"pinw{w - 1}"))

        def cur_of(l, t):
            if t["kind"] == "s":
                return (ksam, ["ksam0", "ksam1"]), (Vsam, "Vsam"), (pinsam, "pinsam")
            w = t["wslot"]
            return ((kTw[:, :, w * 128:(w + 1) * 128], ["kTw0", "kTw1"]), (Vw[:, w], f"Vw{w}"), (pinw[:, w, :], f"pinw{w}"))

        def ap3(x):
            return x if not hasattr(x, "ap") or True else x

        def pool_toep(l, grp, ti):
            if grp["tiles"][ti]["col"] >= cstart(l, grp):
                t = grp["tiles"][ti]
                n = t["n"]
                cs = slice(t["col"], t["col"] + n)
                (_, _), (_, _), (pp, ppr) = prev_of(l, t)
                (_, _), (_, _), (pc, pcr) = cur_of(l, t)
                tc_tab, tcr = (tfirst, "tfirst") if t["first"] else (tcur, "tcur")
                bk = pbank()
                pv4 = psb[bk][:].rearrange("p (g t) -> p g t", g=4)
                for g in range(4):
                    S.op("pe", lambda e, g=g, n=n, pp=pp, pv4=pv4: e.matmul(out=pv4[:, g, 0:n], lhsT=pp[:, g * 128:(g + 1) * 128], rhs=tprev[:, g, 0:n],
                                                                   start=True, stop=False),
                         reads=[ppr, "tprev"], writes=[f"ps{bk}"])
                    S.op("pe", lambda e, g=g, n=n, pc=pc, tc_tab=tc_tab, pv4=pv4: e.matmul(out=pv4[:, g, 0:n], lhsT=pc[0:n, g * 128:(g + 1) * 128],
                                                                                  rhs=tc_tab[0:n, g, 0:n], start=False, stop=True),
                         reads=[pcr, tcr], writes=[f"ps{bk}"])
                S.op("act", lambda e, n=n, cs=cs, pv4=pv4: e.activation(out=dT[:, :, cs], in_=pv4[:, :, 0:n], func=ACTF.Copy),
                     reads=[f"ps{bk}"], writes=[f"dT{ti}"])

        def pool_proj(l, grp, W):
            N = grp["N"]
            C0 = cstart(l, grp)
            wp, wpr = W.get(U_POOL)
            dres = [f"dT{ti}" for ti, t_ in enumerate(grp["tiles"]) if t_["col"] >= C0]
            for g in range(4):
                bk = pbank()
                S.op("pe", lambda e, g=g, bk=bk: e.matmul(out=psb[bk][:, C0:N], lhsT=wp[:, g * 128:(g + 1) * 128], rhs=dT[:, g, C0:N],
                                                          start=True, stop=True),
                     reads=[wpr] + dres, writes=[f"ps{bk}"])
                S.op("dve", lambda e, g=g, bk=bk: e.tensor_scalar(out=plT[:, g, C0:N], in0=psb[bk][:, C0:N], scalar1=pvc(l, OFF_PS + g, 1),
                                                                  scalar2=None, op0=ALU.mult),
                     reads=[f"ps{bk}", "pv"], writes=[f"plT{g}"])
            W.done(U_POOL)

        def attn_scores(l, t, pbi):
            n = t["n"]
            cs = slice(t["col"], t["col"] + n)
            (kp, kpr), _, _ = prev_of(l, t)
            (kc, kcr), _, _ = cur_of(l, t)
            nkp, nkc = 128, n
            for grp_ in range(2):
                for X, (kx, kxr, nk, btab, bres) in enumerate(((kp, kpr, nkp, biasA, "biasA"), (kc, kcr, nkc, biasB, "biasB"))):
                    bk = pbank()
                    Sv = psb[bk][0:nk, 0:4 * n].rearrange("p (h q) -> p h q", h=4)
                    S.op("pe", lambda e, Sv=Sv, nk=nk, btab=btab, grp_=grp_: e.matmul(
                        out=Sv, lhsT=ident[0:nk, 0:nk], rhs=btab[0:nk, grp_ * 4:(grp_ + 1) * 4, 0:n], start=True, stop=False),
                        reads=["ident", bres], writes=[f"ps{bk}"])
                    for hh in range(4):
                        h = grp_ * 4 + hh
                        hb = (h % 2) * 64
                        S.op("pe", lambda e, Sv=Sv, hh=hh, hb=hb, kx=kx, nk=nk, grp_=grp_, h=h: e.matmul(
                            out=Sv[:, hh, :], lhsT=kx[:, grp_ * 2 + (h % 2), 0:nk], rhs=qT[:, h // 2, cs], start=False, stop=(hh == 3)),
                            reads=kxr + [f"qT{h // 2}"], writes=[f"ps{bk}"])
                    S.op("act", lambda e, Sv=Sv, nk=nk, X=X, grp_=grp_: e.activation(out=PT[pbi][0:nk, X, grp_, :, 0:n], in_=Sv, func=ACTF.Exp, scale=0.125),
                         reads=[f"ps{bk}"], writes=[f"PT{pbi}_{X}{grp_}"])

        def attn_pv(l, t, ti, pbi):
            n = t["n"]
            cs = slice(t["col"], t["col"] + n)
            _, (vp, vpr), _ = prev_of(l, t)
            _, (vc, vcr), _ = cur_of(l, t)
            nkp, nkc = 128, n
            ab = t["T"] % 2
            for grp_ in range(2):
                bk = pbank()
                O = psb[bk][0:n, 0:260].rearrange("p (h e) -> p h e", h=4)
                for hh in range(4):
                    S.op("pe", lambda e, O=O, hh=hh, grp_=grp_, vp=vp: e.matmul(out=O[:, hh, :], lhsT=PT[pbi][0:nkp, 0, grp_, hh, 0:n],
                                                                               rhs=vp[0:nkp, grp_, 0:65], start=True, stop=False),
                         reads=[f"PT{pbi}_0{grp_}", vpr], writes=[f"ps{bk}"])
                    S.op("pe", lambda e, O=O, hh=hh, grp_=grp_, vc=vc: e.matmul(out=O[:, hh, :], lhsT=PT[pbi][0:nkc, 1, grp_, hh, 0:n],
                                                                               rhs=vc[0:nkc, grp_, 0:65], start=False, stop=True),
                         reads=[f"PT{pbi}_1{grp_}", vcr], writes=[f"ps{bk}"])
                dn = den[0:n, grp_ * 4:(grp_ + 1) * 4].unsqueeze(2)
                rd = rden[0:n, grp_ * 4:(grp_ + 1) * 4]
                esk = esink[0:n, l * 8 + grp_ * 4: l * 8 + grp_ * 4 + 4].unsqueeze(2)
                S.op("dve", lambda e, O=O, dn=dn, esk=esk: e.tensor_tensor(out=dn, in0=O[:, :, 64:65], in1=esk, op=ALU.add),
                     reads=[f"ps{bk}", "esink"], writes=[f"den{grp_}"])
                S.op("dve", lambda e, rd=rd, grp_=grp_: e.reciprocal(out=rd, in_=den[0:n, grp_ * 4:(grp_ + 1) * 4]),
                     reads=[f"den{grp_}"], writes=[f"rden{grp_}"])
                S.op("dve", lambda e, O=O, rd=rd, grp_=grp_: e.tensor_tensor(
                    out=atok[ab][0:n, grp_ * 256:(grp_ + 1) * 256].rearrange("p (h d) -> p h d", h=4), in0=O[:, :, 0:64],
                    in1=rd.unsqueeze(2).to_broadcast([n, 4, 64]), op=ALU.mult),
                    reads=[f"ps{bk}", f"rden{grp_}"], writes=[f"atok{ab}"])

        def attn_tr(l, t, ti):
            n = t["n"]
            cs = slice(t["col"], t["col"] + n)
            ab = t["T"] % 2
            bk = pbank()
            pT = psb[bk][:].bitcast(BF16).rearrange("p (k t) -> p k t", k=8)
            for j in range(4):
                S.op("pe", lambda e, j=j, pT=pT: e.transpose(out=pT[:, j, 0:n], in_=atok[ab][0:n, j * 128:(j + 1) * 128], identity=ident[0:n, 0:n]),
                     reads=[f"atok{ab}", "ident"], writes=[f"ps{bk}"])
            S.op("act", lambda e, pT=pT: e.activation(out=aT[:, :, cs], in_=pT[:, 0:4, 0:n], func=ACTF.Copy),
                 reads=[f"ps{bk}"], writes=[f"aT{ti}"])

        def stage_D(l, grp, W):
            sub = [(ti, t) for ti, t in enumerate(grp["tiles"]) if t["col"] >= cstart(l, grp)]
            nt_ = len(sub)
            for i in range(nt_ + 2):
                if i < nt_:
                    attn_scores(l, sub[i][1], i % 2)
                if 0 <= i - 1 < nt_:
                    attn_pv(l, sub[i - 1][1], sub[i - 1][0], (i - 1) % 2)
                if 0 <= i - 2 < nt_:
                    attn_tr(l, sub[i - 2][1], sub[i - 2][0])
                if i == 0:
                    pass

        def stage_E(l, grp, W):
            N = grp["N"]
            C0 = cstart(l, grp)
            nt = len(grp["tiles"])
            xres = [f"xnT{ti}" for ti, t_ in enumerate(grp["tiles"]) if t_["col"] >= C0]
            ares = [f"aT{ti}" for ti, t_ in enumerate(grp["tiles"]) if t_["col"] >= C0]
            pres = [f"plT{g}" for g in range(4)]
            for mp in range(4):
                wg, wgr = W.get(U_EG(mp))
                wb, wbr = W.get(U_EB(mp))
                wgv = wg[:].rearrange("p (k c) -> p k c", k=8)
                wbv = wb[:, 0:2048].rearrange("p (k c) -> p k c", k=4)
                for mi in range(2):
                    m = 2 * mp + mi
                    bYA, bYB, bG0, bG1 = pbank(), pbank(), pbank(), pbank()
                    for gi, bG in enumerate((bG0, bG1)):
                        for k in range(8):
                            S.op("pe", lambda e, k=k, mi=mi, bG=bG, gi=gi, wgv=wgv: e.matmul(
                                out=psb[bG][:, C0:N], lhsT=wgv[:, k, gi * 256 + mi * 128: gi * 256 + (mi + 1) * 128],
                                rhs=xnT[:, k, C0:N], start=(k == 0), stop=(k == 7)),
                                reads=[wgr] + xres, writes=[f"ps{bG}"])
                    for k in range(4):
                        S.op("pe", lambda e, k=k, mi=mi, bYB=bYB, wbv=wbv: e.matmul(out=psb[bYB][:, C0:N], lhsT=wbv[:, k, 256 + mi * 128:256 + (mi + 1) * 128],
                                                                          rhs=plT[:, k, C0:N], start=(k == 0), stop=(k == 3)),
                             reads=[wbr] + pres, writes=[f"ps{bYB}"])
                    for k in range(4):
                        S.op("pe", lambda e, k=k, mi=mi, bYA=bYA, wbv=wbv: e.matmul(out=psb[bYA][:, C0:N], lhsT=wbv[:, k, mi * 128:(mi + 1) * 128],
                                                                          rhs=aT[:, k, C0:N], start=(k == 0), stop=(k == 3)),
                             reads=[wbr] + ares, writes=[f"ps{bYA}"])
                    S.op("act", lambda e, bG0=bG0, m=m: e.activation(out=s0b[:, C0:N], in_=psb[bG0][:, C0:N], func=ACTF.Sigmoid,
                                                                     bias=pvc(l, OFF_GB[0] + m, 1), scale=1.0),
                         reads=[f"ps{bG0}", "pv"], writes=["s0b"])
                    S.op("act", lambda e, bG1=bG1, m=m: e.activation(out=s1b[:, C0:N], in_=psb[bG1][:, C0:N], func=ACTF.Sigmoid,
                                                                     bias=pvc(l, OFF_GB[1] + m, 1), scale=1.0),
                         reads=[f"ps{bG1}", "pv"], writes=["s1b"])
                    S.op("dve", lambda e, bYA=bYA: e.tensor_tensor(out=s0b[:, C0:N], in0=psb[bYA][:, C0:N], in1=s0b[:, C0:N], op=ALU.mult),
                         reads=[f"ps{bYA}", "s0b"], writes=["s0b"])
                    S.op("dve", lambda e, bYB=bYB: e.tensor_tensor(out=s1b[:, C0:N], in0=psb[bYB][:, C0:N], in1=s1b[:, C0:N], op=ALU.mult),
                         reads=[f"ps{bYB}", "s1b"], writes=["s1b"])
                    S.op("dve", lambda e, m=m: e.tensor_tensor(out=mixT[:, m, C0:N], in0=s0b[:, C0:N], in1=s1b[:, C0:N], op=ALU.add),
                         reads=["s0b", "s1b"], writes=[f"mixT{m}"])
                W.done(U_EG(mp))
                W.done(U_EB(mp))

        def resid_add(grp, t, bk, cb):
            xt, xr = xs_of(t)
            n = t["n"]
            xv = xt[0:n, cb * 512:(cb + 1) * 512]
            if grp is groups[0] and t["kind"] == "p":
                S.op("dve", lambda e: e.scalar_tensor_tensor(out=xv, in0=psb[bk][0:n, 0:512], scalar=hvt[0:n, 0:1], in1=xv,
                                                             op0=ALU.mult, op1=ALU.add),
                     reads=[f"ps{bk}", xr, "hvt"], writes=[xr])
            else:
                S.op("dve", lambda e: e.tensor_tensor(out=xv, in0=psb[bk][0:n, 0:512], in1=xv, op=ALU.add),
                     reads=[f"ps{bk}", xr], writes=[xr])

        def stage_F(l, grp, W):
            sub = [(ti, t) for ti, t in enumerate(grp["tiles"]) if t["col"] >= cstart(l, grp)]
            nt_ = len(sub)
            wovs = []
            for cb in range(2):
                wo, wor = W.get(U_OUT + cb)
                wovs.append((wo[:].rearrange("p (k c) -> p k c", k=8), wor))
            for si in range(nt_ + 2):
                if si < nt_:
                    ti, t = sub[si]
                    n = t["n"]
                    cs = slice(t["col"], t["col"] + n)
                    fbanks = [pbank(), pbank()]
                    kparts = ((range(0, 7), range(7, 8)) if si == 0 else (range(0, 8),))
                    for kr in kparts:
                        for cb in range(2):
                            wov, wor = wovs[cb]
                            bk = fbanks[cb]
                            for k in kr:
                                S.op("pe", lambda e, k=k, bk=bk, cs=cs, n=n, wov=wov: e.matmul(out=psb[bk][0:n, 0:512], lhsT=mixT[:, k, cs], rhs=wov[:, k, :],
                                                                                     start=(k == 0), stop=(k == 7)),
                                     reads=[wor, f"mixT{k}"], writes=[f"ps{bk}"])
                    for cb in range(2):
                        resid_add(grp, t, fbanks[cb], cb)
                    norm_pre(l, 1, t, ti)
                if 0 <= si - 2 < nt_:
                    norm_post(l, 1, sub[si - 2][1], sub[si - 2][0])
            W.done(U_OUT)
            W.done(U_OUT + 1)

        def stage_G(l, grp, W):
            N = grp["N"]
            C0 = cstart(l, grp)
            xres = [f"xnT{ti}" for ti, t_ in enumerate(grp["tiles"]) if t_["col"] >= C0]
            segs = [(max(a, C0), bb, hid) for (a, bb, hid) in grp["segs"] if bb > C0]
            cw0 = pvc(l, OFF_CW[0], NCH); cw1 = pvc(l, OFF_CW[1], NCH)
            for (a, bb, hid) in segs:
                hs = hist[(hid, l)]
                hr = f"hist{hid}{l}"
                cr = corr[hid]
                S.op("pool", lambda e, hs=hs, cr=cr: e.tensor_tensor(out=cr[:, :, 0], in0=hs[:, :, 0], in1=cw0, op=ALU.mult),
                     reads=[hr, "pv"], writes=[f"corr{hid}"])
                S.op("pool", lambda e, hs=hs: e.tensor_tensor(out=ctmp[:, :], in0=hs[:, :, 1], in1=cw1, op=ALU.mult),
                     reads=[hr, "pv"], writes=["ctmp"])
                S.op("pool", lambda e, cr=cr: e.tensor_tensor(out=cr[:, :, 0], in0=cr[:, :, 0], in1=ctmp[:, :], op=ALU.add),
                     reads=[f"corr{hid}", "ctmp"], writes=[f"corr{hid}"])
                S.op("pool", lambda e, hs=hs, cr=cr: e.tensor_tensor(out=cr[:, :, 1], in0=hs[:, :, 1], in1=cw0, op=ALU.mult),
                     reads=[hr, "pv"], writes=[f"corr{hid}"])
            for f in range(11):
                wu, wur = W.get(U_UP + f)
                wuv = wu[:].rearrange("p (k c) -> p k c", k=8)
                for pi in range(2):
                    j = 2 * f + pi
                    ub = j % 2
                    for half, (c, ubuf, ures) in enumerate(((j, ug[ub], f"ug{ub}"), (22 + j, uv[ub], f"uv{ub}"))):
                        bk = pbank()
                        csp = grp["tiles"][-1]["col"]
                        splits = ((C0, csp, xres[:-1]), (csp, N, xres[-1:])) if (j == 0 and csp > C0) else ((C0, N, xres),)
                        for (c0, c1, rr) in splits:
                            for k in range(8):
                                S.op("pe", lambda e, k=k, bk=bk, half=half, pi=pi, wuv=wuv, c0=c0, c1=c1: e.matmul(
                                    out=psb[bk][:, c0:c1], lhsT=wuv[:, k, half * 256 + pi * 128: half * 256 + (pi + 1) * 128],
                                    rhs=xnT[:, k, c0:c1], start=(k == 0), stop=(k == 7)),
                                    reads=[wur] + rr, writes=[f"ps{bk}"])
                        S.op("act", lambda e, bk=bk, c=c, ubuf=ubuf: e.activation(out=ubuf[:, C0:N], in_=psb[bk][:, C0:N], func=ACTF.Identity,
                                                                                 bias=pvc(l, OFF_CB + c, 1), scale=pvc(l, OFF_CW[2] + c, 1)),
                             reads=[f"ps{bk}", "pv"], writes=[ures])
                        for (a, bb, hid) in segs:
                            S.op("act", lambda e, bk=bk, c=c, bb=bb, hid=hid: e.activation(out=hist[(hid, l)][:, c, :], in_=psb[bk][:, bb - 2:bb],
                                                                                         func=ACTF.Copy),
                                 reads=[f"ps{bk}"], writes=[f"hist{hid}{l}"])
                        for (a, bb, hid) in segs:
                            S.op("dve", lambda e, bk=bk, c=c, ubuf=ubuf, a=a, bb=bb: e.scalar_tensor_tensor(
                                out=ubuf[:, a + 1:bb], in0=psb[bk][:, a:bb - 1], scalar=pvc(l, OFF_CW[1] + c, 1), in1=ubuf[:, a + 1:bb],
                                op0=ALU.mult, op1=ALU.add), reads=[f"ps{bk}", ures, "pv"], writes=[ures])
                            S.op("dve", lambda e, bk=bk, c=c, ubuf=ubuf, a=a, bb=bb: e.scalar_tensor_tensor(
                                out=ubuf[:, a + 2:bb], in0=psb[bk][:, a:bb - 2], scalar=pvc(l, OFF_CW[0] + c, 1), in1=ubuf[:, a + 2:bb],
                                op0=ALU.mult, op1=ALU.add), reads=[f"ps{bk}", ures, "pv"], writes=[ures])
                            S.op("dve", lambda e, c=c, ubuf=ubuf, a=a, hid=hid: e.tensor_tensor(
                                out=ubuf[:, a:a + 2], in0=ubuf[:, a:a + 2], in1=corr[hid][:, c, :], op=ALU.add),
                                reads=[ures, f"corr{hid}"], writes=[ures])
                    S.op("act", lambda e, ub=ub: e.activation(out=ug[ub][:, C0:N], in_=ug[ub][:, C0:N], func=ACTF.Gelu),
                         reads=[f"ug{ub}"], writes=[f"ug{ub}"])
                    S.op("dve" if (grp is groups[0] or j >= 20) else "pool", lambda e, ub=ub, j=j: e.tensor_tensor(out=actT[:, j, C0:N], in0=ug[ub][:, C0:N], in1=uv[ub][:, C0:N], op=ALU.mult),
                         reads=[f"ug{ub}", f"uv{ub}"], writes=[f"actT{j}"])
                W.done(U_UP + f)
            for t in grp["tiles"]:
                if t["kind"] == "s":
                    S.dma("sp", lambda e: e.dma_start(out=cs_o[l], in_=hist[("s", l)][:]), reads=[f"hists{l}"],
                          stream="misc_out", is_output=True)
                elif t["last"]:
                    S.dma("sp", lambda e: e.dma_start(out=co_o[l], in_=hist[("p", l)][:]), reads=[f"histp{l}"],
                          stream="misc_out", is_output=True)

        def stage_H(l, grp, W):
            halves = ((0, 12), (12, 22))
            if l == 1:
                gi_ = groups.index(grp)
                if gi_ + 1 < len(groups):
                    for ti2, t2 in enumerate(groups[gi_ + 1]["tiles"]):
                        if t2["T"] in xloaded:
                            norm_pre(0, 0, t2, ti2)
            for hi, (k0, k1) in enumerate(halves):
                for ti, t in enumerate(grp["tiles"]):
                    n = t["n"]
                    cs = slice(t["col"], t["col"] + n)
                    if (l == 1 and t["halo"]) or t["col"] < cstart(l, grp):
                        if hi == 1 and l == 1:
                            load_x_tile(t["T"] + NXS)
                        continue
                    for cb in range(2):
                        bk = pbank()
                        for kc in range(k0, k1):
                            wd, wdr = W.get(U_DN + kc // 4)
                            wdv = wd[:].rearrange("p (k c) -> p k c", k=4)
                            S.op("pe", lambda e, kc=kc, bk=bk, cs=cs, n=n, wdv=wdv, cb=cb, k0=k0, k1=k1: e.matmul(
                                out=psb[bk][0:n, 0:512], lhsT=actT[:, kc, cs], rhs=wdv[:, kc % 4, cb * 512:(cb + 1) * 512],
                                start=(kc == k0), stop=(kc == k1 - 1)),
                                reads=[wdr, f"actT{kc}"], writes=[f"ps{bk}"])
                        resid_add(grp, t, bk, cb)
                    if hi == 1 and l == 0:
                        norm_pre(1, 0, t, ti)
                        if ti > 0 and (grp["tiles"][ti - 1]["T"], 1, 0) in pre_done:
                            norm_post(1, 0, grp["tiles"][ti - 1], ti - 1)
                    if hi == 1 and l == 1:
                        xt, xr = xs_of(t)
                        if t["kind"] == "s":
                            S.dma("sp", lambda e, xt=xt: e.dma_start(out=ys_o, in_=xt[0:TS, :]), reads=[xr], stream=xr, is_output=True)
                        elif not t["halo"]:
                            S.dma("sp", lambda e, xt=xt, t=t: e.dma_start(out=yp_o[t["orow"]:t["orow"] + 128, :], in_=xt[:, :]),
                                  reads=[xr], stream=xr, is_output=True)
                        load_x_tile(t["T"] + NXS)
                for u in range(3):
                    W.done(U_DN + hi * 3 + u)
                if hi == 0 and l == 1:
                    gi_ = groups.index(grp)
                    if gi_ + 1 < len(groups):
                        for ti2, t2 in enumerate(groups[gi_ + 1]["tiles"]):
                            if (t2["T"], 0, 0) in pre_done:
                                norm_post(0, 0, t2, ti2)

        def carry(l, grp):
            npr = grp["np"]
            w = npr - 1
            S.op("pool", lambda e: e.tensor_copy(out=kcar[l][:], in_=kTw[:, :, w * 128:(w + 1) * 128]),
                 reads=["kTw0", "kTw1"], writes=[f"kcar{l}"])
            S.op("pool", lambda e: e.tensor_copy(out=Vcar[l][:], in_=Vw[:, w]), reads=[f"Vw{w}"], writes=[f"Vcar{l}"])
            S.op("pool", lambda e: e.tensor_copy(out=pincar[l][:], in_=pinw[:, w, :]), reads=[f"pinw{w}"], writes=[f"pincar{l}"])

        for Tn in range(NXS):
            load_x_tile(Tn)
        late_setup()
        for gi, grp in enumerate(groups):
            S.epoch = gi
            for l in range(2):
                W = WU()
                if DEBUG_STOP == "setup":
                    break
                for ti, t in enumerate(grp["tiles"]):
                    if (t["T"], l, 0) not in pre_done:
                        norm_pre(l, 0, t, ti)
                for ti, t in enumerate(grp["tiles"]):
                    if (t["T"], l, 0) not in post_done:
                        if ti == len(grp["tiles"]) - 1 and ti > 0:
                            pending_post.append(lambda l=l, t=t, ti=ti: norm_post(l, 0, t, ti))
                        else:
                            norm_post(l, 0, t, ti)
                if DEBUG_STOP == "A":
                    break
                stage_B(l, grp, W)
                if DEBUG_STOP in ("B", "B1", "B2", "B3", "B4", "B5"):
                    break
                pool_proj(l, grp, W)
                stage_D(l, grp, W)
                if DEBUG_STOP == "D":
                    break
                stage_E(l, grp, W)
                if DEBUG_STOP == "E":
                    break
                stage_F(l, grp, W)
                if DEBUG_STOP == "F":
                    break
                stage_G(l, grp, W)
                if DEBUG_STOP == "G":
                    break
                stage_H(l, grp, W)
                carry(l, grp)
                ucur["i"] += UNITS_PER_LAYER
                if DEBUG_ONE_LAYER and gi == len(groups) - 1:
                    break
            if gi == 0:
                for t in range(4):
                    S.op("pool", lambda e, t=t: e.memset(Vw[:, t, :, 64:65], 1.0), writes=[f"Vw{t}"])
        S.emit()
        build_nc.stats = S.stats
    return nc


def _fm(a, k):
    C = a.shape[1]
    return np.ascontiguousarray(a.reshape(k, 128, C).transpose(1, 0, 2)).reshape(128, k * C)


def _pack_units(w_in, w_pool, w_br_attn, w_br_pool, w_out, w_up, w_down):
    out = np.zeros((2 * UNITS_PER_LAYER, 128, UEL), np.float32)
    for l in range(2):
        b = l * UNITS_PER_LAYER
        wi = w_in[l]
        out[b + U_Q] = _fm(wi[:, 0:512], 8)
        k0, k1, v = wi[:, 512:576], wi[:, 576:640], wi[:, 640:768]
        out[b + U_KV, :, 0:3072] = _fm(np.concatenate([k0, k0, k1, k1, v], axis=1), 8)
        out[b + U_PIN] = _fm(wi[:, 768:1280], 8)
        out[b + U_POOL, :, 0:512] = np.ascontiguousarray(w_pool[l].transpose(1, 0, 2)).reshape(128, 512)
        for mp in range(4):
            g0 = wi[:, 1280 + mp * 256: 1280 + (mp + 1) * 256]
            g1 = wi[:, 2304 + mp * 256: 2304 + (mp + 1) * 256]
            out[b + U_EG(mp)] = _fm(np.concatenate([g0, g1], axis=1), 8)
            ba = w_br_attn[l][:, mp * 256:(mp + 1) * 256]
            bp = w_br_pool[l][:, mp * 256:(mp + 1) * 256]
            out[b + U_EB(mp), :, 0:2048] = _fm(np.concatenate([ba, bp], axis=1), 4)
        out[b + U_OUT] = _fm(w_out[l][:, 0:512], 8)
        out[b + U_OUT + 1] = _fm(w_out[l][:, 512:1024], 8)
        for f in range(11):
            ga = w_up[l][:, 2 * f * 128:(2 * f + 2) * 128]
            va = w_up[l][:, DFF + 2 * f * 128: DFF + (2 * f + 2) * 128]
            out[b + U_UP + f] = _fm(np.concatenate([ga, va], axis=1), 8)
        for i in range(6):
            rows = w_down[l][i * 512:(i + 1) * 512]
            kk = rows.shape[0] // 128
            out[b + U_DN + i, :, 0:kk * 1024] = _fm(rows, kk)
    return out


def _pack_vec(norm_mix, norm_ffn, gate_bias, pool_scale, conv_w, conv_b, q_norm, k_norm, sinks):
    pvt = np.zeros((128, 2 * PVL), np.float32)
    for l in range(2):
        b = l * PVL
        pvt[:, b + 0:b + 8] = norm_mix[l].reshape(8, 128).T
        pvt[:, b + 8:b + 16] = norm_ffn[l].reshape(8, 128).T
        pvt[:, b + 16:b + 24] = gate_bias[l, 0].reshape(8, 128).T
        pvt[:, b + 24:b + 32] = gate_bias[l, 1].reshape(8, 128).T
        pvt[:, b + 32:b + 36] = pool_scale[l].reshape(4, 128).T
        for j in range(3):
            pvt[:, b + 36 + 44 * j: b + 36 + 44 * (j + 1)] = conv_w[l, j].reshape(NCH, 128).T
        pvt[:, b + 168:b + 212] = conv_b[l].reshape(NCH, 128).T
        pvt[:, b + 212] = np.concatenate([q_norm[l], q_norm[l]])
        pvt[:, b + 213] = np.concatenate([k_norm[l], k_norm[l]])
        pvt[:, b + 214:b + 222] = np.broadcast_to(sinks[l][None, :], (128, 8))
    return pvt


def _const_tables():
    slopes = np.array([2.0 ** (-(h + 1)) for h in range(8)], np.float32)
    s = np.arange(128)[:, None]
    q = np.arange(128)[None, :]
    NEG = np.float32(-1e30)
    distA = (q + 128 - s).astype(np.float32)
    maskA = (q >= 64) & (s < 64)
    distB = np.abs(q - s).astype(np.float32)
    maskB = (q < 64) & (s >= 64)
    biasA = np.zeros((128, 8, 128), np.float32)
    biasB = np.zeros((128, 8, 128), np.float32)
    for h in range(8):
        biasA[:, h, :] = np.where(maskA, NEG, -8.0 * slopes[h] * distA)
        biasB[:, h, :] = np.where(maskB, NEG, -8.0 * slopes[h] * distB)
    wins = (2, 4, 8, 16)
    tcur = np.zeros((128, 4, 128), np.float32)
    tprev = np.zeros((128, 4, 128), np.float32)
    tfirst = np.zeros((128, 4, 128), np.float32)
    t = np.arange(128)[None, :]
    for g, w in enumerate(wins):
        inwin = (s <= t) & (s > t - w)
        tcur[:, g, :] = np.where(inwin, 1.0 / w, 0.0) - (s == t)
        tprev[:, g, :] = np.where(s > 128 + t - w, 1.0 / w, 0.0)
        cnt = np.minimum(t + 1, w).astype(np.float32)
        tfirst[:, g, :] = np.where(inwin, 1.0 / cnt, 0.0) - (s == t)
    bones = np.zeros((128, 128), np.float32)
    bones[0:64, 0:64] = 1.0 / 64
    bones[64:128, 64:128] = 1.0 / 64
    ident = np.eye(128, dtype=np.float32)
    return biasA, biasB, tcur, tprev, tfirst, bones, ident


_NC_CACHE = {}


def kernel(x_prompt, x_sample, cache_k, cache_v, state_pool, state_conv,
           norm_mix, w_in, q_norm, k_norm, sinks, w_pool, pool_scale,
           w_br_attn, w_br_pool, gate_bias, w_out, norm_ffn, w_up, conv_w, conv_b, w_down):
    f = lambda a: np.asarray(a, dtype=np.float32)
    x_prompt, x_sample, cache_k, cache_v, state_pool, state_conv = map(f, (x_prompt, x_sample, cache_k, cache_v, state_pool, state_conv))
    wun = _pack_units(f(w_in), f(w_pool), f(w_br_attn), f(w_br_pool), f(w_out), f(w_up), f(w_down))
    pvt = _pack_vec(f(norm_mix), f(norm_ffn), f(gate_bias), f(pool_scale), f(conv_w), f(conv_b), f(q_norm), f(k_norm), f(sinks))
    biasA, biasB, tcur, tprev, tfirst, bones, ident = _const_tables()
    B, SEQ = x_prompt.shape[0], x_prompt.shape[1]
    per = SEQ // 4
    in_maps = []
    for c in range(NCORES):
        b, qt = c // 4, c % 4
        start = qt * per
        xp = np.zeros((NPT * 128, D), np.float32)
        if qt > 0:
            xp[0:384] = x_prompt[b, start - 384:start]
        xp[384:] = x_prompt[b, start:start + per]
        ckT = np.zeros((2, 128, 4, 128), np.float32)
        for l in range(2):
            kt = cache_k[l, c].transpose(1, 2, 0)
            for g in range(2):
                ckT[l, 0:64, 2 * g] = kt[g]
                ckT[l, 64:128, 2 * g + 1] = kt[g]
        sph = np.zeros((2, 128, 512), np.float32)
        sph[:, 113:128] = state_pool[:, c]
        scT = np.ascontiguousarray(state_conv[:, c].reshape(2, 2, NCH, 128).transpose(0, 3, 2, 1))
        in_maps.append({
            "xp": xp, "xsm": np.ascontiguousarray(x_sample[c]), "wun": wun, "pvec": pvt,
            "biasA": biasA, "biasB": biasB, "tcur": tcur, "tprev": tprev,
            "tfirst": tfirst if qt == 0 else tcur, "bones": bones, "ident": ident,
            "hv": np.full((128, 1), 0.0 if qt == 0 else 1.0, np.float32),
            "ckT": ckT, "cktm": np.ascontiguousarray(cache_k[:, c].reshape(2, 128, 128)), "cv": np.ascontiguousarray(cache_v[:, c]), "sph": sph, "scT": scT,
        })
    if "nc" not in _NC_CACHE:
        _NC_CACHE["nc"] = build_nc()
    nc = _NC_CACHE["nc"]
    res = run_bass_kernel_spmd(nc, in_maps, core_ids=list(range(NCORES)))
    R = res.results
    y_prompt = np.zeros((B, SEQ, D), np.float32)
    for c in range(NCORES):
        b, qt = c // 4, c % 4
        y_prompt[b, qt * per:(qt + 1) * per] = R[c]["yp"]
    y_sample = np.stack([R[c]["ys"] for c in range(NCORES)], 0)

    def kfix(a):
        return np.ascontiguousarray(a.transpose(0, 3, 2, 1))

    def cfix(a):
        return np.ascontiguousarray(a.transpose(0, 3, 2, 1)).reshape(2, 2, NCH * 128)

    lastc = [3, 7]
    k_prompt = np.stack([kfix(R[c]["kTo"]) for c in lastc], 1)
    v_prompt = np.stack([R[c]["vo"].reshape(2, 128, 2, 64) for c in lastc], 1)
    pool_prompt = np.stack([R[c]["po"][:, 113:128] for c in lastc], 1)
    conv_prompt = np.stack([cfix(R[c]["co"]) for c in lastc], 1)
    k_sample = np.stack([np.concatenate([R[c]["kcp"].reshape(2, 128 - TS, 2, 64), kfix(R[c]["kTs"])], axis=1) for c in range(NCORES)], 1)
    v_sample = np.stack([R[c]["vs"].reshape(2, 128, 2, 64) for c in range(NCORES)], 1)
    pool_sample = np.stack([R[c]["pso"][:, 1:TS] for c in range(NCORES)], 1)
    conv_sample = np.stack([cfix(R[c]["cso"]) for c in range(NCORES)], 1)
    return (y_prompt, y_sample, k_prompt.astype(np.float32), v_prompt.astype(np.float32), pool_prompt, conv_prompt,
            k_sample.astype(np.float32), v_sample, pool_sample, conv_sample)
```

```python
from contextlib import ExitStack
import numpy as np
import concourse.bass as bass
import concourse.mybir as mybir
from concourse.bass_utils import run_bass_kernel_spmd

F32 = mybir.dt.float32
BF16 = mybir.dt.bfloat16
ACTF = mybir.ActivationFunctionType
ALU = mybir.AluOpType

NCORES = 8
D = 1024
OWN_TILES = 32
HALO_TILES = 3
NPT = OWN_TILES + HALO_TILES
TS = 16
NSLOT = 6
UEL = 4096
NXS = 7
EPS = 1e-6
DFF = 2816
NCH = 44
UNITS_PER_LAYER = 31
DEBUG_ONE_LAYER = False
DEBUG_STOP = None
PVL = 222


class _Op:
    __slots__ = ("eng", "fn", "reads", "writes", "kind", "stream", "is_output", "eidx", "sidx",
                 "waits", "signal", "sig", "K", "epoch", "gidx")


class Sched:
    ENGS = ["pe", "act", "dve", "pool", "sp"]

    def __init__(self, nc, es, nsem=4):
        self.nc = nc
        self.es = es
        self.ops = []
        self.epoch = 0
        self.nsem = nsem
        self.eng_sems = {}
        for e in self.ENGS:
            self.eng_sems[e] = [es.enter_context(nc.semaphore(f"sem_{e}{i}")) for i in range(nsem)]
        self.stream_sem = {}
        self.stream_cnt = {}
        self.out_streams = set()
        self.touch = {}

    def op(self, eng, fn, reads=(), writes=()):
        o = _Op()
        o.eng = eng; o.fn = fn; o.reads = tuple(reads); o.writes = tuple(writes)
        o.kind = "op"; o.stream = None; o.is_output = False
        o.waits = []; o.signal = False; o.sig = None; o.K = None; o.epoch = self.epoch
        o.gidx = len(self.ops)
        self.ops.append(o)
        for r in o.reads + o.writes:
            if r.startswith("ps"):
                self.touch[r] = o.gidx
        return o

    def dma(self, eng, fn, reads=(), writes=(), stream=None, is_output=False):
        o = self.op(eng, fn, reads, writes)
        o.kind = "dma"
        o.stream = stream
        o.is_output = is_output
        if stream not in self.stream_sem:
            self.stream_sem[stream] = self.es.enter_context(self.nc.semaphore(f"dsem_{stream}"))
            self.stream_cnt[stream] = 0
        o.sidx = self.stream_cnt[stream]
        self.stream_cnt[stream] += 1
        if is_output:
            self.out_streams.add(stream)
        return o

    def _analyze(self):
        last_writer = {}
        readers = {}
        stream_last = {}
        eng_ops = {e: [] for e in self.ENGS}
        eng_K = {e: {} for e in self.ENGS}
        for op in self.ops:
            deps = {}
            for r in op.reads:
                w = last_writer.get(r)
                if w is not None:
                    deps[w] = "raw"
                if r.startswith("ps"):
                    for rd in readers.get(r, ()):
                        if rd.eng != op.eng and rd not in deps:
                            deps[rd] = "rar"
            for wr in op.writes:
                w = last_writer.get(wr)
                if w is not None and w not in deps:
                    deps[w] = "waw"
                for rd in readers.get(wr, ()):
                    if rd is not op and rd not in deps:
                        deps[rd] = "war"
            if op.kind == "dma":
                prev = stream_last.get(op.stream)
                if prev is not None:
                    deps[prev] = "raw"
                stream_last[op.stream] = op
            for r in op.reads:
                readers.setdefault(r, []).append(op)
            for wr in op.writes:
                last_writer[wr] = op
                readers[wr] = []
            op.eidx = len(eng_ops[op.eng])
            eng_ops[op.eng].append(op)
            K = dict(eng_K[op.eng])
            need = {}
            for Dp, t in deps.items():
                if Dp.kind == "op":
                    if Dp.eng == op.eng and op.kind == "op" and op.eng == "pe":
                        continue
                    key = Dp.eng
                    val = Dp.eidx
                else:
                    key = ("s", Dp.stream)
                    val = Dp.sidx
                if K.get(key, -1) >= val:
                    continue
                cur = need.get(key)
                if cur is None or cur[0] < val:
                    need[key] = (val, Dp)
            for key, (val, Dp) in sorted(need.items(), key=lambda kv: -kv[1][1].gidx):
                if K.get(key, -1) >= val:
                    continue
                op.waits.append(Dp)
                Dp.signal = True
                for k2, v2 in Dp.K.items():
                    if K.get(k2, -1) < v2:
                        K[k2] = v2
                K[key] = val
            op.K = K
            eng_K[op.eng] = K
        cnt = {}
        for e in self.ENGS:
            for op in eng_ops[e]:
                if op.kind == "dma":
                    op.sig = (self.stream_sem[op.stream], 16 * (op.sidx + 1), 16)
                elif op.signal:
                    sem = self.eng_sems[e][op.epoch % self.nsem]
                    c = cnt.get(id(sem), 0) + 1
                    cnt[id(sem)] = c
                    op.sig = (sem, c, 1)
        return eng_ops

    def emit(self):
        eng_ops = self._analyze()
        nc = self.nc
        self.stats = {e: (len(eng_ops[e]), sum(len(o.waits) for o in eng_ops[e])) for e in self.ENGS}

        def run(eh, name):
            for op in eng_ops[name]:
                for Dp in op.waits:
                    eh.wait_ge(Dp.sig[0], Dp.sig[1])
                ins = op.fn(eh)
                if op.sig is not None:
                    ins.then_inc(op.sig[0], op.sig[2])
            if name == "sp":
                for s in sorted(self.out_streams):
                    eh.wait_ge(self.stream_sem[s], 16 * self.stream_cnt[s])

        with nc.Block() as block:
            @block.tensor
            def _(e):
                run(e, "pe")

            @block.scalar
            def _(e):
                run(e, "act")

            @block.vector
            def _(e):
                run(e, "dve")

            @block.gpsimd
            def _(e):
                run(e, "pool")

            @block.sync
            def _(e):
                run(e, "sp")


def unit_sizes():
    sz = [4096, 3072, 4096, 512]
    for _ in range(4):
        sz += [4096, 2048]
    sz += [4096, 4096]
    sz += [4096] * 11
    sz += [4096] * 5 + [2048]
    return sz


U_Q, U_KV, U_PIN, U_POOL = 0, 1, 2, 3
def U_EG(mp): return 4 + 2 * mp
def U_EB(mp): return 5 + 2 * mp
U_OUT = 12
U_UP = 14
U_DN = 25


def build_nc(n_groups_limit=None):
    nc = bass.Bass("TRN2", target_bir_lowering=False)

    def din(name, shape, dt=F32):
        return nc.dram_tensor(name, list(shape), dt, kind="ExternalInput").ap()

    def dout(name, shape, dt=F32):
        return nc.dram_tensor(name, list(shape), dt, kind="ExternalOutput").ap()

    xp = din("xp", [NPT * 128, D])
    xsm = din("xsm", [TS, D])
    wun = din("wun", [2 * UNITS_PER_LAYER, 128, UEL])
    pvec = din("pvec", [128, 2 * PVL])
    biasA_d = din("biasA", [128, 8, 128])
    biasB_d = din("biasB", [128, 8, 128])
    tcur_d = din("tcur", [128, 4, 128])
    tprev_d = din("tprev", [128, 4, 128])
    tfirst_d = din("tfirst", [128, 4, 128])
    bones_d = din("bones", [128, 128])
    ident_d = din("ident", [128, 128])
    hv_d = din("hv", [128, 1])
    ckT_d = din("ckT", [2, 128, 4, 128])
    cv_d = din("cv", [2, 128, 2, 64])
    sph_d = din("sph", [2, 128, 512])
    scT_d = din("scT", [2, 128, NCH, 2])
    cktm_d = din("cktm", [2, 128, 128])

    yp_o = dout("yp", [OWN_TILES * 128, D])
    ys_o = dout("ys", [TS, D])
    kTo_o = dout("kTo", [2, 64, 2, 128])
    vo_o = dout("vo", [2, 128, 128])
    po_o = dout("po", [2, 128, 512])
    co_o = dout("co", [2, 128, NCH, 2])
    kTs_o = dout("kTs", [2, 64, 2, TS])
    kcp_o = dout("kcp", [2, 128 - TS, 128])
    vs_o = dout("vs", [2, 128, 128])
    ps_o = dout("pso", [2, TS, 512])
    cs_o = dout("cso", [2, 128, NCH, 2])

    wsc = nc.dram_tensor("wsc", [2 * UNITS_PER_LAYER, 128, UEL], BF16, kind="Internal").ap()
    USZ = unit_sizes()

    es = ExitStack()
    with es:
        def sb(name, shape, dt):
            return es.enter_context(nc.sbuf_tensor(name, list(shape), dt))

        S = Sched(nc, es)

        xsl = [sb(f"xsl{i}", [128, D], F32) for i in range(NXS)]
        xh = [sb(f"xh{i}", [128, D], BF16) for i in range(4)]
        ssq = sb("ssq", [128, 4], F32)
        srs = sb("srs", [128, 4], F32)
        rstd = sb("rstd", [128, 4], F32)
        xnT = sb("xnT", [128, 8, 512], BF16)
        qT = sb("qT", [128, 4, 512], BF16)
        kTw = sb("kTw", [128, 4, 512], BF16)
        kcar = [sb(f"kcar{l}", [128, 4, 128], BF16) for l in range(2)]
        ksam = sb("ksam", [128, 4, TS], BF16)
        kcache = [sb(f"kcache{l}", [128, 4, 128], BF16) for l in range(2)]
        sq = [sb(f"sq{i}", [128, 512], BF16) for i in range(2)]
        srt = [sb(f"srt{i}", [128, 512], F32) for i in range(2)]
        Vw = sb("Vw", [128, 4, 2, 66], BF16)
        Vcar = [sb(f"Vcar{l}", [128, 2, 66], BF16) for l in range(2)]
        Vsam = sb("Vsam", [128, 2, 66], BF16)
        Vcache = [sb(f"Vcache{l}", [128, 2, 66], BF16) for l in range(2)]
        pinw = sb("pinw", [128, 4, 512], BF16)
        pincar = [sb(f"pincar{l}", [128, 512], BF16) for l in range(2)]
        pinsam = sb("pinsam", [128, 512], BF16)
        pinhist = [sb(f"pinhist{l}", [128, 512], BF16) for l in range(2)]
        dT = sb("dT", [128, 4, 512], BF16)
        plT = sb("plT", [128, 4, 512], BF16)
        PT = [sb(f"PT{i}", [128, 2, 2, 4, 128], BF16) for i in range(2)]
        atok = [sb(f"atok{i}", [128, 512], BF16) for i in range(2)]
        den = sb("den", [128, 8], F32)
        rden = sb("rden", [128, 8], F32)
        aT = sb("aT", [128, 4, 512], BF16)
        s0b = sb("s0b", [128, 512], F32)
        s1b = sb("s1b", [128, 512], F32)
        mixT = sb("mixT", [128, 8, 512], BF16)
        ug = [sb(f"ug{i}", [128, 512], F32) for i in range(2)]
        uv = [sb(f"uv{i}", [128, 512], F32) for i in range(2)]
        actT = sb("actT", [128, 22, 512], BF16)
        hist = {(h, l): sb(f"hist{h}{l}", [128, NCH, 2], F32) for h in "ps" for l in range(2)}
        corr = {h: sb(f"corr{h}", [128, NCH, 2], F32) for h in "ps"}
        ctmp = sb("ctmp", [128, NCH], F32)
        pv = sb("pv", [128, 2 * PVL], F32)
        esink = sb("esink", [128, 16], F32)
        epst = sb("epst", [128, 1], F32)
        hvt = sb("hvt", [128, 1], F32)
        biasA = sb("biasA_s", [128, 8, 128], BF16)
        biasB = sb("biasB_s", [128, 8, 128], BF16)
        tcur = sb("tcur_s", [128, 4, 128], BF16)
        tprev = sb("tprev_s", [128, 4, 128], BF16)
        tfirst = sb("tfirst_s", [128, 4, 128], BF16)
        bones = sb("bones_s", [128, 128], BF16)
        ident = sb("ident_s", [128, 128], BF16)
        wslot = [sb(f"wslot{i}", [128, UEL], BF16) for i in range(NSLOT)]
        kn32 = {(h, l, g): sb(f"kn32{h}{l}{g}", [128, 128 if h == "p" else TS], F32)
                for h in "ps" for l in range(2) for g in range(2)}
        v32 = {(h, l): sb(f"v32{h}{l}", [128, 128], F32) for h in "ps" for l in range(2)}
        _pin32 = [sb(f"pin32_{l}", [128, 512], F32) for l in range(2)]
        pin32 = {(h, l): _pin32[l] for h in "ps" for l in range(2)}

        psb = [es.enter_context(nc.psum_tensor(f"psb{i}", [128, 512], F32)) for i in range(8)]
        pstate = {"i": 0}

        def pbank():
            i = min(range(8), key=lambda b: S.touch.get(f"ps{b}", -1 - (8 - b)))
            S.touch[f"ps{i}"] = len(S.ops)
            pstate["i"] = (pstate["i"] + 1) % 8
            return i

        def pvc(l, off, n=1):
            return pv[:, l * PVL + off: l * PVL + off + n]
        OFF_G = [0, 8]; OFF_GB = [16, 24]; OFF_PS = 32
        OFF_CW = [36, 80, 124]; OFF_CB = 168; OFF_GQ = 212; OFF_GK = 213; OFF_SK = 214

        S.dma("sp", lambda e: e.dma_start(out=pv[:], in_=pvec), writes=["pv"], stream="setup0")
        S.dma("sp", lambda e: e.dma_start(out=hvt[:], in_=hv_d), writes=["hvt"], stream="setup1")
        def const_load(nm, dst, src):
            S.dma("pool", lambda e, dst=dst, src=src: e.dma_start(out=dst[:], in_=src), writes=[nm], stream="c_" + nm[:5] + nm[-1])
        for (nm, dst, src) in (("ident", ident, ident_d), ("bones", bones, bones_d)):
            const_load(nm, dst, src)

        def late_setup():
          for (nm, dst, src) in (("tcur", tcur, tcur_d), ("tprev", tprev, tprev_d), ("tfirst", tfirst, tfirst_d),
                                 ("biasA", biasA, biasA_d), ("biasB", biasB, biasB_d)):
              const_load(nm, dst, src)
          for l in range(2):
            S.dma("pool", lambda e, l=l: e.dma_start(out=kcache[l][:], in_=ckT_d[l]), writes=[f"kcache{l}"], stream="c_kc")
            S.dma("pool", lambda e, l=l: e.dma_start(out=Vcache[l][:, :, 0:64], in_=cv_d[l]), writes=[f"Vcache{l}"], stream="c_vc")
            S.dma("pool", lambda e, l=l: e.dma_start(out=pinhist[l][:], in_=sph_d[l]), writes=[f"pinhist{l}"], stream="c_ph")
            S.dma("sp", lambda e, l=l: e.dma_start(out=hist[("s", l)][:], in_=scT_d[l]), writes=[f"hists{l}"], stream="setup2")
            S.op("pool", lambda e, l=l: e.memset(Vcache[l][:, :, 64:66], 1.0), writes=[f"Vcache{l}"])
            S.op("pool", lambda e, l=l: e.memset(kcar[l][:], 0.0), writes=[f"kcar{l}"])
            S.op("pool", lambda e, l=l: e.memset(Vcar[l][:], 0.0), writes=[f"Vcar{l}"])
            S.op("pool", lambda e, l=l: e.memset(pincar[l][:], 0.0), writes=[f"pincar{l}"])
            S.op("pool", lambda e, l=l: e.memset(hist[("p", l)][:], 0.0), writes=[f"histp{l}"])
            S.dma("sp", lambda e, l=l: e.dma_start(out=vs_o[l, 0:128 - TS, :],
                                                   in_=cv_d[l, TS:128].rearrange("s g d -> s (g d)")),
                  stream="misc_out", is_output=True)
            S.dma("sp", lambda e, l=l: e.dma_start(out=kcp_o[l], in_=cktm_d[l, TS:128, :]),
                  stream="misc_out", is_output=True)
        S.op("pool", lambda e: e.memset(epst[:], EPS), writes=["epst"])
        S.op("pool", lambda e: e.memset(kTw[:], 0.0), writes=["kTw0", "kTw1"])
        S.op("pool", lambda e: e.memset(ksam[:], 0.0), writes=["ksam0", "ksam1"])
        S.op("pool", lambda e: e.memset(Vsam[:], 1.0), writes=["Vsam"])
        S.op("pool", lambda e: e.memset(Vw[:], 0.0), writes=[f"Vw{t}" for t in range(4)])
        for t in range(4):
            S.op("dve", lambda e, t=t: e.tensor_copy(out=Vw[:, t, :, 64:65], in_=hvt[:, 0:1].unsqueeze(1).to_broadcast([128, 2, 1])),
                 reads=["hvt"], writes=[f"Vw{t}"])
        for l in range(2):
            S.op("act", lambda e, l=l: e.activation(out=esink[:, l * 8:(l + 1) * 8], in_=pvc(l, OFF_SK, 8), func=ACTF.Exp),
                 reads=["pv"], writes=["esink"])

        n_groups = 1 + OWN_TILES // 4
        if n_groups_limit is not None:
            n_groups = n_groups_limit
        useq = [(g, l, u) for g in range(n_groups) for l in range(2) for u in range(UNITS_PER_LAYER)]
        wst = {"next": 0}

        def issue_load():
            i = wst["next"]
            if i >= len(useq):
                return
            wst["next"] = i + 1
            g, l, u = useq[i]
            slot = i % NSLOT
            gu = l * UNITS_PER_LAYER + u
            n = USZ[u]
            if g == 0:
                S.dma("pool", lambda e: e.dma_start(out=wslot[slot][:, 0:n], in_=wun[gu, :, 0:n]),
                      writes=[f"ws{slot}"], stream=f"wp{slot}")
                S.dma("sp", lambda e: e.dma_start(out=wsc[gu, :, 0:n], in_=wslot[slot][:, 0:n]),
                      reads=[f"ws{slot}"], writes=[f"wsc{gu}"], stream=f"ww{slot}")
            else:
                S.dma("sp", lambda e: e.dma_start(out=wslot[slot][:, 0:n], in_=wsc[gu, :, 0:n]),
                      reads=[f"wsc{gu}"], writes=[f"ws{slot}"], stream=f"ws{slot}")

        ucur = {"i": 0}

        class WU:
            def __init__(self):
                self.base = ucur["i"]

            def get(self, u):
                i = self.base + u
                slot = i % NSLOT
                return wslot[slot], f"ws{slot}"

            def done(self, u):
                issue_load()

        for _ in range(NSLOT):
            issue_load()

        groups = []
        g0 = {"tiles": [], "N": 3 * 128 + TS, "segs": [(0, 384, "p"), (384, 384 + TS, "s")], "np": 3}
        T = 0
        for i in range(3):
            g0["tiles"].append(dict(kind="p", n=128, col=i * 128, row=i * 128, halo=True, wslot=i, T=T, first=False, last=False))
            T += 1
        g0["tiles"].append(dict(kind="s", n=TS, col=384, row=0, halo=False, wslot=3, T=T, first=False, last=False))
        T += 1
        groups.append(g0)
        for gi in range(OWN_TILES // 4):
            gg = {"tiles": [], "N": 512, "segs": [(0, 512, "p")], "np": 4}
            for i in range(4):
                ot = gi * 4 + i
                gg["tiles"].append(dict(kind="p", n=128, col=i * 128, row=(3 + ot) * 128, halo=False, wslot=i, T=T,
                                        first=(ot == 0), last=(ot == OWN_TILES - 1), orow=ot * 128))
                T += 1
            groups.append(gg)
        groups = groups[:n_groups]
        if n_groups_limit is not None:
            groups[-1]["tiles"][-1]["last"] = True

        def xs_of(t):
            return xsl[t["T"] % NXS], f"x{t['T'] % NXS}"

        all_tiles = [t for g_ in groups for t in g_["tiles"]]

        xloaded = set()

        def load_x_tile(Tn):
            if Tn >= len(all_tiles):
                return
            xloaded.add(Tn)
            t = all_tiles[Tn]
            xt, xr = xs_of(t)
            if t["kind"] == "p":
                S.dma("sp", lambda e, xt=xt, t=t: e.dma_start(out=xt[:, :], in_=xp[t["row"]:t["row"] + 128, :]),
                      writes=[xr], stream=xr)
            else:
                S.dma("sp", lambda e, xt=xt: e.dma_start(out=xt[0:TS, :], in_=xsm), writes=[xr], stream=xr)

        pre_done = set()

        def norm_pre(l, ni, t, ti):
            pre_done.add((t["T"], l, ni))
            xt, xr = xs_of(t)
            n = t["n"]
            b = ti
            S.op("act", lambda e: e.activation(out=xh[b][0:n, :], in_=xt[0:n, :], func=ACTF.Square, scale=1.0 / 32.0,
                                               accum_out=ssq[0:n, ti:ti + 1]), reads=[xr], writes=[f"xh{b}", f"ssq{ti}"])
            S.op("act", lambda e: e.activation(out=srs[0:n, ti:ti + 1], in_=ssq[0:n, ti:ti + 1], func=ACTF.Ln,
                                               bias=epst[0:n, 0:1], scale=1.0), reads=[f"ssq{ti}", "epst"], writes=[f"srs{ti}"])
            S.op("act", lambda e: e.activation(out=rstd[0:n, ti:ti + 1], in_=srs[0:n, ti:ti + 1], func=ACTF.Exp, scale=-0.5),
                 reads=[f"srs{ti}"], writes=[f"rstd{ti}"])
            S.op("act", lambda e: e.activation(out=xh[b][0:n, :], in_=xt[0:n, :], func=ACTF.Identity, scale=rstd[0:n, ti:ti + 1]),
                 reads=[xr, f"rstd{ti}"], writes=[f"xh{b}"])

        post_done = set()
        pending_post = []

        def norm_post(l, ni, t, ti):
            post_done.add((t["T"], l, ni))
            n = t["n"]
            b = ti
            cs = slice(t["col"], t["col"] + n)
            bk = pbank()
            pT = psb[bk][:].bitcast(BF16).rearrange("p (k t) -> p k t", k=8)
            for k in range(8):
                S.op("pe", lambda e, k=k: e.transpose(out=pT[:, k, 0:n], in_=xh[b][0:n, k * 128:(k + 1) * 128], identity=ident[0:n, 0:n]),
                     reads=[f"xh{b}", "ident"], writes=[f"ps{bk}"])
            gv = pvc(l, OFF_G[ni], 8).unsqueeze(2).to_broadcast([128, 8, n])
            S.op("dve", lambda e: e.tensor_tensor(out=xnT[:, :, cs], in0=pT[:, :, 0:n], in1=gv, op=ALU.mult),
                 reads=[f"ps{bk}", "pv"], writes=[f"xnT{ti}"])

        def xnT_res(grp):
            return [f"xnT{ti}" for ti in range(len(grp["tiles"]))]

        qkst = {"n": 0, "pend": None}

        def qk_flush():
            if qkst["pend"] is not None:
                args = qkst["pend"]
                qkst["pend"] = None
                qk_post(*args)

        def qk_norm(l, bk, N, gain_off, dests, xres, extra32=None):
            b = qkst["n"] % 2
            qkst["n"] += 1
            S.op("act", lambda e: e.activation(out=sq[b][:, 0:N], in_=psb[bk][:, 0:N], func=ACTF.Square),
                 reads=[f"ps{bk}"], writes=[f"sq{b}"])
            qkst["pend"] = (l, bk, N, gain_off, dests, b)

        def qk_post(l, bk, N, gain_off, dests, b):
            bk2 = pbank()
            S.op("pe", lambda e: e.matmul(out=psb[bk2][:, 0:N], lhsT=bones[:, :], rhs=sq[b][:, 0:N], start=True, stop=True),
                 reads=[f"sq{b}", "bones"], writes=[f"ps{bk2}"])
            S.op("act", lambda e: e.activation(out=srt[b][:, 0:N], in_=psb[bk2][:, 0:N], func=ACTF.Ln, bias=epst[:, 0:1], scale=1.0),
                 reads=[f"ps{bk2}", "epst"], writes=[f"srt{b}"])
            S.op("act", lambda e: e.activation(out=srt[b][:, 0:N], in_=srt[b][:, 0:N], func=ACTF.Exp, scale=-0.5),
                 reads=[f"srt{b}"], writes=[f"srt{b}"])
            gain = pvc(l, gain_off, 1)
            for dd in dests:
                (a, bb, dst, rn) = dd[:4]
                p0, p1 = dd[4] if len(dd) > 4 else (0, 128)
                S.op("dve", lambda e, a=a, bb=bb, dst=dst, p0=p0, p1=p1: e.scalar_tensor_tensor(
                    out=dst, in0=psb[bk][p0:p1, a:bb], scalar=gain[p0:p1, :], in1=srt[b][p0:p1, a:bb], op0=ALU.mult, op1=ALU.mult),
                     reads=[f"ps{bk}", f"srt{b}", "pv"], writes=[rn])

        def stage_B(l, grp, W):
            N = grp["N"]
            tiles = grp["tiles"]
            xres = xnT_res(grp)
            wq, wqr = W.get(U_Q)
            wqv = wq[:].rearrange("p (k c) -> p k c", k=8)
            def q_job(j):
                bk = pbank()
                for k in range(8):
                    S.op("pe", lambda e, j=j, k=k, bk=bk: e.matmul(out=psb[bk][:, 0:N], lhsT=wqv[:, k, j * 128:(j + 1) * 128],
                                                                   rhs=xnT[:, k, 0:N], start=(k == 0), stop=(k == 7)),
                         reads=[wqr] + xres, writes=[f"ps{bk}"])
                qk_flush()
                qk_norm(l, bk, N, OFF_GQ, [(0, N, qT[:, j, 0:N], f"qT{j}")], xres)
                if j == 3:
                    W.done(U_Q)
            wkv, wkvr = W.get(U_KV)
            wkvv = wkv[:, 0:3072].rearrange("p (k c) -> p k c", k=8)
            npr = grp["np"]
            def k_job(g):
                bk = pbank()
                for (c0, c1, rr) in ((0, N, xres),):
                    for k in range(8):
                        S.op("pe", lambda e, g=g, k=k, bk=bk, c0=c0, c1=c1: e.matmul(out=psb[bk][:, c0:c1], lhsT=wkvv[:, k, g * 128:(g + 1) * 128],
                                                                       rhs=xnT[:, k, c0:c1], start=(k == 0), stop=(k == 7)),
                             reads=[wkvr] + rr, writes=[f"ps{bk}"])
                dests = [(0, npr * 128, kTw[0:64, 2 * g, 0:npr * 128], f"kTw{g}", (0, 64)),
                         (0, npr * 128, kTw[64:128, 2 * g + 1, 0:npr * 128], f"kTw{g}", (64, 128))]
                for t in tiles:
                    if t["kind"] == "s":
                        dests.append((t["col"], t["col"] + TS, ksam[0:64, 2 * g, :], f"ksam{g}", (0, 64)))
                        dests.append((t["col"], t["col"] + TS, ksam[64:128, 2 * g + 1, :], f"ksam{g}", (64, 128)))
                for t in tiles:
                    if t["kind"] == "s" or t["last"]:
                        h = t["kind"]
                        dests.append((t["col"], t["col"] + t["n"], kn32[(h, l, g)][:, :], f"kn32{h}{l}{g}"))
                qk_flush()
                qk_norm(l, bk, N, OFF_GK, dests, xres)

            def k_state_out():
                qk_flush()
                for g in range(2):
                    for t in tiles:
                        if t["kind"] == "s":
                            S.dma("sp", lambda e, g=g: e.dma_start(out=kTs_o[l, :, g, :], in_=kn32[("s", l, g)][0:64, :]),
                                  reads=[f"kn32s{l}{g}"], stream="misc_out", is_output=True)
                        elif t["last"]:
                            S.dma("sp", lambda e, g=g: e.dma_start(out=kTo_o[l, :, g, :], in_=kn32[("p", l, g)][0:64, :]),
                                  reads=[f"kn32p{l}{g}"], stream="misc_out", is_output=True)
            wpin, wpinr = W.get(U_PIN)
            wpinv = wpin[:].rearrange("p (k c) -> p k c", k=8)

            def tile_job(ti, t):
                n = t["n"]
                cs = slice(t["col"], t["col"] + n)
                bk = pbank()
                for k in range(8):
                    S.op("pe", lambda e, k=k, bk=bk, cs=cs, n=n: e.matmul(out=psb[bk][0:n, 0:128], lhsT=xnT[:, k, cs], rhs=wkvv[:, k, 256:384],
                                                                         start=(k == 0), stop=(k == 7)),
                         reads=[wkvr, f"xnT{ti}"], writes=[f"ps{bk}"])
                if t["kind"] == "p":
                    vdst = Vw[0:n, t["wslot"], :, 0:64]; vres = f"Vw{t['wslot']}"
                else:
                    vdst = Vsam[0:n, :, 0:64]; vres = "Vsam"
                src = psb[bk][0:n, 0:128].rearrange("p (g d) -> p g d", g=2)
                S.op("act", lambda e, vdst=vdst, src=src: e.activation(out=vdst, in_=src, func=ACTF.Copy),
                     reads=[f"ps{bk}"], writes=[vres])
                if DEBUG_STOP == "B3":
                    return
                if t["kind"] == "s" or t["last"]:
                    h = t["kind"]
                    S.op("act", lambda e, bk=bk, n=n, h=h: e.activation(out=v32[(h, l)][0:n, :], in_=psb[bk][0:n, 0:128], func=ACTF.Copy),
                         reads=[f"ps{bk}"], writes=[f"v32{h}{l}"])
                    if h == "s":
                        S.dma("sp", lambda e: e.dma_start(out=vs_o[l, 128 - TS:128, :], in_=v32[("s", l)][0:TS, :]),
                              reads=[f"v32s{l}"], stream="misc_out", is_output=True)
                    else:
                        S.dma("sp", lambda e: e.dma_start(out=vo_o[l], in_=v32[("p", l)][:, :]),
                              reads=[f"v32p{l}"], stream="misc_out", is_output=True)
                if DEBUG_STOP == "B4":
                    return
                bk = pbank()
                for k in range(8):
                    S.op("pe", lambda e, k=k, bk=bk, cs=cs, n=n: e.matmul(out=psb[bk][0:n, 0:512], lhsT=xnT[:, k, cs], rhs=wpinv[:, k, :],
                                                                         start=(k == 0), stop=(k == 7)),
                         reads=[wpinr, f"xnT{ti}"], writes=[f"ps{bk}"])
                if t["kind"] == "p":
                    pdst = pinw[0:n, t["wslot"], :]; pres = f"pinw{t['wslot']}"
                else:
                    pdst = pinsam[0:n, :]; pres = "pinsam"
                S.op("dve", lambda e, pdst=pdst, bk=bk, n=n: e.tensor_copy(out=pdst, in_=psb[bk][0:n, 0:512]),
                     reads=[f"ps{bk}"], writes=[pres])
                if DEBUG_STOP == "B5":
                    return
                if t["kind"] == "s" or t["last"]:
                    h = t["kind"]
                    S.op("act", lambda e, bk=bk, n=n, h=h: e.activation(out=pin32[(h, l)][0:n, :], in_=psb[bk][0:n, 0:512], func=ACTF.Copy),
                         reads=[f"ps{bk}"], writes=[f"pin32{l}"])
                    if h == "s":
                        S.dma("sp", lambda e: e.dma_start(out=ps_o[l], in_=pin32[("s", l)][0:TS, :]),
                              reads=[f"pin32{l}"], stream="misc_out", is_output=True)
                    else:
                        S.dma("sp", lambda e: e.dma_start(out=po_o[l], in_=pin32[("p", l)][:, :]),
                              reads=[f"pin32{l}"], stream="misc_out", is_output=True)
            _tile_job = tile_job

            def tile_job(ti, t):
                _tile_job(ti, t)
                qk_flush()
                if ti > 0:
                    pool_toep(l, grp, ti - 1)
            cjobs = [lambda g=g: k_job(g) for g in range(2)] + [lambda j=j: q_job(j) for j in range(4)]
            tjobs = [lambda ti=ti, t=t: tile_job(ti, t) for ti, t in enumerate(tiles)]
            order = []
            tj = 0
            if pending_post:
                n_early = max(len(tjobs) - 1, 0)
                flush_pending = lambda: [p() for p in [pending_post.pop(0) for _ in range(len(pending_post))]]
                if n_early >= 2:
                    order += tjobs[:n_early - 1]
                    order.append(flush_pending)
                    order.append(tjobs[n_early - 1])
                else:
                    order += tjobs[:n_early]
                    order.append(flush_pending)
                tj = n_early
            if pending_post or tj > 0:
                order += cjobs
                ci = len(cjobs)
            else:
                order += [cjobs[0], cjobs[1]]
                ci = 2
            while ci < len(cjobs) or tj < len(tjobs):
                if tj < len(tjobs):
                    order.append(tjobs[tj]); tj += 1
                if ci < len(cjobs):
                    order.append(cjobs[ci]); ci += 1
            for jb in order:
                jb()
            k_state_out()
            pool_toep(l, grp, len(tiles) - 1)
            W.done(U_KV)
            W.done(U_PIN)

        def cstart(l, grp):
            if grp is groups[0] and n_groups_limit != 1:
                return 128 if l == 0 else 256
            return 0

        def prev_of(l, t):
            if t["kind"] == "s":
                return (kcache[l], [f"kcache{l}"]), (Vcache[l], f"Vcache{l}"), (pinhist[l], f"pinhist{l}")
            w = t["wslot"]
            if w == 0:
                return (kcar[l], [f"kcar{l}"]), (Vcar[l], f"Vcar{l}"), (pincar[l], f"pincar{l}")
            return ((kTw[:, :, (w - 1) * 128: w * 128], ["kTw0", "kTw1"]), (Vw[:, w - 1], f"Vw{w - 1}"),
                    (pinw[:, w - 1, :], f"pinw{w - 1}"))

        def cur_of(l, t):
            if t["kind"] == "s":
                return (ksam, ["ksam0", "ksam1"]), (Vsam, "Vsam"), (pinsam, "pinsam")
            w = t["wslot"]
            return ((kTw[:, :, w * 128:(w + 1) * 128], ["kTw0", "kTw1"]), (Vw[:, w], f"Vw{w}"), (pinw[:, w, :], f"pinw{w}"))

        def ap3(x):
            return x if not hasattr(x, "ap") or True else x

        def pool_toep(l, grp, ti):
            if grp["tiles"][ti]["col"] >= cstart(l, grp):
                t = grp["tiles"][ti]
                n = t["n"]
                cs = slice(t["col"], t["col"] + n)
                (_, _), (_, _), (pp, ppr) = prev_of(l, t)
                (_, _), (_, _), (pc, pcr) = cur_of(l, t)
                tc_tab, tcr = (tfirst, "tfirst") if t["first"] else (tcur, "tcur")
                bk = pbank()
                pv4 = psb[bk][:].rearrange("p (g t) -> p g t", g=4)
                for g in range(4):
                    S.op("pe", lambda e, g=g, n=n, pp=pp, pv4=pv4: e.matmul(out=pv4[:, g, 0:n], lhsT=pp[:, g * 128:(g + 1) * 128], rhs=tprev[:, g, 0:n],
                                                                   start=True, stop=False),
                         reads=[ppr, "tprev"], writes=[f"ps{bk}"])
                    S.op("pe", lambda e, g=g, n=n, pc=pc, tc_tab=tc_tab, pv4=pv4: e.matmul(out=pv4[:, g, 0:n], lhsT=pc[0:n, g * 128:(g + 1) * 128],
                                                                                  rhs=tc_tab[0:n, g, 0:n], start=False, stop=True),
                         reads=[pcr, tcr], writes=[f"ps{bk}"])
                S.op("act", lambda e, n=n, cs=cs, pv4=pv4: e.activation(out=dT[:, :, cs], in_=pv4[:, :, 0:n], func=ACTF.Copy),
                     reads=[f"ps{bk}"], writes=[f"dT{ti}"])

        def pool_proj(l, grp, W):
            N = grp["N"]
            C0 = cstart(l, grp)
            wp, wpr = W.get(U_POOL)
            dres = [f"dT{ti}" for ti, t_ in enumerate(grp["tiles"]) if t_["col"] >= C0]
            for g in range(4):
                bk = pbank()
                S.op("pe", lambda e, g=g, bk=bk: e.matmul(out=psb[bk][:, C0:N], lhsT=wp[:, g * 128:(g + 1) * 128], rhs=dT[:, g, C0:N],
                                                          start=True, stop=True),
                     reads=[wpr] + dres, writes=[f"ps{bk}"])
                S.op("dve", lambda e, g=g, bk=bk: e.tensor_scalar(out=plT[:, g, C0:N], in0=psb[bk][:, C0:N], scalar1=pvc(l, OFF_PS + g, 1),
                                                                  scalar2=None, op0=ALU.mult),
                     reads=[f"ps{bk}", "pv"], writes=[f"plT{g}"])
            W.done(U_POOL)

        def attn_scores(l, t, pbi):
            n = t["n"]
            cs = slice(t["col"], t["col"] + n)
            (kp, kpr), _, _ = prev_of(l, t)
            (kc, kcr), _, _ = cur_of(l, t)
            nkp, nkc = 128, n
            for grp_ in range(2):
                for X, (kx, kxr, nk, btab, bres) in enumerate(((kp, kpr, nkp, biasA, "biasA"), (kc, kcr, nkc, biasB, "biasB"))):
                    bk = pbank()
                    Sv = psb[bk][0:nk, 0:4 * n].rearrange("p (h q) -> p h q", h=4)
                    S.op("pe", lambda e, Sv=Sv, nk=nk, btab=btab, grp_=grp_: e.matmul(
                        out=Sv, lhsT=ident[0:nk, 0:nk], rhs=btab[0:nk, grp_ * 4:(grp_ + 1) * 4, 0:n], start=True, stop=False),
                        reads=["ident", bres], writes=[f"ps{bk}"])
                    for hh in range(4):
                        h = grp_ * 4 + hh
                        hb = (h % 2) * 64
                        S.op("pe", lambda e, Sv=Sv, hh=hh, hb=hb, kx=kx, nk=nk, grp_=grp_, h=h: e.matmul(
                            out=Sv[:, hh, :], lhsT=kx[:, grp_ * 2 + (h % 2), 0:nk], rhs=qT[:, h // 2, cs], start=False, stop=(hh == 3)),
                            reads=kxr + [f"qT{h // 2}"], writes=[f"ps{bk}"])
                    S.op("act", lambda e, Sv=Sv, nk=nk, X=X, grp_=grp_: e.activation(out=PT[pbi][0:nk, X, grp_, :, 0:n], in_=Sv, func=ACTF.Exp, scale=0.125),
                         reads=[f"ps{bk}"], writes=[f"PT{pbi}_{X}{grp_}"])

        def attn_pv(l, t, ti, pbi):
            n = t["n"]
            cs = slice(t["col"], t["col"] + n)
            _, (vp, vpr), _ = prev_of(l, t)
            _, (vc, vcr), _ = cur_of(l, t)
            nkp, nkc = 128, n
            ab = t["T"] % 2
            for grp_ in range(2):
                bk = pbank()
                O = psb[bk][0:n, 0:260].rearrange("p (h e) -> p h e", h=4)
                for hh in range(4):
                    S.op("pe", lambda e, O=O, hh=hh, grp_=grp_, vp=vp: e.matmul(out=O[:, hh, :], lhsT=PT[pbi][0:nkp, 0, grp_, hh, 0:n],
                                                                               rhs=vp[0:nkp, grp_, 0:65], start=True, stop=False),
                         reads=[f"PT{pbi}_0{grp_}", vpr], writes=[f"ps{bk}"])
                    S.op("pe", lambda e, O=O, hh=hh, grp_=grp_, vc=vc: e.matmul(out=O[:, hh, :], lhsT=PT[pbi][0:nkc, 1, grp_, hh, 0:n],
                                                                               rhs=vc[0:nkc, grp_, 0:65], start=False, stop=True),
                         reads=[f"PT{pbi}_1{grp_}", vcr], writes=[f"ps{bk}"])
                dn = den[0:n, grp_ * 4:(grp_ + 1) * 4].unsqueeze(2)
                rd = rden[0:n, grp_ * 4:(grp_ + 1) * 4]
                esk = esink[0:n, l * 8 + grp_ * 4: l * 8 + grp_ * 4 + 4].unsqueeze(2)
                S.op("dve", lambda e, O=O, dn=dn, esk=esk: e.tensor_tensor(out=dn, in0=O[:, :, 64:65], in1=esk, op=ALU.add),
                     reads=[f"ps{bk}", "esink"], writes=[f"den{grp_}"])
                S.op("dve", lambda e, rd=rd, grp_=grp_: e.reciprocal(out=rd, in_=den[0:n, grp_ * 4:(grp_ + 1) * 4]),
                     reads=[f"den{grp_}"], writes=[f"rden{grp_}"])
                S.op("dve", lambda e, O=O, rd=rd, grp_=grp_: e.tensor_tensor(
                    out=atok[ab][0:n, grp_ * 256:(grp_ + 1) * 256].rearrange("p (h d) -> p h d", h=4), in0=O[:, :, 0:64],
                    in1=rd.unsqueeze(2).to_broadcast([n, 4, 64]), op=ALU.mult),
                    reads=[f"ps{bk}", f"rden{grp_}"], writes=[f"atok{ab}"])

        def attn_tr(l, t, ti):
            n = t["n"]
            cs = slice(t["col"], t["col"] + n)
            ab = t["T"] % 2
            bk = pbank()
            pT = psb[bk][:].bitcast(BF16).rearrange("p (k t) -> p k t", k=8)
            for j in range(4):
                S.op("pe", lambda e, j=j, pT=pT: e.transpose(out=pT[:, j, 0:n], in_=atok[ab][0:n, j * 128:(j + 1) * 128], identity=ident[0:n, 0:n]),
                     reads=[f"atok{ab}", "ident"], writes=[f"ps{bk}"])
            S.op("act", lambda e, pT=pT: e.activation(out=aT[:, :, cs], in_=pT[:, 0:4, 0:n], func=ACTF.Copy),
                 reads=[f"ps{bk}"], writes=[f"aT{ti}"])

        tr_defer = []

        def stage_D(l, grp, W):
            sub = [(ti, t) for ti, t in enumerate(grp["tiles"]) if t["col"] >= cstart(l, grp)]
            nt_ = len(sub)
            for i in range(nt_ + 2):
                if i < nt_:
                    attn_scores(l, sub[i][1], i % 2)
                if 0 <= i - 1 < nt_:
                    attn_pv(l, sub[i - 1][1], sub[i - 1][0], (i - 1) % 2)
                if 0 <= i - 2 < nt_:
                    if i - 2 >= nt_ - 2:
                        tr_defer.append(lambda l=l, t=sub[i - 2][1], ti=sub[i - 2][0]: attn_tr(l, t, ti))
                    else:
                        attn_tr(l, sub[i - 2][1], sub[i - 2][0])
                if i == 0:
                    pool_proj(l, grp, W)

        def stage_E(l, grp, W):
            N = grp["N"]
            C0 = cstart(l, grp)
            nt = len(grp["tiles"])
            xres = [f"xnT{ti}" for ti, t_ in enumerate(grp["tiles"]) if t_["col"] >= C0]
            ares = [f"aT{ti}" for ti, t_ in enumerate(grp["tiles"]) if t_["col"] >= C0]
            pres = [f"plT{g}" for g in range(4)]
            for mp in range(4):
                wg, wgr = W.get(U_EG(mp))
                wb, wbr = W.get(U_EB(mp))
                wgv = wg[:].rearrange("p (k c) -> p k c", k=8)
                wbv = wb[:, 0:2048].rearrange("p (k c) -> p k c", k=4)
                for mi in range(2):
                    m = 2 * mp + mi
                    bYA, bYB, bG0, bG1 = pbank(), pbank(), pbank(), pbank()
                    for gi, bG in enumerate((bG0, bG1)):
                        for k in range(8):
                            S.op("pe", lambda e, k=k, mi=mi, bG=bG, gi=gi, wgv=wgv: e.matmul(
                                out=psb[bG][:, C0:N], lhsT=wgv[:, k, gi * 256 + mi * 128: gi * 256 + (mi + 1) * 128],
                                rhs=xnT[:, k, C0:N], start=(k == 0), stop=(k == 7)),
                                reads=[wgr] + xres, writes=[f"ps{bG}"])
                    for k in range(4):
                        S.op("pe", lambda e, k=k, mi=mi, bYB=bYB, wbv=wbv: e.matmul(out=psb[bYB][:, C0:N], lhsT=wbv[:, k, 256 + mi * 128:256 + (mi + 1) * 128],
                                                                          rhs=plT[:, k, C0:N], start=(k == 0), stop=(k == 3)),
                             reads=[wbr] + pres, writes=[f"ps{bYB}"])
                    while tr_defer:
                        tr_defer.pop(0)()
                    for k in range(4):
                        S.op("pe", lambda e, k=k, mi=mi, bYA=bYA, wbv=wbv: e.matmul(out=psb[bYA][:, C0:N], lhsT=wbv[:, k, mi * 128:(mi + 1) * 128],
                                                                          rhs=aT[:, k, C0:N], start=(k == 0), stop=(k == 3)),
                             reads=[wbr] + ares, writes=[f"ps{bYA}"])
                    S.op("act", lambda e, bG0=bG0, m=m: e.activation(out=s0b[:, C0:N], in_=psb[bG0][:, C0:N], func=ACTF.Sigmoid,
                                                                     bias=pvc(l, OFF_GB[0] + m, 1), scale=1.0),
                         reads=[f"ps{bG0}", "pv"], writes=["s0b"])
                    S.op("act", lambda e, bG1=bG1, m=m: e.activation(out=s1b[:, C0:N], in_=psb[bG1][:, C0:N], func=ACTF.Sigmoid,
                                                                     bias=pvc(l, OFF_GB[1] + m, 1), scale=1.0),
                         reads=[f"ps{bG1}", "pv"], writes=["s1b"])
                    S.op("dve", lambda e, bYA=bYA: e.tensor_tensor(out=s0b[:, C0:N], in0=psb[bYA][:, C0:N], in1=s0b[:, C0:N], op=ALU.mult),
                         reads=[f"ps{bYA}", "s0b"], writes=["s0b"])
                    S.op("dve", lambda e, bYB=bYB: e.tensor_tensor(out=s1b[:, C0:N], in0=psb[bYB][:, C0:N], in1=s1b[:, C0:N], op=ALU.mult),
                         reads=[f"ps{bYB}", "s1b"], writes=["s1b"])
                    S.op("dve", lambda e, m=m: e.tensor_tensor(out=mixT[:, m, C0:N], in0=s0b[:, C0:N], in1=s1b[:, C0:N], op=ALU.add),
                         reads=["s0b", "s1b"], writes=[f"mixT{m}"])
                W.done(U_EG(mp))
                W.done(U_EB(mp))

        def resid_add(grp, t, bk, cb):
            xt, xr = xs_of(t)
            n = t["n"]
            xv = xt[0:n, cb * 512:(cb + 1) * 512]
            if grp is groups[0] and t["kind"] == "p":
                S.op("dve", lambda e: e.scalar_tensor_tensor(out=xv, in0=psb[bk][0:n, 0:512], scalar=hvt[0:n, 0:1], in1=xv,
                                                             op0=ALU.mult, op1=ALU.add),
                     reads=[f"ps{bk}", xr, "hvt"], writes=[xr])
            else:
                S.op("dve", lambda e: e.tensor_tensor(out=xv, in0=psb[bk][0:n, 0:512], in1=xv, op=ALU.add),
                     reads=[f"ps{bk}", xr], writes=[xr])

        def stage_F(l, grp, W):
            sub = [(ti, t) for ti, t in enumerate(grp["tiles"]) if t["col"] >= cstart(l, grp)]
            nt_ = len(sub)
            wovs = []
            for cb in range(2):
                wo, wor = W.get(U_OUT + cb)
                wovs.append((wo[:].rearrange("p (k c) -> p k c", k=8), wor))
            for si in range(nt_ + 2):
                if si < nt_:
                    ti, t = sub[si]
                    n = t["n"]
                    cs = slice(t["col"], t["col"] + n)
                    fbanks = [pbank(), pbank()]
                    kparts = ((range(0, 7), range(7, 8)) if si == 0 else (range(0, 8),))
                    for kr in kparts:
                        for cb in range(2):
                            wov, wor = wovs[cb]
                            bk = fbanks[cb]
                            for k in kr:
                                S.op("pe", lambda e, k=k, bk=bk, cs=cs, n=n, wov=wov: e.matmul(out=psb[bk][0:n, 0:512], lhsT=mixT[:, k, cs], rhs=wov[:, k, :],
                                                                                     start=(k == 0), stop=(k == 7)),
                                     reads=[wor, f"mixT{k}"], writes=[f"ps{bk}"])
                    for cb in range(2):
                        resid_add(grp, t, fbanks[cb], cb)
                    norm_pre(l, 1, t, ti)
                if 0 <= si - 2 < nt_:
                    norm_post(l, 1, sub[si - 2][1], sub[si - 2][0])
            W.done(U_OUT)
            W.done(U_OUT + 1)

        def stage_G(l, grp, W):
            N = grp["N"]
            C0 = cstart(l, grp)
            xres = [f"xnT{ti}" for ti, t_ in enumerate(grp["tiles"]) if t_["col"] >= C0]
            segs = [(max(a, C0), bb, hid) for (a, bb, hid) in grp["segs"] if bb > C0]
            cw0 = pvc(l, OFF_CW[0], NCH); cw1 = pvc(l, OFF_CW[1], NCH)
            for (a, bb, hid) in segs:
                hs = hist[(hid, l)]
                hr = f"hist{hid}{l}"
                cr = corr[hid]
                S.op("pool", lambda e, hs=hs, cr=cr: e.tensor_tensor(out=cr[:, :, 0], in0=hs[:, :, 0], in1=cw0, op=ALU.mult),
                     reads=[hr, "pv"], writes=[f"corr{hid}"])
                S.op("pool", lambda e, hs=hs: e.tensor_tensor(out=ctmp[:, :], in0=hs[:, :, 1], in1=cw1, op=ALU.mult),
                     reads=[hr, "pv"], writes=["ctmp"])
                S.op("pool", lambda e, cr=cr: e.tensor_tensor(out=cr[:, :, 0], in0=cr[:, :, 0], in1=ctmp[:, :], op=ALU.add),
                     reads=[f"corr{hid}", "ctmp"], writes=[f"corr{hid}"])
                S.op("pool", lambda e, hs=hs, cr=cr: e.tensor_tensor(out=cr[:, :, 1], in0=hs[:, :, 1], in1=cw0, op=ALU.mult),
                     reads=[hr, "pv"], writes=[f"corr{hid}"])
            for f in range(11):
                wu, wur = W.get(U_UP + f)
                wuv = wu[:].rearrange("p (k c) -> p k c", k=8)
                for pi in range(2):
                    j = 2 * f + pi
                    ub = j % 2
                    for half, (c, ubuf, ures) in enumerate(((j, ug[ub], f"ug{ub}"), (22 + j, uv[ub], f"uv{ub}"))):
                        bk = pbank()
                        csp = grp["tiles"][-1]["col"]
                        splits = ((C0, csp, xres[:-1]), (csp, N, xres[-1:])) if (j == 0 and csp > C0) else ((C0, N, xres),)
                        for (c0, c1, rr) in splits:
                            for k in range(8):
                                S.op("pe", lambda e, k=k, bk=bk, half=half, pi=pi, wuv=wuv, c0=c0, c1=c1: e.matmul(
                                    out=psb[bk][:, c0:c1], lhsT=wuv[:, k, half * 256 + pi * 128: half * 256 + (pi + 1) * 128],
                                    rhs=xnT[:, k, c0:c1], start=(k == 0), stop=(k == 7)),
                                    reads=[wur] + rr, writes=[f"ps{bk}"])
                        S.op("act", lambda e, bk=bk, c=c, ubuf=ubuf: e.activation(out=ubuf[:, C0:N], in_=psb[bk][:, C0:N], func=ACTF.Identity,
                                                                                 bias=pvc(l, OFF_CB + c, 1), scale=pvc(l, OFF_CW[2] + c, 1)),
                             reads=[f"ps{bk}", "pv"], writes=[ures])
                        for (a, bb, hid) in segs:
                            S.op("act", lambda e, bk=bk, c=c, bb=bb, hid=hid: e.activation(out=hist[(hid, l)][:, c, :], in_=psb[bk][:, bb - 2:bb],
                                                                                         func=ACTF.Copy),
                                 reads=[f"ps{bk}"], writes=[f"hist{hid}{l}"])
                        for (a, bb, hid) in segs:
                            S.op("dve", lambda e, bk=bk, c=c, ubuf=ubuf, a=a, bb=bb: e.scalar_tensor_tensor(
                                out=ubuf[:, a + 1:bb], in0=psb[bk][:, a:bb - 1], scalar=pvc(l, OFF_CW[1] + c, 1), in1=ubuf[:, a + 1:bb],
                                op0=ALU.mult, op1=ALU.add), reads=[f"ps{bk}", ures, "pv"], writes=[ures])
                            S.op("dve", lambda e, bk=bk, c=c, ubuf=ubuf, a=a, bb=bb: e.scalar_tensor_tensor(
                                out=ubuf[:, a + 2:bb], in0=psb[bk][:, a:bb - 2], scalar=pvc(l, OFF_CW[0] + c, 1), in1=ubuf[:, a + 2:bb],
                                op0=ALU.mult, op1=ALU.add), reads=[f"ps{bk}", ures, "pv"], writes=[ures])
                            S.op("dve", lambda e, c=c, ubuf=ubuf, a=a, hid=hid: e.tensor_tensor(
                                out=ubuf[:, a:a + 2], in0=ubuf[:, a:a + 2], in1=corr[hid][:, c, :], op=ALU.add),
                                reads=[ures, f"corr{hid}"], writes=[ures])
                    S.op("act", lambda e, ub=ub: e.activation(out=ug[ub][:, C0:N], in_=ug[ub][:, C0:N], func=ACTF.Gelu),
                         reads=[f"ug{ub}"], writes=[f"ug{ub}"])
                    S.op("dve" if (grp is groups[0] or j >= 20) else "pool", lambda e, ub=ub, j=j: e.tensor_tensor(out=actT[:, j, C0:N], in0=ug[ub][:, C0:N], in1=uv[ub][:, C0:N], op=ALU.mult),
                         reads=[f"ug{ub}", f"uv{ub}"], writes=[f"actT{j}"])
                W.done(U_UP + f)
            for t in grp["tiles"]:
                if t["kind"] == "s":
                    S.dma("sp", lambda e: e.dma_start(out=cs_o[l], in_=hist[("s", l)][:]), reads=[f"hists{l}"],
                          stream="misc_out", is_output=True)
                elif t["last"]:
                    S.dma("sp", lambda e: e.dma_start(out=co_o[l], in_=hist[("p", l)][:]), reads=[f"histp{l}"],
                          stream="misc_out", is_output=True)

        def stage_H(l, grp, W):
            halves = ((0, 12), (12, 22))
            if l == 1:
                gi_ = groups.index(grp)
                if gi_ + 1 < len(groups):
                    for ti2, t2 in enumerate(groups[gi_ + 1]["tiles"]):
                        if t2["T"] in xloaded:
                            norm_pre(0, 0, t2, ti2)
            for hi, (k0, k1) in enumerate(halves):
                for ti, t in enumerate(grp["tiles"]):
                    n = t["n"]
                    cs = slice(t["col"], t["col"] + n)
                    if (l == 1 and t["halo"]) or t["col"] < cstart(l, grp):
                        if hi == 1 and l == 1:
                            load_x_tile(t["T"] + NXS)
                        continue
                    for cb in range(2):
                        bk = pbank()
                        for kc in range(k0, k1):
                            wd, wdr = W.get(U_DN + kc // 4)
                            wdv = wd[:].rearrange("p (k c) -> p k c", k=4)
                            S.op("pe", lambda e, kc=kc, bk=bk, cs=cs, n=n, wdv=wdv, cb=cb, k0=k0, k1=k1: e.matmul(
                                out=psb[bk][0:n, 0:512], lhsT=actT[:, kc, cs], rhs=wdv[:, kc % 4, cb * 512:(cb + 1) * 512],
                                start=(kc == k0), stop=(kc == k1 - 1)),
                                reads=[wdr, f"actT{kc}"], writes=[f"ps{bk}"])
                        resid_add(grp, t, bk, cb)
                    if hi == 1 and l == 0:
                        norm_pre(1, 0, t, ti)
                        if ti > 0 and (grp["tiles"][ti - 1]["T"], 1, 0) in pre_done:
                            norm_post(1, 0, grp["tiles"][ti - 1], ti - 1)
                    if hi == 1 and l == 1:
                        xt, xr = xs_of(t)
                        if t["kind"] == "s":
                            S.dma("sp", lambda e, xt=xt: e.dma_start(out=ys_o, in_=xt[0:TS, :]), reads=[xr], stream=xr, is_output=True)
                        elif not t["halo"]:
                            S.dma("sp", lambda e, xt=xt, t=t: e.dma_start(out=yp_o[t["orow"]:t["orow"] + 128, :], in_=xt[:, :]),
                                  reads=[xr], stream=xr, is_output=True)
                        load_x_tile(t["T"] + NXS)
                for u in range(3):
                    W.done(U_DN + hi * 3 + u)
                if hi == 0 and l == 1:
                    gi_ = groups.index(grp)
                    if gi_ + 1 < len(groups):
                        for ti2, t2 in enumerate(groups[gi_ + 1]["tiles"]):
                            if (t2["T"], 0, 0) in pre_done:
                                norm_post(0, 0, t2, ti2)

        def carry(l, grp):
            npr = grp["np"]
            w = npr - 1
            S.op("pool", lambda e: e.tensor_copy(out=kcar[l][:], in_=kTw[:, :, w * 128:(w + 1) * 128]),
                 reads=["kTw0", "kTw1"], writes=[f"kcar{l}"])
            S.op("pool", lambda e: e.tensor_copy(out=Vcar[l][:], in_=Vw[:, w]), reads=[f"Vw{w}"], writes=[f"Vcar{l}"])
            S.op("pool", lambda e: e.tensor_copy(out=pincar[l][:], in_=pinw[:, w, :]), reads=[f"pinw{w}"], writes=[f"pincar{l}"])

        for Tn in range(NXS):
            load_x_tile(Tn)
        late_setup()
        for gi, grp in enumerate(groups):
            S.epoch = gi
            for l in range(2):
                W = WU()
                if DEBUG_STOP == "setup":
                    break
                for ti, t in enumerate(grp["tiles"]):
                    if (t["T"], l, 0) not in pre_done:
                        norm_pre(l, 0, t, ti)
                for ti, t in enumerate(grp["tiles"]):
                    if (t["T"], l, 0) not in post_done:
                        if ti == len(grp["tiles"]) - 1 and ti > 0:
                            pending_post.append(lambda l=l, t=t, ti=ti: norm_post(l, 0, t, ti))
                        else:
                            norm_post(l, 0, t, ti)
                if DEBUG_STOP == "A":
                    break
                stage_B(l, grp, W)
                if DEBUG_STOP in ("B", "B1", "B2", "B3", "B4", "B5"):
                    break
                stage_D(l, grp, W)
                if DEBUG_STOP == "D":
                    break
                stage_E(l, grp, W)
                if DEBUG_STOP == "E":
                    break
                stage_F(l, grp, W)
                if DEBUG_STOP == "F":
                    break
                stage_G(l, grp, W)
                if DEBUG_STOP == "G":
                    break
                stage_H(l, grp, W)
                carry(l, grp)
                ucur["i"] += UNITS_PER_LAYER
                if DEBUG_ONE_LAYER and gi == len(groups) - 1:
                    break
            if gi == 0:
                for t in range(4):
                    S.op("pool", lambda e, t=t: e.memset(Vw[:, t, :, 64:65], 1.0), writes=[f"Vw{t}"])
        S.emit()
        build_nc.stats = S.stats
    return nc


def _fm(a, k):
    C = a.shape[1]
    return np.ascontiguousarray(a.reshape(k, 128, C).transpose(1, 0, 2)).reshape(128, k * C)


def _pack_units(w_in, w_pool, w_br_attn, w_br_pool, w_out, w_up, w_down):
    out = np.zeros((2 * UNITS_PER_LAYER, 128, UEL), np.float32)
    for l in range(2):
        b = l * UNITS_PER_LAYER
        wi = w_in[l]
        out[b + U_Q] = _fm(wi[:, 0:512], 8)
        k0, k1, v = wi[:, 512:576], wi[:, 576:640], wi[:, 640:768]
        out[b + U_KV, :, 0:3072] = _fm(np.concatenate([k0, k0, k1, k1, v], axis=1), 8)
        out[b + U_PIN] = _fm(wi[:, 768:1280], 8)
        out[b + U_POOL, :, 0:512] = np.ascontiguousarray(w_pool[l].transpose(1, 0, 2)).reshape(128, 512)
        for mp in range(4):
            g0 = wi[:, 1280 + mp * 256: 1280 + (mp + 1) * 256]
            g1 = wi[:, 2304 + mp * 256: 2304 + (mp + 1) * 256]
            out[b + U_EG(mp)] = _fm(np.concatenate([g0, g1], axis=1), 8)
            ba = w_br_attn[l][:, mp * 256:(mp + 1) * 256]
            bp = w_br_pool[l][:, mp * 256:(mp + 1) * 256]
            out[b + U_EB(mp), :, 0:2048] = _fm(np.concatenate([ba, bp], axis=1), 4)
        out[b + U_OUT] = _fm(w_out[l][:, 0:512], 8)
        out[b + U_OUT + 1] = _fm(w_out[l][:, 512:1024], 8)
        for f in range(11):
            ga = w_up[l][:, 2 * f * 128:(2 * f + 2) * 128]
            va = w_up[l][:, DFF + 2 * f * 128: DFF + (2 * f + 2) * 128]
            out[b + U_UP + f] = _fm(np.concatenate([ga, va], axis=1), 8)
        for i in range(6):
            rows = w_down[l][i * 512:(i + 1) * 512]
            kk = rows.shape[0] // 128
            out[b + U_DN + i, :, 0:kk * 1024] = _fm(rows, kk)
    return out


def _pack_vec(norm_mix, norm_ffn, gate_bias, pool_scale, conv_w, conv_b, q_norm, k_norm, sinks):
    pvt = np.zeros((128, 2 * PVL), np.float32)
    for l in range(2):
        b = l * PVL
        pvt[:, b + 0:b + 8] = norm_mix[l].reshape(8, 128).T
        pvt[:, b + 8:b + 16] = norm_ffn[l].reshape(8, 128).T
        pvt[:, b + 16:b + 24] = gate_bias[l, 0].reshape(8, 128).T
        pvt[:, b + 24:b + 32] = gate_bias[l, 1].reshape(8, 128).T
        pvt[:, b + 32:b + 36] = pool_scale[l].reshape(4, 128).T
        for j in range(3):
            pvt[:, b + 36 + 44 * j: b + 36 + 44 * (j + 1)] = conv_w[l, j].reshape(NCH, 128).T
        pvt[:, b + 168:b + 212] = conv_b[l].reshape(NCH, 128).T
        pvt[:, b + 212] = np.concatenate([q_norm[l], q_norm[l]])
        pvt[:, b + 213] = np.concatenate([k_norm[l], k_norm[l]])
        pvt[:, b + 214:b + 222] = np.broadcast_to(sinks[l][None, :], (128, 8))
    return pvt


def _const_tables():
    slopes = np.array([2.0 ** (-(h + 1)) for h in range(8)], np.float32)
    s = np.arange(128)[:, None]
    q = np.arange(128)[None, :]
    NEG = np.float32(-1e30)
    distA = (q + 128 - s).astype(np.float32)
    maskA = (q >= 64) & (s < 64)
    distB = np.abs(q - s).astype(np.float32)
    maskB = (q < 64) & (s >= 64)
    biasA = np.zeros((128, 8, 128), np.float32)
    biasB = np.zeros((128, 8, 128), np.float32)
    for h in range(8):
        biasA[:, h, :] = np.where(maskA, NEG, -8.0 * slopes[h] * distA)
        biasB[:, h, :] = np.where(maskB, NEG, -8.0 * slopes[h] * distB)
    wins = (2, 4, 8, 16)
    tcur = np.zeros((128, 4, 128), np.float32)
    tprev = np.zeros((128, 4, 128), np.float32)
    tfirst = np.zeros((128, 4, 128), np.float32)
    t = np.arange(128)[None, :]
    for g, w in enumerate(wins):
        inwin = (s <= t) & (s > t - w)
        tcur[:, g, :] = np.where(inwin, 1.0 / w, 0.0) - (s == t)
        tprev[:, g, :] = np.where(s > 128 + t - w, 1.0 / w, 0.0)
        cnt = np.minimum(t + 1, w).astype(np.float32)
        tfirst[:, g, :] = np.where(inwin, 1.0 / cnt, 0.0) - (s == t)
    bones = np.zeros((128, 128), np.float32)
    bones[0:64, 0:64] = 1.0 / 64
    bones[64:128, 64:128] = 1.0 / 64
    ident = np.eye(128, dtype=np.float32)
    return biasA, biasB, tcur, tprev, tfirst, bones, ident


_NC_CACHE = {}


def kernel(x_prompt, x_sample, cache_k, cache_v, state_pool, state_conv,
           norm_mix, w_in, q_norm, k_norm, sinks, w_pool, pool_scale,
           w_br_attn, w_br_pool, gate_bias, w_out, norm_ffn, w_up, conv_w, conv_b, w_down):
    f = lambda a: np.asarray(a, dtype=np.float32)
    x_prompt, x_sample, cache_k, cache_v, state_pool, state_conv = map(f, (x_prompt, x_sample, cache_k, cache_v, state_pool, state_conv))
    wun = _pack_units(f(w_in), f(w_pool), f(w_br_attn), f(w_br_pool), f(w_out), f(w_up), f(w_down))
    pvt = _pack_vec(f(norm_mix), f(norm_ffn), f(gate_bias), f(pool_scale), f(conv_w), f(conv_b), f(q_norm), f(k_norm), f(sinks))
    biasA, biasB, tcur, tprev, tfirst, bones, ident = _const_tables()
    B, SEQ = x_prompt.shape[0], x_prompt.shape[1]
    per = SEQ // 4
    in_maps = []
    for c in range(NCORES):
        b, qt = c // 4, c % 4
        start = qt * per
        xp = np.zeros((NPT * 128, D), np.float32)
        if qt > 0:
            xp[0:384] = x_prompt[b, start - 384:start]
        xp[384:] = x_prompt[b, start:start + per]
        ckT = np.zeros((2, 128, 4, 128), np.float32)
        for l in range(2):
            kt = cache_k[l, c].transpose(1, 2, 0)
            for g in range(2):
                ckT[l, 0:64, 2 * g] = kt[g]
                ckT[l, 64:128, 2 * g + 1] = kt[g]
        sph = np.zeros((2, 128, 512), np.float32)
        sph[:, 113:128] = state_pool[:, c]
        scT = np.ascontiguousarray(state_conv[:, c].reshape(2, 2, NCH, 128).transpose(0, 3, 2, 1))
        in_maps.append({
            "xp": xp, "xsm": np.ascontiguousarray(x_sample[c]), "wun": wun, "pvec": pvt,
            "biasA": biasA, "biasB": biasB, "tcur": tcur, "tprev": tprev,
            "tfirst": tfirst if qt == 0 else tcur, "bones": bones, "ident": ident,
            "hv": np.full((128, 1), 0.0 if qt == 0 else 1.0, np.float32),
            "ckT": ckT, "cktm": np.ascontiguousarray(cache_k[:, c].reshape(2, 128, 128)), "cv": np.ascontiguousarray(cache_v[:, c]), "sph": sph, "scT": scT,
        })
    if "nc" not in _NC_CACHE:
        _NC_CACHE["nc"] = build_nc()
    nc = _NC_CACHE["nc"]
    res = run_bass_kernel_spmd(nc, in_maps, core_ids=list(range(NCORES)))
    R = res.results
    y_prompt = np.zeros((B, SEQ, D), np.float32)
    for c in range(NCORES):
        b, qt = c // 4, c % 4
        y_prompt[b, qt * per:(qt + 1) * per] = R[c]["yp"]
    y_sample = np.stack([R[c]["ys"] for c in range(NCORES)], 0)

    def kfix(a):
        return np.ascontiguousarray(a.transpose(0, 3, 2, 1))

    def cfix(a):
        return np.ascontiguousarray(a.transpose(0, 3, 2, 1)).reshape(2, 2, NCH * 128)

    lastc = [3, 7]
    k_prompt = np.stack([kfix(R[c]["kTo"]) for c in lastc], 1)
    v_prompt = np.stack([R[c]["vo"].reshape(2, 128, 2, 64) for c in lastc], 1)
    pool_prompt = np.stack([R[c]["po"][:, 113:128] for c in lastc], 1)
    conv_prompt = np.stack([cfix(R[c]["co"]) for c in lastc], 1)
    k_sample = np.stack([np.concatenate([R[c]["kcp"].reshape(2, 128 - TS, 2, 64), kfix(R[c]["kTs"])], axis=1) for c in range(NCORES)], 1)
    v_sample = np.stack([R[c]["vs"].reshape(2, 128, 2, 64) for c in range(NCORES)], 1)
    pool_sample = np.stack([R[c]["pso"][:, 1:TS] for c in range(NCORES)], 1)
    conv_sample = np.stack([cfix(R[c]["cso"]) for c in range(NCORES)], 1)
    return (y_prompt, y_sample, k_prompt.astype(np.float32), v_prompt.astype(np.float32), pool_prompt, conv_prompt,
            k_sample.astype(np.float32), v_sample, pool_sample, conv_sample)
```
